# Optimizing a Trainium2 kernel written in Bass

```python
import math
import jax, jax.numpy as jnp
from jax import lax
import numpy as np

D_MODEL = 1024
BATCH = 4
SEQ = 8192
DEPTH = 2

GRID_W = 64
CTX_LEN = 256
HEAD_DIM = 64
BLOCK = 128
ROPE_BASE = 10000.0
EPS = 1e-6
POOL_WIDTH = D_MODEL // 2
POOL_WINDOWS = (2, 4, 8, 16)
POOL_GROUPS = len(POOL_WINDOWS)
POOL_GROUP_DIM = POOL_WIDTH // POOL_GROUPS
SWA_Q_HEADS = D_MODEL // 128
SWA_KV_HEADS = SWA_Q_HEADS // 4
SWA_WINDOW = 128
DIFF_HEADS = D_MODEL // 256
DIFF_V_DIM = 2 * HEAD_DIM
D_FF = 4 * D_MODEL

IN_SIZES = (POOL_WIDTH,
            SWA_Q_HEADS * HEAD_DIM,
            SWA_KV_HEADS * HEAD_DIM,
            SWA_KV_HEADS * HEAD_DIM,
            DIFF_HEADS * 2 * HEAD_DIM,
            DIFF_HEADS * 2 * HEAD_DIM,
            DIFF_HEADS * DIFF_V_DIM,
            3 * D_MODEL)
IN_OFFSETS = tuple(int(o) for o in np.cumsum(IN_SIZES)[:-1])
D_IN = int(sum(IN_SIZES))

kernel_name = "hybrid_pool_swa_diffattn_prefix_dit"


def rms_norm(x, g):
    xf = x.astype(jnp.float32)
    y = xf * lax.rsqrt(jnp.mean(xf * xf, axis=-1, keepdims=True) + EPS)
    return (y * g.astype(jnp.float32)).astype(x.dtype)


def modulate(h, shift, scale):
    return h * (1 + scale) + shift


def axial_rope_tables(rows):
    row = jnp.repeat(jnp.arange(rows, dtype=jnp.float32), GRID_W)
    col = jnp.tile(jnp.arange(GRID_W, dtype=jnp.float32), rows)
    n_freq = HEAD_DIM // 4
    inv = ROPE_BASE ** (-jnp.arange(n_freq, dtype=jnp.float32) / n_freq)
    ang = jnp.concatenate([row[:, None] * inv, col[:, None] * inv], axis=-1)
    return jnp.cos(ang), jnp.sin(ang)


def apply_rope(x, cos, sin):
    x1, x2 = jnp.split(x.astype(jnp.float32), 2, axis=-1)
    c = cos[:, None, :]
    s = sin[:, None, :]
    return jnp.concatenate([x1 * c - x2 * s, x2 * c + x1 * s], axis=-1).astype(x.dtype)


def centred_mean_minus_self(u, window):
    n = u.shape[1]
    uf = u.astype(jnp.float32)
    cs = jnp.concatenate([jnp.zeros_like(uf[:, :1]), jnp.cumsum(uf, axis=1)], axis=1)
    t = jnp.arange(n)
    lo = jnp.clip(t - window // 2, 0, n)
    hi = jnp.clip(t + window // 2, 0, n)
    cnt = (hi - lo).astype(jnp.float32)[None, :, None]
    return ((cs[:, hi] - cs[:, lo]) / cnt - uf).astype(u.dtype)


def pool_mixer(u, w_pool, pool_scale):
    groups = jnp.split(u, POOL_GROUPS, axis=-1)
    pooled = jnp.stack([centred_mean_minus_self(g, w) for g, w in zip(groups, POOL_WINDOWS)], axis=-2)
    mixed = jnp.einsum('bngc,gcd->bngd', pooled, w_pool)
    return mixed.reshape(u.shape) * pool_scale


def softmax_with_sink(s, sink):
    m = jnp.maximum(jnp.max(s, axis=-1, keepdims=True), sink)
    e = jnp.exp(s - m)
    return e / (jnp.sum(e, axis=-1, keepdims=True) + jnp.exp(sink - m))


def windowed_gqa_latent(q, k, v, k_ctx, v_ctx, sink):
    b, n, hq, dh = q.shape
    hkv = k.shape[2]
    grp = hq // hkv
    nb = n // BLOCK
    scale = dh ** -0.5
    qb = q.reshape(b, nb, BLOCK, hkv, grp, dh)
    pad = jnp.zeros((b, BLOCK, hkv, dh), k.dtype)

    def band(t):
        tr = jnp.concatenate([pad, t, pad], axis=1).reshape(b, nb + 2, BLOCK, hkv, dh)
        return jnp.concatenate([tr[:, :-2], tr[:, 1:-1], tr[:, 2:]], axis=2)

    kb, vb = band(k), band(v)
    a = jnp.arange(BLOCK)[:, None]
    j = jnp.arange(3 * BLOCK)[None, :]
    kpos = jnp.arange(nb)[:, None, None] * BLOCK - BLOCK + j[None]
    valid = (jnp.abs(j - BLOCK - a) <= SWA_WINDOW)[None] & (kpos >= 0) & (kpos < n)
    s_loc = jnp.einsum('bnqhgd,bnkhd->bnhgqk', qb, kb, preferred_element_type=jnp.float32) * scale
    s_loc = jnp.where(valid[None, :, None, None], s_loc, -jnp.inf)
    s_ctx = jnp.einsum('bnqhgd,bkhd->bnhgqk', qb, k_ctx, preferred_element_type=jnp.float32) * scale
    p = softmax_with_sink(jnp.concatenate([s_loc, s_ctx], axis=-1),
                          sink.astype(jnp.float32).reshape(1, 1, hkv, grp, 1, 1)).astype(v.dtype)
    o = (jnp.einsum('bnhgqk,bnkhd->bnqhgd', p[..., :3 * BLOCK], vb)
         + jnp.einsum('bnhgqk,bkhd->bnqhgd', p[..., 3 * BLOCK:], v_ctx))
    return o.reshape(b, n, hq * dh)


def gqa_context(q, k, v, sink):
    b, m, hq, dh = q.shape
    hkv = k.shape[2]
    grp = hq // hkv
    qg = q.reshape(b, m, hkv, grp, dh)
    s = jnp.einsum('bqhgd,bkhd->bhgqk', qg, k, preferred_element_type=jnp.float32) * dh ** -0.5
    p = softmax_with_sink(s, sink.astype(jnp.float32).reshape(1, hkv, grp, 1, 1)).astype(v.dtype)
    return jnp.einsum('bhgqk,bkhd->bqhgd', p, v).reshape(b, m, hq * dh)


def diff_attn(q1, q2, k1, k2, v, lam):
    scale = q1.shape[-1] ** -0.5
    s1 = jnp.einsum('bqhd,bkhd->bhqk', q1, k1, preferred_element_type=jnp.float32) * scale
    s2 = jnp.einsum('bqhd,bkhd->bhqk', q2, k2, preferred_element_type=jnp.float32) * scale
    p = jax.nn.softmax(s1, axis=-1) - lam * jax.nn.softmax(s2, axis=-1)
    return jnp.einsum('bhqk,bkhd->bqhd', p.astype(v.dtype), v)


def diff_attention_latent(q1, q2, k1, k2, v, k1c, k2c, vc, lam):
    b, n, h, dh = q1.shape
    nb = n // BLOCK
    k1a = jnp.concatenate([k1, k1c], axis=1)
    k2a = jnp.concatenate([k2, k2c], axis=1)
    va = jnp.concatenate([v, vc], axis=1)

    def to_blocks(t):
        return jnp.moveaxis(t.reshape(b, nb, BLOCK, h, dh), 1, 0)

    o = lax.map(lambda qs: diff_attn(qs[0], qs[1], k1a, k2a, va, lam), (to_blocks(q1), to_blocks(q2)))
    return jnp.moveaxis(o, 0, 1).reshape(b, n, h, DIFF_V_DIM)


def diff_head_out(o, subln, lam_init):
    b, m = o.shape[:2]
    return (rms_norm(o, subln) * (1 - lam_init)).reshape(b, m, DIFF_HEADS * DIFF_V_DIM)


def project_streams(z, swa_qn, swa_kn, diff_qn, diff_kn, cos, sin):
    b, m = z.shape[:2]
    u, qs, ks, vs, qd, kd, vd, g = jnp.split(z, IN_OFFSETS, axis=-1)
    qs = rms_norm(qs.reshape(b, m, SWA_Q_HEADS, HEAD_DIM), swa_qn)
    ks = rms_norm(ks.reshape(b, m, SWA_KV_HEADS, HEAD_DIM), swa_kn)
    qd = rms_norm(qd.reshape(b, m, 2 * DIFF_HEADS, HEAD_DIM), diff_qn)
    kd = rms_norm(kd.reshape(b, m, 2 * DIFF_HEADS, HEAD_DIM), diff_kn)
    if cos is not None:
        qs, ks, qd, kd = [apply_rope(t, cos, sin) for t in (qs, ks, qd, kd)]
    qd = qd.reshape(b, m, DIFF_HEADS, 2, HEAD_DIM)
    kd = kd.reshape(b, m, DIFF_HEADS, 2, HEAD_DIM)
    vs = vs.reshape(b, m, SWA_KV_HEADS, HEAD_DIM)
    vd = vd.reshape(b, m, DIFF_HEADS, DIFF_V_DIM)
    gates = jax.nn.sigmoid(g.astype(jnp.float32)).astype(z.dtype)
    return (u, qs, ks, vs, qd[..., 0, :], qd[..., 1, :], kd[..., 0, :], kd[..., 1, :], vd, gates)


def merge_branches(y_pool, y_swa, y_diff, gates, w_bp, w_bs, w_bd, w_out):
    g = gates.reshape(gates.shape[:-1] + (3, D_MODEL))
    merged = (g[..., 0, :] * (y_pool @ w_bp) + g[..., 1, :] * (y_swa @ w_bs)
              + g[..., 2, :] * (y_diff @ w_bd))
    return merged @ w_out


def sq_relu_mlp(h, w1, w2):
    return jnp.square(jax.nn.relu(h @ w1)) @ w2


def setup_inputs(seed: int = 0) -> dict:
    key = jax.random.key(seed)
    ks = jax.random.split(key, 24)
    f = jnp.float32
    nrm = lambda k, shape, s: jax.random.normal(k, shape, f) * s
    gain = lambda k, shape: 1.0 + 0.02 * jax.random.normal(k, shape, f)
    return {
        'x': nrm(ks[0], (BATCH, SEQ, D_MODEL), 1.0),
        'c': nrm(ks[1], (BATCH, D_MODEL), 1.0),
        'ctx': nrm(ks[2], (BATCH, CTX_LEN, D_MODEL), 1.0),
        'c_ctx': nrm(ks[3], (D_MODEL,), 1.0),
        'w_ada': nrm(ks[4], (DEPTH, D_MODEL, 6 * D_MODEL), 0.5 * D_MODEL ** -0.5),
        'b_ada': nrm(ks[5], (DEPTH, 6 * D_MODEL), 0.01),
        'norm1': gain(ks[6], (DEPTH, D_MODEL)),
        'norm2': gain(ks[7], (DEPTH, D_MODEL)),
        'w_in': nrm(ks[8], (DEPTH, D_MODEL, D_IN), D_MODEL ** -0.5),
        'w_pool': nrm(ks[9], (DEPTH, POOL_GROUPS, POOL_GROUP_DIM, POOL_GROUP_DIM), POOL_GROUP_DIM ** -0.5),
        'pool_scale': gain(ks[10], (DEPTH, POOL_WIDTH)),
        'swa_q_norm': gain(ks[11], (DEPTH, HEAD_DIM)),
        'swa_k_norm': gain(ks[12], (DEPTH, HEAD_DIM)),
        'swa_sink': nrm(ks[13], (DEPTH, SWA_Q_HEADS), 0.5),
        'diff_q_norm': gain(ks[14], (DEPTH, HEAD_DIM)),
        'diff_k_norm': gain(ks[15], (DEPTH, HEAD_DIM)),
        'diff_lambda': nrm(ks[16], (DEPTH, 4, HEAD_DIM), 0.1),
        'diff_subln': gain(ks[17], (DEPTH, DIFF_V_DIM)),
        'w_br_pool': nrm(ks[18], (DEPTH, POOL_WIDTH, D_MODEL), POOL_WIDTH ** -0.5),
        'w_br_swa': nrm(ks[19], (DEPTH, SWA_Q_HEADS * HEAD_DIM, D_MODEL), (SWA_Q_HEADS * HEAD_DIM) ** -0.5),
        'w_br_diff': nrm(ks[20], (DEPTH, DIFF_HEADS * DIFF_V_DIM, D_MODEL), (DIFF_HEADS * DIFF_V_DIM) ** -0.5),
        'w_out': nrm(ks[21], (DEPTH, D_MODEL, D_MODEL), D_MODEL ** -0.5),
        'w_ff1': nrm(ks[22], (DEPTH, D_MODEL, D_FF), D_MODEL ** -0.5),
        'w_ff2': nrm(ks[23], (DEPTH, D_FF, D_MODEL), D_FF ** -0.5),
    }


def reference(x, c, ctx, c_ctx, w_ada, b_ada, norm1, norm2, w_in, w_pool, pool_scale,
              swa_q_norm, swa_k_norm, swa_sink, diff_q_norm, diff_k_norm, diff_lambda,
              diff_subln, w_br_pool, w_br_swa, w_br_diff, w_out, w_ff1, w_ff2):
    n = x.shape[1]
    rows = n // GRID_W
    cos, sin = axial_rope_tables(rows)
    c_act = jax.nn.silu(c)
    cc_act = jax.nn.silu(c_ctx)
    xc = ctx
    for l in range(DEPTH):
        lam_init = 0.8 - 0.6 * math.exp(-0.3 * l)
        lam = (jnp.exp(jnp.sum(diff_lambda[l, 0] * diff_lambda[l, 1]))
               - jnp.exp(jnp.sum(diff_lambda[l, 2] * diff_lambda[l, 3])) + lam_init)
        mod = c_act @ w_ada[l] + b_ada[l]
        mod_c = cc_act @ w_ada[l] + b_ada[l]
        sh1, sc1, g1, sh2, sc2, g2 = [m[:, None, :] for m in jnp.split(mod, 6, axis=-1)]
        sh1c, sc1c, g1c, sh2c, sc2c, g2c = jnp.split(mod_c, 6, axis=-1)

        hc = modulate(rms_norm(xc, norm1[l]), sh1c, sc1c)
        (uc, qsc, ksc, vsc, q1c, q2c, k1c, k2c, vdc, gc) = project_streams(
            hc @ w_in[l], swa_q_norm[l], swa_k_norm[l], diff_q_norm[l], diff_k_norm[l], None, None)

        h = modulate(rms_norm(x, norm1[l]), sh1, sc1)
        (u, qs, ks_, vs, q1, q2, k1, k2, vd, gl) = project_streams(
            h @ w_in[l], swa_q_norm[l], swa_k_norm[l], diff_q_norm[l], diff_k_norm[l], cos, sin)
        y_pool = pool_mixer(u, w_pool[l], pool_scale[l])
        y_swa = windowed_gqa_latent(qs, ks_, vs, ksc, vsc, swa_sink[l])
        y_diff = diff_head_out(diff_attention_latent(q1, q2, k1, k2, vd, k1c, k2c, vdc, lam),
                               diff_subln[l], lam_init)
        x = x + g1 * merge_branches(y_pool, y_swa, y_diff, gl,
                                    w_br_pool[l], w_br_swa[l], w_br_diff[l], w_out[l])
        x = x + g2 * sq_relu_mlp(modulate(rms_norm(x, norm2[l]), sh2, sc2), w_ff1[l], w_ff2[l])

        if l < DEPTH - 1:
            yc_pool = pool_mixer(uc, w_pool[l], pool_scale[l])
            yc_swa = gqa_context(qsc, ksc, vsc, swa_sink[l])
            yc_diff = diff_head_out(diff_attn(q1c, q2c, k1c, k2c, vdc, lam), diff_subln[l], lam_init)
            xc = xc + g1c * merge_branches(yc_pool, yc_swa, yc_diff, gc,
                                           w_br_pool[l], w_br_swa[l], w_br_diff[l], w_out[l])
            xc = xc + g2c * sq_relu_mlp(modulate(rms_norm(xc, norm2[l]), sh2c, sc2c), w_ff1[l], w_ff2[l])
    return x
```

```python
import math
import numpy as np
import ml_dtypes
import concourse.bass as bass
import concourse.mybir as mybir
from concourse.bass_utils import run_bass_kernel_spmd

F32 = mybir.dt.float32
BF16 = mybir.dt.bfloat16
AF = mybir.ActivationFunctionType
ALU = mybir.AluOpType
AX = mybir.AxisListType

D = 1024
DEPTH = 2
CTX = 256
HD = 64
EPS = 1e-6
DFF = 4096
POOL_WINDOWS = (2, 4, 8, 16)
OFF_U, OFF_QS, OFF_KS, OFF_VS, OFF_QD, OFF_KD, OFF_VD, OFF_G = 0, 512, 1024, 1152, 1280, 1792, 2304, 2816
P1_BLOCKS = [(OFF_QS, 512), (OFF_KS, 128), (OFF_QD, 512), (OFF_KD, 512), (OFF_VS, 128), (OFF_VD, 512), (OFF_U, 512)]
NH = 26
C_VS, C_VD, C_U = 1664, 1792, 2304
P1_COLS = 2816


class Sched:
    COMPUTE = ("pe", "act", "dve", "pool")

    def __init__(self, nc, stack):
        self.nc = nc
        self.eng = {"pe": nc.tensor, "act": nc.scalar, "dve": nc.vector, "pool": nc.gpsimd, "sp": nc.sync}
        self.tsem = {e: stack.enter_context(nc.semaphore("tl_" + e)) for e in self.COMPUTE}
        self.tick = {e: 0 for e in self.COMPUTE}
        self.nds = {"sp": 8, "pool": 6, "act": 2}
        self.dsem = {q: [stack.enter_context(nc.semaphore("d_%s%d" % (q, i))) for i in range(n)]
                     for q, n in self.nds.items()}
        self.dcnt = {q: 0 for q in self.nds}
        self.ccsem = stack.enter_context(nc.semaphore("ccsem"))
        self.cccnt = 0
        self.ops = []
        self.last_w = {}
        self.readers = {}
        self.waited = {e: {} for e in self.eng}
        self.sems = {}
        self.n_inst = 0

    def op(self, eng, fn, reads=(), writes=(), kind="c"):
        idx = len(self.ops)
        deps = set()
        for k in reads:
            if k in self.last_w:
                deps.add(self.last_w[k])
        for k in writes:
            if k in self.last_w:
                deps.add(self.last_w[k])
            deps.update(self.readers.get(k, ()))
        for k in reads:
            self.readers.setdefault(k, []).append(idx)
        for k in writes:
            self.last_w[k] = idx
            self.readers[k] = []
        self.ops.append({"eng": eng, "fn": fn, "deps": deps, "kind": kind, "sig": kind != "c", "signal": None})
        return idx

    def dma(self, q, out, in_, reads=(), writes=()):
        return self.op(q, lambda e: e.dma_start(out=out, in_=in_), reads, writes, kind="d")

    def _wait(self, e, sem, val):
        w = self.waited[e]
        key = id(sem)
        if w.get(key, 0) < val:
            self.eng[e].wait_ge(sem, val)
            w[key] = val
            self.sems[key] = sem

    def flush(self):
        ops = self.ops
        for o in ops:
            for d in o["deps"]:
                ops[d]["sig"] = True
        last = {}
        for i, o in enumerate(ops):
            if o["kind"] == "c":
                last[o["eng"]] = i
        for i in last.values():
            ops[i]["sig"] = True
        final = {}
        for o in ops:
            e = o["eng"]
            for d in sorted(o["deps"]):
                sem, val = ops[d]["signal"]
                self._wait(e, sem, val)
            if o["kind"] == "d":
                j = self.dcnt[e]
                self.dcnt[e] += 1
                sem = self.dsem[e][j % self.nds[e]]
                val = 16 * (j // self.nds[e] + 1)
                self._wait(e, sem, val - 16)
                o["fn"](self.eng[e]).then_inc(sem, 16)
                o["signal"] = (sem, val)
                final[id(sem)] = (sem, val)
            elif o["kind"] == "cc":
                self.cccnt += 1
                o["fn"](self.eng[e]).then_inc(self.ccsem)
                o["signal"] = (self.ccsem, self.cccnt)
                final[id(self.ccsem)] = (self.ccsem, self.cccnt)
            else:
                ins = o["fn"](self.eng[e])
                if o["sig"]:
                    self.tick[e] += 1
                    ins.then_inc(self.tsem[e], 1)
                    o["signal"] = (self.tsem[e], self.tick[e])
                    final[id(self.tsem[e])] = (self.tsem[e], self.tick[e])
            self.n_inst += 1
        for e in self.eng:
            for sem, val in final.values():
                self._wait(e, sem, val)
        self.ops = []
        self.last_w = {}
        self.readers = {}


def dram_bcast(ap_1d, parts, mid=None):
    n = ap_1d.shape[-1]
    dims = [[0, parts]]
    if mid is not None:
        dims.append([0, mid])
    dims.append([1, n])
    return bass.AP(ap_1d.tensor, ap_1d.offset, dims)


class Ctx:
    pass


_UID = [0]


def mk_alloc(g, st):
    nc = g.nc
    _UID[0] += 1
    u = "_%d" % _UID[0]

    def sb(name, shape, dt=F32):
        return st.enter_context(nc.sbuf_tensor(name + u, list(shape), dt))

    def ps(name, shape, dt=F32):
        return st.enter_context(nc.psum_tensor(name + u, list(shape), dt))
    return sb, ps


def build_program(T, debug_outs=(), depth=DEPTH, phases=None):
    assert T % 512 == 0
    NT = T // 128
    NTC = CTX // 128
    TK = T + CTX
    NKT = TK // 128
    nc = bass.Bass("TRN2", target_bir_lowering=False)
    g = Ctx()
    g.nc = nc
    g.T, g.NT, g.NTC, g.TK, g.NKT = T, NT, NTC, TK, NKT

    def din(name, shape, dt=F32):
        return nc.dram_tensor(name, list(shape), dt, kind="ExternalInput").ap()

    def dscr(name, shape, dt=BF16):
        kind = "ExternalOutput" if name in debug_outs else "Internal"
        return nc.dram_tensor(name, list(shape), dt, kind=kind).ap()

    I = {}
    I["x"] = din("x", [T, D])
    I["ctx"] = din("ctx", [CTX, D])
    I["cvec"] = din("cvec", [128, 16])
    I["w_ada"] = din("w_ada", [DEPTH, D, 6 * D])
    I["b_ada"] = din("b_ada", [DEPTH, 6 * D])
    I["norm1"] = din("norm1", [DEPTH, D])
    I["norm2"] = din("norm2", [DEPTH, D])
    I["w_in"] = din("w_in", [DEPTH, D, 5888])
    I["w_pool"] = din("w_pool", [DEPTH, 4, 128, 128])
    I["pool_scale"] = din("pool_scale", [DEPTH, 512])
    I["swa_q_norm"] = din("swa_q_norm", [DEPTH, 64])
    I["swa_k_norm"] = din("swa_k_norm", [DEPTH, 64])
    I["swa_sink"] = din("swa_sink", [DEPTH, 8])
    I["diff_q_norm"] = din("diff_q_norm", [DEPTH, 64])
    I["diff_k_norm"] = din("diff_k_norm", [DEPTH, 64])
    I["diff_lambda"] = din("diff_lambda", [DEPTH, 256])
    I["diff_subln"] = din("diff_subln", [DEPTH, 128])
    I["w_br_pool"] = din("w_br_pool", [DEPTH, 512, D])
    I["w_br_swa"] = din("w_br_swa", [DEPTH, 512, D])
    I["w_br_diff"] = din("w_br_diff", [DEPTH, 512, D])
    I["w_out"] = din("w_out", [DEPTH, D, D])
    I["w_ff1"] = din("w_ff1", [DEPTH, D, DFF])
    I["w_ff2"] = din("w_ff2", [DEPTH, DFF, D])
    I["cos"] = din("cos", [T, 32])
    I["sin"] = din("sin", [T, 32])
    I["cosc"] = din("cosc", [CTX, 32])
    I["sinc"] = din("sinc", [CTX, 32])
    I["ident"] = din("ident", [128, 128], BF16)
    I["ones"] = din("ones", [128, 128], BF16)
    I["band"] = din("band", [4, 5, 128, 128], BF16)
    I["bandc"] = din("bandc", [4, 5, 128, 128], BF16)
    I["mask_prev"] = din("mask_prev", [128, 128], BF16)
    I["mask_next"] = din("mask_next", [128, 128], BF16)
    g.I = I
    y = nc.dram_tensor("y", [T, D], F32, kind="ExternalOutput").ap()
    g.y = y

    Sc = {}
    Sc["mod"] = dscr("mod", [DEPTH, 2, 6 * D], F32)
    for s, n in (("L", T), ("C", CTX)):
        Sc["hT" + s] = dscr("hT" + s, [8, 128, n])
        Sc["QsT" + s] = dscr("QsT" + s, [64, 8, n])
        Sc["QdT" + s] = dscr("QdT" + s, [4, 128, n])
        Sc["u" + s] = dscr("u" + s, [n, 512])
        Sc["ypT" + s] = dscr("ypT" + s, [4, 128, n])
        Sc["ysT" + s] = dscr("ysT" + s, [4, 128, n])
        Sc["ydT" + s] = dscr("ydT" + s, [4, 128, n])
        Sc["x1" + s] = dscr("x1" + s, [n, D], F32)
        Sc["xo" + s] = dscr("xo" + s, [n, D], F32)
    Sc["KsT"] = dscr("KsT", [64, 2, TK])
    Sc["KdT"] = dscr("KdT", [4, 128, TK])
    Sc["Vs"] = dscr("Vs", [TK, 128])
    Sc["Vd"] = dscr("Vd", [TK, 512])
    g.Sc = Sc

    import contextlib
    with contextlib.ExitStack() as stack:
        stack.enter_context(nc.allow_non_contiguous_dma(reason="small strided parameter / layout loads"))
        S = Sched(nc, stack)
        g.S = S
        ident = stack.enter_context(nc.sbuf_tensor("ident_sb", [128, 128], BF16))
        ones = stack.enter_context(nc.sbuf_tensor("ones_sb", [128, 128], BF16))
        S.dma("sp", ident[:], I["ident"], writes=["ident"])
        S.dma("sp", ones[:], I["ones"], writes=["ones"])
        g.ident, g.ones = ident, ones
        S.flush()

        for l in range(depth):
            last = l == DEPTH - 1
            phase_mod(g, l)
            xin_L = I["x"] if l == 0 else Sc["xoL"]
            xin_C = I["ctx"] if l == 0 else Sc["xoC"]
            phase_p1(g, l, "C", xin_C, NTC, I["cosc"], I["sinc"], T)
            phase_p1(g, l, "L", xin_L, NT, I["cos"], I["sin"], 0)
            streams = [("L", T, xin_L, (y if last else Sc["xoL"]))]
            if not last:
                streams.append(("C", CTX, xin_C, Sc["xoC"]))
            for s, n, xin, xout in streams:
                if phases is None or "pool" in phases:
                    phase_pool(g, l, s, n)
                if phases is None or "swa" in phases:
                    phase_swa(g, l, s, n)
                if phases is None or "diff" in phases:
                    phase_diff(g, l, s, n)
                if phases is None or "merge" in phases:
                    phase_merge(g, l, s, n, xin)
                if phases is None or "ffn" in phases:
                    phase_ffn(g, l, s, n, xout)
    return nc


def phase_mod(g, l):
    nc, S, I, Sc = g.nc, g.S, g.I, g.Sc
    import contextlib
    with contextlib.ExitStack() as st:
        sb, ps = mk_alloc(g, st)
        cv = sb("cv", [128, 16])
        cact = sb("cact", [128, 16])
        wada = sb("wada", [128, 2, 8, 512])
        bada = sb("bada", [2, 2, 512])
        modo = sb("modo", [2, 2, 512])
        modps = ps("modps", [2, 2, 512])
        S.dma("sp", cv[:], I["cvec"], writes=["cv"])
        S.op("act", lambda e: e.activation(out=cact[:], in_=cv[:], func=AF.Silu), ["cv"], ["cact"])
        for n in range(12):
            b = n % 2
            S.dma("sp", wada[:, b], I["w_ada"][l, :, n * 512:(n + 1) * 512].rearrange("(kc p) n -> p kc n", p=128),
                  writes=["wada%d" % b])
            S.dma("sp", bada[:, b], dram_bcast(I["b_ada"][l, n * 512:(n + 1) * 512], 2), writes=["bada%d" % b])
            for kc in range(8):
                lhsT = cact[:].rearrange("p (r k) -> p k r", r=2)[:, kc, :]
                S.op("pe", lambda e, lhsT=lhsT, kc=kc, b=b: e.matmul(modps[:, b], lhsT, wada[:, b, kc, :],
                                                                  start=(kc == 0), stop=(kc == 7)),
                     ["cact", "wada%d" % b], ["modps%d" % b])
            S.op("dve", lambda e, b=b: e.tensor_tensor(out=modo[:, b], in0=modps[:, b], in1=bada[:, b], op=ALU.add),
                 ["modps%d" % b, "bada%d" % b], ["modo%d" % b])
            S.dma("sp", Sc["mod"][l, :, n * 512:(n + 1) * 512], modo[:, b], reads=["modo%d" % b])
        S.flush()


def load_bc(S, q, tile_ap, vec_ap, key):
    S.dma(q, tile_ap, dram_bcast(vec_ap, 128), writes=[key])


def phase_p1(g, l, s, xin, ntiles, cos_d, sin_d, koff):
    nc, S, I, Sc = g.nc, g.S, g.I, g.Sc
    r = 0 if s == "L" else 1
    n = ntiles * 128
    import contextlib
    with contextlib.ExitStack() as st:
        sb, ps = mk_alloc(g, st)
        w = sb("p1w", [128, 8, P1_COLS], BF16)
        A1 = sb("A1", [128, D])
        B1 = sb("B1", [128, D])
        nrm = sb("nrm", [128, D])
        G = sb("G", [128, NH, 64])
        xt = sb("xt", [128, 2, D])
        junk = sb("junk", [128, D])
        ssx = sb("ssx", [128, 2])
        hb = sb("hb", [128, 2, D], BF16)
        hT = sb("hT", [128, 2, 8, 128], BF16)
        zs = sb("zs", [128, 2, 1664])
        sq = sb("sq", [128, 1664])
        ssh = sb("ssh", [128, 2, NH])
        qn = sb("qn", [128, NH, 64])
        t1 = sb("t1", [128, NH, 32])
        t2 = sb("t2", [128, NH, 32])
        t3 = sb("t3", [128, NH, 32])
        t4 = sb("t4", [128, NH, 32])
        qo = sb("qo", [128, 2, NH, 64], BF16)
        vu = sb("vu", [128, 2, 1152], BF16)
        cs = sb("cs", [128, 2, 2, 32])
        stQs = sb("stQs", [64, 2, 8, 512], BF16)
        stKs = sb("stKs", [64, 2, 2, 512], BF16)
        stQd = sb("stQd", [128, 2, 4, 512], BF16)
        stKd = sb("stKd", [128, 2, 4, 512], BF16)
        zp = [ps("zp%d" % i, [128, 512]) for i in range(6)]
        tr1 = ps("tr1", [128, 8, 128], BF16)
        tr2 = ps("tr2", [128, 8, 128], BF16)

        c0 = 0
        for (off, wd) in P1_BLOCKS:
            for kh in range(2):
                S.dma("pool", w[:, kh * 4:(kh + 1) * 4, c0:c0 + wd],
                      I["w_in"][l, kh * 512:(kh + 1) * 512, off:off + wd].rearrange("(kc p) n -> p kc n", p=128),
                      writes=["w"])
            c0 += wd
        load_bc(S, "sp", B1[:], Sc["mod"][l, r, 0:D], "B1")
        load_bc(S, "sp", A1[:], Sc["mod"][l, r, D:2 * D], "A1")
        load_bc(S, "sp", nrm[:], I["norm1"][l], "nrm")
        S.op("dve", lambda e: e.scalar_tensor_tensor(out=A1[:], in0=A1[:], scalar=1.0, in1=nrm[:],
                                                     op0=ALU.add, op1=ALU.mult), ["A1", "nrm"], ["A1"])
        for (h0, hn, nm) in ((0, 8, "swa_q_norm"), (8, 2, "swa_k_norm"), (10, 8, "diff_q_norm"), (18, 8, "diff_k_norm")):
            S.dma("sp", G[:, h0:h0 + hn, :], dram_bcast(I[nm][l], 128, hn), writes=["G"])

        for t in range(ntiles):
            b = t % 2
            gi = t // 4
            gb = gi % 2
            tq = t % 4
            S.dma("sp", xt[:, b], xin[t * 128:(t + 1) * 128, :], writes=["xt%d" % b])
            S.dma("sp", cs[:, b, 0], cos_d[t * 128:(t + 1) * 128, :], writes=["cs%d" % b])
            S.dma("sp", cs[:, b, 1], sin_d[t * 128:(t + 1) * 128, :], writes=["cs%d" % b])
            S.op("act", lambda e, b=b: e.activation(out=junk[:], in_=xt[:, b], func=AF.Square, accum_out=ssx[:, b:b + 1]),
                 ["xt%d" % b], ["junk", "ssx%d" % b])
            S.op("act", lambda e, b=b: e.activation(out=ssx[:, b:b + 1], in_=ssx[:, b:b + 1], func=AF.Sqrt, bias=EPS, scale=1.0 / D),
                 ["ssx%d" % b], ["ssx%d" % b])
            S.op("dve", lambda e, b=b: e.reciprocal(out=ssx[:, b:b + 1], in_=ssx[:, b:b + 1]), ["ssx%d" % b], ["ssx%d" % b])
            S.op("dve", lambda e, b=b: e.scalar_tensor_tensor(out=xt[:, b], in0=xt[:, b], scalar=ssx[:, b:b + 1], in1=A1[:],
                                                              op0=ALU.mult, op1=ALU.mult), ["xt%d" % b, "ssx%d" % b, "A1"], ["xt%d" % b])
            S.op("pool", lambda e, b=b: e.tensor_tensor(out=hb[:, b], in0=xt[:, b], in1=B1[:], op=ALU.add),
                 ["xt%d" % b, "B1"], ["hb%d" % b])
            for kc in range(8):
                S.op("pe", lambda e, b=b, kc=kc: e.transpose(tr1[:, kc, :], hb[:, b, kc * 128:(kc + 1) * 128], g.ident[:]),
                     ["hb%d" % b, "ident"], ["tr1"])
            S.op("act", lambda e, b=b: e.copy(out=hT[:, b], in_=tr1[:]), ["tr1"], ["hT%d" % b])
            S.dma("sp", Sc["hT" + s][:, :, t * 128:(t + 1) * 128].rearrange("k p t -> p k t"), hT[:, b], reads=["hT%d" % b])
            for ci in range(6):
                cw = min(512, P1_COLS - ci * 512)
                for kc in range(8):
                    S.op("pe", lambda e, b=b, kc=kc, ci=ci, cw=cw: e.matmul(zp[ci][:, 0:cw], hT[:, b, kc, :], w[:, kc, ci * 512:ci * 512 + cw],
                                                                       start=(kc == 0), stop=(kc == 7)),
                         ["hT%d" % b, "w"], ["zp%d" % ci])
            for ci in range(6):
                lo, hi = ci * 512, min(P1_COLS, ci * 512 + 512)
                if hi <= 1664 or lo < 1664:
                    h2 = min(hi, 1664)
                    S.op("act", lambda e, b=b, ci=ci, lo=lo, h2=h2: e.copy(out=zs[:, b, lo:h2], in_=zp[ci][:, 0:h2 - lo]),
                         ["zp%d" % ci], ["zs%d" % b])
                    S.op("act", lambda e, ci=ci, lo=lo, h2=h2: e.activation(out=sq[:, lo:h2], in_=zp[ci][:, 0:h2 - lo], func=AF.Square),
                         ["zp%d" % ci], ["sq"])
                if hi > 1664:
                    l2 = max(lo, 1664)
                    S.op("act", lambda e, b=b, ci=ci, lo=lo, l2=l2, hi=hi: e.copy(out=vu[:, b, l2 - 1664:hi - 1664], in_=zp[ci][:, l2 - lo:hi - lo]),
                         ["zp%d" % ci], ["vu%d" % b])
            S.dma("sp", Sc["Vs"][koff + t * 128:koff + (t + 1) * 128, :], vu[:, b, 0:128], reads=["vu%d" % b])
            S.dma("sp", Sc["Vd"][koff + t * 128:koff + (t + 1) * 128, :], vu[:, b, 128:640], reads=["vu%d" % b])
            S.dma("sp", Sc["u" + s][t * 128:(t + 1) * 128, :], vu[:, b, 640:1152], reads=["vu%d" % b])
            S.op("dve", lambda e, b=b: e.tensor_reduce(out=ssh[:, b], in_=sq[:].rearrange("p (h d) -> p h d", d=64), axis=AX.X, op=ALU.add),
                 ["sq"], ["ssh%d" % b])
            S.op("act", lambda e, b=b: e.activation(out=ssh[:, b], in_=ssh[:, b], func=AF.Sqrt, bias=EPS, scale=1.0 / 64),
                 ["ssh%d" % b], ["ssh%d" % b])
            S.op("dve", lambda e, b=b: e.reciprocal(out=ssh[:, b], in_=ssh[:, b]), ["ssh%d" % b], ["ssh%d" % b])
            zh = zs[:, b].rearrange("p (h d) -> p h d", d=64)
            S.op("dve", lambda e, b=b, zh=zh: e.tensor_tensor(out=qn[:], in0=zh, in1=ssh[:, b].unsqueeze(2).to_broadcast([128, NH, 64]), op=ALU.mult),
                 ["zs%d" % b, "ssh%d" % b], ["qn"])
            S.op("pool", lambda e: e.tensor_tensor(out=qn[:], in0=qn[:], in1=G[:], op=ALU.mult), ["qn", "G"], ["qn"])
            cb = cs[:, b, 0].unsqueeze(1).to_broadcast([128, NH, 32])
            sb_ = cs[:, b, 1].unsqueeze(1).to_broadcast([128, NH, 32])
            x1 = qn[:, :, 0:32]
            x2 = qn[:, :, 32:64]
            S.op("dve", lambda e, cb=cb, x1=x1: e.tensor_tensor(out=t1[:], in0=x1, in1=cb, op=ALU.mult), ["qn", "cs%d" % b], ["t1"])
            S.op("dve", lambda e, sb_=sb_, x2=x2: e.tensor_tensor(out=t2[:], in0=x2, in1=sb_, op=ALU.mult), ["qn", "cs%d" % b], ["t2"])
            S.op("dve", lambda e, b=b: e.tensor_tensor(out=qo[:, b, :, 0:32], in0=t1[:], in1=t2[:], op=ALU.subtract), ["t1", "t2"], ["qo%d" % b])
            S.op("pool", lambda e, cb=cb, x2=x2: e.tensor_tensor(out=t3[:], in0=x2, in1=cb, op=ALU.mult), ["qn", "cs%d" % b], ["t3"])
            S.op("pool", lambda e, sb_=sb_, x1=x1: e.tensor_tensor(out=t4[:], in0=x1, in1=sb_, op=ALU.mult), ["qn", "cs%d" % b], ["t4"])
            S.op("pool", lambda e, b=b: e.tensor_tensor(out=qo[:, b, :, 32:64], in0=t3[:], in1=t4[:], op=ALU.add), ["t3", "t4"], ["qo%d" % b])
            for h in range(8):
                S.op("pe", lambda e, b=b, h=h: e.transpose(tr2[0:64, h, :], qo[:, b, h, :], g.ident[:]), ["qo%d" % b, "ident"], ["tr2"])
            S.op("act", lambda e, gb=gb, tq=tq: e.copy(out=stQs[:, gb, :, tq * 128:(tq + 1) * 128], in_=tr2[0:64]), ["tr2"], ["stQs%d" % gb])
            for h in range(2):
                S.op("pe", lambda e, b=b, h=h: e.transpose(tr2[0:64, h, :], qo[:, b, 8 + h, :], g.ident[:]), ["qo%d" % b, "ident"], ["tr2"])
            S.op("act", lambda e, gb=gb, tq=tq: e.copy(out=stKs[:, gb, :, tq * 128:(tq + 1) * 128], in_=tr2[0:64, 0:2, :]), ["tr2"], ["stKs%d" % gb])
            for j in range(8):
                src = qo[:, b, 10 + 2 * j:12 + 2 * j, :].rearrange("p h d -> p (h d)")
                S.op("pe", lambda e, j=j, src=src: e.transpose(tr1[:, j, :], src, g.ident[:]), ["qo%d" % b, "ident"], ["tr1"])
            S.op("act", lambda e, gb=gb, tq=tq: e.copy(out=stQd[:, gb, :, tq * 128:(tq + 1) * 128], in_=tr1[:, 0:4, :]), ["tr1"], ["stQd%d" % gb])
            S.op("act", lambda e, gb=gb, tq=tq: e.copy(out=stKd[:, gb, :, tq * 128:(tq + 1) * 128], in_=tr1[:, 4:8, :]), ["tr1"], ["stKd%d" % gb])
            if tq == 3 or t == ntiles - 1:
                nt = (tq + 1) * 128
                t0 = gi * 512
                S.dma("sp", Sc["QsT" + s][:, :, t0:t0 + nt], stQs[:, gb, :, 0:nt], reads=["stQs%d" % gb])
                S.dma("sp", Sc["KsT"][:, :, koff + t0:koff + t0 + nt], stKs[:, gb, :, 0:nt], reads=["stKs%d" % gb])
                S.dma("sp", Sc["QdT" + s][:, :, t0:t0 + nt].rearrange("h p t -> p h t"), stQd[:, gb, :, 0:nt], reads=["stQd%d" % gb])
                S.dma("sp", Sc["KdT"][:, :, koff + t0:koff + t0 + nt].rearrange("h p t -> p h t"), stKd[:, gb, :, 0:nt], reads=["stKd%d" % gb])
        S.flush()


def phase_pool(g, l, s, n):
    nc, S, I, Sc = g.nc, g.S, g.I, g.Sc
    nt = n // 128
    band_d = I["band"] if s == "L" else I["bandc"]
    import contextlib
    with contextlib.ExitStack() as st:
        sb, ps = mk_alloc(g, st)
        band = sb("band", [128, 4, 5, 128], BF16)
        wp = sb("wp", [128, 4, 128], BF16)
        psc = sb("psc", [128, 4])
        ut = sb("ut", [128, 3, 512], BF16)
        pl = sb("pl", [128, 2, 128], BF16)
        yo = sb("yo", [128, 2, 4, 128], BF16)
        pp = [ps("pp%d" % i, [128, 128]) for i in range(2)]
        yp = [ps("yp%d" % i, [128, 128]) for i in range(2)]
        S.dma("sp", band[:], band_d.rearrange("w k s t -> s w k t"), writes=["band"])
        S.dma("pool", wp[:], I["w_pool"][l].rearrange("g c d -> c g d"), writes=["wp"])
        S.dma("sp", psc[:], I["pool_scale"][l].rearrange("(g d) -> d g", g=4), writes=["psc"])
        S.dma("sp", ut[:, 0], Sc["u" + s][0:128, :], writes=["ut0"])
        for t in range(nt):
            b = t % 2
            if t + 1 < nt:
                S.dma("sp", ut[:, (t + 1) % 3], Sc["u" + s][(t + 1) * 128:(t + 2) * 128, :], writes=["ut%d" % ((t + 1) % 3)])
            srcs = []
            if t > 0:
                srcs.append(((t - 1) % 3, 0))
            srcs.append((t % 3, 3 if t == 0 else (4 if t == nt - 1 else 1)))
            if t + 1 < nt:
                srcs.append(((t + 1) % 3, 2))
            for gq in range(4):
                pb = gq % 2
                for i, (slot, kind) in enumerate(srcs):
                    S.op("pe", lambda e, pb=pb, slot=slot, kind=kind, gq=gq, i=i, ns=len(srcs): e.matmul(
                        pp[pb][:], ut[:, slot, gq * 128:(gq + 1) * 128], band[:, gq, kind, :], start=(i == 0), stop=(i == ns - 1)),
                        ["ut%d" % slot, "band"], ["pp%d" % pb])
                S.op("act", lambda e, pb=pb: e.copy(out=pl[:, pb], in_=pp[pb][:]), ["pp%d" % pb], ["pl%d" % pb])
                S.op("pe", lambda e, pb=pb, gq=gq: e.matmul(yp[pb][:], wp[:, gq, :], pl[:, pb], start=True, stop=True),
                     ["pl%d" % pb, "wp"], ["yp%d" % pb])
                S.op("dve", lambda e, pb=pb, gq=gq, b=b: e.tensor_scalar(out=yo[:, b, gq, :], in0=yp[pb][:], scalar1=psc[:, gq:gq + 1],
                                                                         scalar2=None, op0=ALU.mult),
                     ["yp%d" % pb, "psc"], ["yo%d" % b])
            S.dma("sp", Sc["ypT" + s][:, :, t * 128:(t + 1) * 128].rearrange("g p t -> p g t"), yo[:, b], reads=["yo%d" % b])
        S.flush()


def phase_swa(g, l, s, n):
    nc, S, I, Sc = g.nc, g.S, g.I, g.Sc
    T, TK = g.T, g.TK
    nt = n // 128
    nkt_all = TK // 128
    ctx_kts = [T // 128, T // 128 + 1]
    import contextlib
    with contextlib.ExitStack() as st:
        sb, ps = mk_alloc(g, st)
        kst = sb("kst", [64, 2, TK], BF16)
        vs = sb("vs", [128, nkt_all, 128], BF16)
        qb_ = sb("qb", [64, 2, 8, 128], BF16)
        mk = sb("mk", [128, 2, 128], BF16)
        esk = sb("esk", [64, 8])
        pt = sb("pt", [128, 3, 512], BF16)
        den = sb("den", [64, 512])
        yo = sb("yo", [64, 2, 4, 128], BF16)
        spp = [ps("spp%d" % i, [128, 512]) for i in range(2)]
        opp = [ps("opp%d" % i, [64, 512]) for i in range(2)]
        rpp = [ps("rpp%d" % i, [64, 512]) for i in range(2)]
        if s == "L":
            S.dma("sp", kst[:], Sc["KsT"], writes=["kst"])
            S.dma("sp", vs[:], Sc["Vs"].rearrange("(kt p) d -> p kt d", p=128), writes=["vs"])
        else:
            S.dma("sp", kst[:, :, T:TK], Sc["KsT"][:, :, T:TK], writes=["kst"])
            S.dma("sp", vs[:, T // 128:, :], Sc["Vs"][T:TK, :].rearrange("(kt p) d -> p kt d", p=128), writes=["vs"])
        S.dma("sp", mk[:, 0], I["mask_prev"], writes=["mk"])
        S.dma("sp", mk[:, 1], I["mask_next"], writes=["mk"])
        S.dma("sp", esk[:], dram_bcast(I["swa_sink"][l], 64), writes=["esk"])
        S.op("act", lambda e: e.activation(out=esk[:], in_=esk[:], func=AF.Exp), ["esk"], ["esk"])
        pcnt = 0
        for qi in range(nt):
            b = qi % 2
            S.dma("sp", qb_[:, b], Sc["QsT" + s][:, :, qi * 128:(qi + 1) * 128], writes=["qb%d" % b])
            kts = []
            if s == "L":
                if qi > 0:
                    kts.append((qi - 1, 0))
                kts.append((qi, None))
                if qi + 1 < nt:
                    kts.append((qi + 1, 1))
            kts += [(k, None) for k in ctx_kts]
            for gk in range(2):
                ob = gk
                for i, (kt, m) in enumerate(kts):
                    sl = pcnt % 3
                    sp_ = pcnt % 2
                    pcnt += 1
                    S.op("pe", lambda e, sp_=sp_, gk=gk, kt=kt, b=b: e.matmul(
                        spp[sp_][:], kst[:, gk, kt * 128:(kt + 1) * 128], qb_[:, b, 4 * gk:4 * gk + 4, :], start=True, stop=True),
                        ["kst", "qb%d" % b], ["spp%d" % sp_])
                    S.op("act", lambda e, sp_=sp_, sl=sl: e.activation(out=pt[:, sl], in_=spp[sp_][:], func=AF.Exp, scale=0.125),
                         ["spp%d" % sp_], ["pt%d" % sl])
                    if m is not None:
                        pv = pt[:, sl].rearrange("p (h q) -> p h q", h=4)
                        S.op("dve", lambda e, pv=pv, m=m: e.tensor_tensor(out=pv, in0=pv, in1=mk[:, m].unsqueeze(1).to_broadcast([128, 4, 128]),
                                                                          op=ALU.mult), ["pt%d" % sl, "mk"], ["pt%d" % sl])
                    S.op("pe", lambda e, ob=ob, kt=kt, gk=gk, sl=sl, i=i, nk=len(kts): e.matmul(
                        opp[ob][:], vs[:, kt, gk * 64:(gk + 1) * 64], pt[:, sl], start=(i == 0), stop=(i == nk - 1)),
                        ["vs", "pt%d" % sl], ["opp%d" % ob])
                    S.op("pe", lambda e, ob=ob, sl=sl, i=i, nk=len(kts): e.matmul(
                        rpp[ob][:], g.ones[:, 0:64], pt[:, sl], start=(i == 0), stop=(i == nk - 1)),
                        ["ones", "pt%d" % sl], ["rpp%d" % ob])
                dv = den[:].rearrange("p (h q) -> p h q", h=4)
                S.op("dve", lambda e, ob=ob, gk=gk, dv=dv: e.tensor_tensor(
                    out=dv, in0=rpp[ob][:].rearrange("p (h q) -> p h q", h=4),
                    in1=esk[:, 4 * gk:4 * gk + 4].unsqueeze(2).to_broadcast([64, 4, 128]), op=ALU.add),
                    ["rpp%d" % ob, "esk"], ["den"])
                S.op("dve", lambda e: e.reciprocal(out=den[:], in_=den[:]), ["den"], ["den"])
                S.op("dve", lambda e, ob=ob, gk=gk: e.tensor_tensor(out=yo[:, gk].rearrange("p h q -> p (h q)"), in0=opp[ob][:], in1=den[:], op=ALU.mult),
                     ["opp%d" % ob, "den"], ["yo%d" % gk])
                for jj in range(2):
                    j = 2 * gk + jj
                    dst = Sc["ysT" + s][j].rearrange("(hh d) t -> d hh t", hh=2)[:, :, qi * 128:(qi + 1) * 128]
                    S.dma("sp", dst, yo[:, gk, 2 * jj:2 * jj + 2, :], reads=["yo%d" % gk])
        S.flush()


def phase_diff(g, l, s, n):
    nc, S, I, Sc = g.nc, g.S, g.I, g.Sc
    T, TK = g.T, g.TK
    lam_init = 0.8 - 0.6 * math.exp(-0.3 * l)
    k0 = 0 if s == "L" else T
    nk = TK - k0
    nkt = nk // 128
    GW = min(512, n)
    ng = n // GW
    import contextlib
    with contextlib.ExitStack() as st:
        sb, ps = mk_alloc(g, st)
        kt_ = sb("dk", [128, 2, nk], BF16)
        vt = sb("dv", [128, 2, nkt, 128], BF16)
        qt = sb("dq", [128, 2, n], BF16)
        dl = sb("dl", [128, 256])
        pr = sb("pr", [128, 2, 64])
        ee = sb("ee", [128, 2])
        nlam = sb("nlam", [128, 1])
        sg = sb("sg", [128, 1])
        hi = sb("hi", [128, GW], BF16)
        lo = sb("lo", [128, GW], BF16)
        pt = sb("pt", [128, 2, 3, GW], BF16)
        acc = sb("acc", [128, 2, GW])
        ra = sb("ra", [128, GW])
        oa = sb("oa", [128, GW])
        ob = sb("ob", [128, GW])
        sq = sb("sq", [128, GW])
        yb = sb("yb", [128, 2, GW], BF16)
        sps = [ps("sps%d" % i, [128, GW]) for i in range(4)]
        ops_ = [ps("ops%d" % i, [128, GW]) for i in range(2)]
        rps = ps("rps", [128, GW])
        S.dma("sp", dl[:], dram_bcast(I["diff_lambda"][l], 128), writes=["dl"])
        dl4 = dl[:].rearrange("p (a b d) -> p a b d", a=2, b=2)
        S.op("dve", lambda e: e.tensor_tensor(out=pr[:], in0=dl4[:, :, 0, :], in1=dl4[:, :, 1, :], op=ALU.mult), ["dl"], ["pr"])
        S.op("dve", lambda e: e.tensor_reduce(out=ee[:], in_=pr[:], axis=AX.X, op=ALU.add), ["pr"], ["ee"])
        S.op("act", lambda e: e.activation(out=ee[:], in_=ee[:], func=AF.Exp), ["ee"], ["ee"])
        S.op("dve", lambda e: e.tensor_tensor(out=nlam[:], in0=ee[:, 1:2], in1=ee[:, 0:1], op=ALU.subtract), ["ee"], ["nlam"])
        S.op("dve", lambda e: e.tensor_scalar(out=nlam[:], in0=nlam[:], scalar1=-lam_init, scalar2=None, op0=ALU.add), ["nlam"], ["nlam"])
        S.dma("sp", sg[:], I["diff_subln"][l].rearrange("(p o) -> p o", o=1), writes=["sg"])
        S.op("dve", lambda e: e.tensor_scalar(out=sg[:], in0=sg[:], scalar1=(1.0 - lam_init), scalar2=None, op0=ALU.mult), ["sg"], ["sg"])

        def load_head(h):
            hb = h % 2
            S.dma("sp", kt_[:, hb], Sc["KdT"][h][:, k0:TK], writes=["dk%d" % hb])
            S.dma("sp", vt[:, hb], Sc["Vd"][k0:TK, h * 128:(h + 1) * 128].rearrange("(kt p) d -> p kt d", p=128), writes=["dv%d" % hb])
            S.dma("sp", qt[:, hb], Sc["QdT" + s][h], writes=["dq%d" % hb])
        load_head(0)
        cnt = 0
        for h in range(4):
            hb = h % 2
            if h + 1 < 4:
                load_head(h + 1)
            for gq in range(ng):
                qs_ = slice(gq * GW, (gq + 1) * GW)

                def scores(kt, c):
                    pb = 2 * (c % 2)
                    for hf in range(2):
                        S.op("pe", lambda e, pb=pb, hf=hf, kt=kt, hb=hb, qs_=qs_: e.matmul(
                            sps[pb + hf][:], kt_[hf * 64:(hf + 1) * 64, hb, kt * 128:(kt + 1) * 128], qt[hf * 64:(hf + 1) * 64, hb, qs_],
                            start=True, stop=True), ["dk%d" % hb, "dq%d" % hb], ["sps%d" % (pb + hf)])
                scores(0, cnt)
                for kt in range(nkt):
                    c = cnt + kt
                    if kt + 1 < nkt:
                        scores(kt + 1, c + 1)
                    pb = 2 * (c % 2)
                    sl = c % 3
                    for hf in range(2):
                        S.op("act", lambda e, pb=pb, hf=hf, sl=sl: e.activation(out=pt[:, hf, sl], in_=sps[pb + hf][:], func=AF.Exp, scale=0.125),
                             ["sps%d" % (pb + hf)], ["pt%d_%d" % (hf, sl)])
                    for hf in range(2):
                        S.op("pe", lambda e, hf=hf, sl=sl, kt=kt, hb=hb: e.matmul(ops_[hf][:], vt[:, hb, kt, :], pt[:, hf, sl],
                                                                        start=(kt == 0), stop=(kt == nkt - 1)),
                             ["dv%d" % hb, "pt%d_%d" % (hf, sl)], ["ops%d" % hf])
                    for hf, en in ((0, "dve"), (1, "pool")):
                        if kt == 0:
                            S.op(en, lambda e, hf=hf, sl=sl: e.tensor_copy(out=acc[:, hf], in_=pt[:, hf, sl]), ["pt%d_%d" % (hf, sl)], ["acc%d" % hf])
                        else:
                            S.op(en, lambda e, hf=hf, sl=sl: e.tensor_tensor(out=acc[:, hf], in0=acc[:, hf], in1=pt[:, hf, sl], op=ALU.add),
                                 ["pt%d_%d" % (hf, sl), "acc%d" % hf], ["acc%d" % hf])
                cnt += nkt
                def colsum(src_ap, src_key):
                    S.op("dve", lambda e: e.tensor_copy(out=hi[:], in_=src_ap), [src_key], ["hi"])
                    S.op("pool", lambda e: e.tensor_tensor(out=sq[:], in0=src_ap, in1=hi[:], op=ALU.subtract), [src_key, "hi"], ["sq"])
                    S.op("pool", lambda e: e.tensor_copy(out=lo[:], in_=sq[:]), ["sq"], ["lo"])
                    S.op("pe", lambda e: e.matmul(rps[:], g.ones[:], hi[:], start=True, stop=False), ["ones", "hi"], ["rps"])
                    S.op("pe", lambda e: e.matmul(rps[:], g.ones[:], lo[:], start=False, stop=True), ["ones", "lo"], ["rps"])
                for hf, dst in ((0, oa), (1, ob)):
                    colsum(acc[:, hf], "acc%d" % hf)
                    S.op("dve", lambda e: e.reciprocal(out=ra[:], in_=rps[:]), ["rps"], ["ra"])
                    S.op("dve", lambda e, hf=hf, dst=dst: e.tensor_tensor(out=dst[:], in0=ops_[hf][:], in1=ra[:], op=ALU.mult),
                         ["ops%d" % hf, "ra"], ["o%d" % hf])
                S.op("dve", lambda e: e.scalar_tensor_tensor(out=oa[:], in0=ob[:], scalar=nlam[:, 0:1], in1=oa[:], op0=ALU.mult, op1=ALU.add),
                     ["o0", "o1", "nlam"], ["o0"])
                S.op("pool", lambda e: e.tensor_tensor(out=ob[:], in0=oa[:], in1=oa[:], op=ALU.mult), ["o0"], ["o1"])
                colsum(ob[:], "o1")
                S.op("act", lambda e: e.activation(out=ra[:], in_=rps[:], func=AF.Sqrt, bias=EPS, scale=1.0 / 128), ["rps"], ["ra"])
                S.op("dve", lambda e: e.reciprocal(out=ra[:], in_=ra[:]), ["ra"], ["ra"])
                yb_ = gq % 2
                S.op("dve", lambda e, yb_=yb_: e.scalar_tensor_tensor(out=yb[:, yb_], in0=oa[:], scalar=sg[:, 0:1], in1=ra[:], op0=ALU.mult, op1=ALU.mult),
                     ["o0", "sg", "ra"], ["yb%d" % yb_])
                S.dma("sp", Sc["ydT" + s][h][:, qs_], yb[:, yb_], reads=["yb%d" % yb_])
        S.flush()


def phase_merge(g, l, s, n, xin):
    nc, S, I, Sc = g.nc, g.S, g.I, g.Sc
    r = 0 if s == "L" else 1
    GW = min(512, n)
    ng = n // GW
    import contextlib
    with contextlib.ExitStack() as st:
        sb, ps = mk_alloc(g, st)
        wg = sb("wg", [128, 8, 3072], BF16)
        wb = sb("wb", [128, 3, 4, D], BF16)
        wo = sb("wo", [128, 8, D], BF16)
        g1 = sb("g1", [128, D])
        hTg = sb("hTg", [128, 2, 8, GW], BF16)
        ybr = sb("ybr", [128, 2, 3, 4, GW], BF16)
        sig = sb("sig", [128, 2, 3, GW])
        m0 = sb("m0", [128, GW])
        m1 = sb("m1", [128, GW])
        m2 = sb("m2", [128, GW])
        mT = sb("mT", [128, 8, GW], BF16)
        xt = sb("xt", [128, 2, D])
        tmp = sb("tmp", [128, D])
        gp = [ps("gp%d" % i, [128, GW]) for i in range(3)]
        bp = [ps("bp%d" % i, [128, GW]) for i in range(3)]
        ao = [ps("ao%d" % i, [128, 512]) for i in range(2)]
        for kh in range(4):
            for cb in range(3):
                S.dma("pool", wg[:, 2 * kh:2 * kh + 2, cb * 1024:(cb + 1) * 1024],
                      I["w_in"][l, kh * 256:(kh + 1) * 256, OFF_G + cb * 1024:OFF_G + (cb + 1) * 1024].rearrange("(kc p) n -> p kc n", p=128),
                      writes=["wg"])
        for bi, nm in enumerate(("w_br_pool", "w_br_swa", "w_br_diff")):
            S.dma("pool", wb[:, bi], I[nm][l].rearrange("(kc p) n -> p kc n", p=128), writes=["wb"])
        for kh in range(2):
            S.dma("pool", wo[:, 4 * kh:4 * kh + 4, :], I["w_out"][l, kh * 512:(kh + 1) * 512, :].rearrange("(kc p) n -> p kc n", p=128), writes=["wo"])
        load_bc(S, "sp", g1[:], Sc["mod"][l, r, 2 * D:3 * D], "g1")
        xcnt = 0
        for gi in range(ng):
            b = gi % 2
            cs_ = slice(gi * GW, (gi + 1) * GW)
            S.dma("sp", hTg[:, b], Sc["hT" + s][:, :, cs_].rearrange("k p t -> p k t"), writes=["hTg%d" % b])
            for bi, nm in enumerate(("ypT", "ysT", "ydT")):
                S.dma("sp", ybr[:, b, bi], Sc[nm + s][:, :, cs_].rearrange("c p t -> p c t"), writes=["ybr%d" % b])
            for j in range(8):
                sb_ = j % 2
                for bi in range(3):
                    for kc in range(8):
                        S.op("pe", lambda e, bi=bi, kc=kc, j=j, b=b: e.matmul(
                            gp[bi][:], wg[:, kc, bi * 1024 + j * 128:bi * 1024 + (j + 1) * 128], hTg[:, b, kc, :], start=(kc == 0), stop=(kc == 7)),
                            ["wg", "hTg%d" % b], ["gp%d" % bi])
                    for kc in range(4):
                        S.op("pe", lambda e, bi=bi, kc=kc, j=j, b=b: e.matmul(
                            bp[bi][:], wb[:, bi, kc, j * 128:(j + 1) * 128], ybr[:, b, bi, kc, :], start=(kc == 0), stop=(kc == 3)),
                            ["wb", "ybr%d" % b], ["bp%d" % bi])
                for bi in range(3):
                    S.op("act", lambda e, bi=bi, sb_=sb_: e.activation(out=sig[:, sb_, bi], in_=gp[bi][:], func=AF.Sigmoid),
                         ["gp%d" % bi], ["sig%d_%d" % (sb_, bi)])
                for bi, mm_ in enumerate((m0, m1, m2)):
                    S.op("dve", lambda e, bi=bi, mm_=mm_, sb_=sb_: e.tensor_tensor(out=mm_[:], in0=bp[bi][:], in1=sig[:, sb_, bi], op=ALU.mult),
                         ["bp%d" % bi, "sig%d_%d" % (sb_, bi)], ["m%d" % bi])
                S.op("pool", lambda e: e.tensor_tensor(out=m0[:], in0=m0[:], in1=m1[:], op=ALU.add), ["m0", "m1"], ["m0"])
                S.op("pool", lambda e, j=j: e.tensor_tensor(out=mT[:, j, :], in0=m0[:], in1=m2[:], op=ALU.add), ["m0", "m2"], ["mT"])
            for tt in range(GW // 128):
                xb = xcnt % 2
                xcnt += 1
                row0 = gi * GW + tt * 128
                S.dma("sp", xt[:, xb], xin[row0:row0 + 128, :], writes=["xt%d" % xb])
                for nn in range(2):
                    for kc in range(8):
                        S.op("pe", lambda e, nn=nn, kc=kc, tt=tt: e.matmul(
                            ao[nn][:], mT[:, kc, tt * 128:(tt + 1) * 128], wo[:, kc, nn * 512:(nn + 1) * 512], start=(kc == 0), stop=(kc == 7)),
                            ["mT", "wo"], ["ao%d" % nn])
                for nn in range(2):
                    S.op("dve", lambda e, nn=nn: e.tensor_tensor(out=tmp[:, nn * 512:(nn + 1) * 512], in0=ao[nn][:], in1=g1[:, nn * 512:(nn + 1) * 512], op=ALU.mult),
                         ["ao%d" % nn, "g1"], ["tmp"])
                S.op("pool", lambda e, xb=xb: e.tensor_tensor(out=xt[:, xb], in0=xt[:, xb], in1=tmp[:], op=ALU.add), ["tmp", "xt%d" % xb], ["xt%d" % xb])
                S.dma("sp", Sc["x1" + s][row0:row0 + 128, :], xt[:, xb], reads=["xt%d" % xb])
        S.flush()


def phase_ffn(g, l, s, n, xout):
    nc, S, I, Sc = g.nc, g.S, g.I, g.Sc
    r = 0 if s == "L" else 1
    GW = 256
    ng = n // GW
    import contextlib
    with contextlib.ExitStack() as st:
        sb, ps = mk_alloc(g, st)
        w1 = sb("w1", [128, 8, DFF], BF16)
        w2 = sb("w2", [128, 32, D], BF16)
        A2 = sb("A2", [128, D])
        B2 = sb("B2", [128, D])
        g2 = sb("g2", [128, D])
        tmp = sb("tmp", [128, D])
        xt = sb("xt", [128, 4, D])
        ssx = sb("ssx", [128, 4])
        hb = sb("hb", [128, 2, D], BF16)
        h2T = sb("h2T", [128, 2, 8, GW], BF16)
        aT = sb("aT", [128, 32, GW], BF16)
        r32 = sb("r32", [128, 2, GW])
        tr = ps("tr", [128, 8, 128], BF16)
        fp = [ps("fp%d" % i, [128, GW]) for i in range(2)]
        yp = [ps("yp%d" % i, [128, 512]) for i in range(4)]
        for kc in range(8):
            for cb in range(2):
                S.dma("pool", w1[:, kc, cb * 2048:(cb + 1) * 2048], I["w_ff1"][l, kc * 128:(kc + 1) * 128, cb * 2048:(cb + 1) * 2048], writes=["w1"])
        for fh in range(8):
            S.dma("pool", w2[:, 4 * fh:4 * fh + 4, :], I["w_ff2"][l, fh * 512:(fh + 1) * 512, :].rearrange("(kc p) n -> p kc n", p=128), writes=["w2"])
        load_bc(S, "sp", B2[:], Sc["mod"][l, r, 3 * D:4 * D], "B2")
        load_bc(S, "sp", A2[:], Sc["mod"][l, r, 4 * D:5 * D], "A2")
        load_bc(S, "sp", g2[:], Sc["mod"][l, r, 5 * D:6 * D], "g2")
        load_bc(S, "sp", tmp[:], I["norm2"][l], "tmp")
        S.op("dve", lambda e: e.scalar_tensor_tensor(out=A2[:], in0=A2[:], scalar=1.0, in1=tmp[:], op0=ALU.add, op1=ALU.mult), ["A2", "tmp"], ["A2"])
        tcnt = 0
        for gi in range(ng):
            gb = gi % 2
            for tt in range(2):
                xs = 2 * gb + tt
                hb_ = tcnt % 2
                tcnt += 1
                row0 = gi * GW + tt * 128
                S.dma("sp", xt[:, xs], Sc["x1" + s][row0:row0 + 128, :], writes=["xt%d" % xs])
                S.op("act", lambda e, xs=xs: e.activation(out=tmp[:], in_=xt[:, xs], func=AF.Square, accum_out=ssx[:, xs:xs + 1]),
                     ["xt%d" % xs], ["tmp", "ssx%d" % xs])
                S.op("act", lambda e, xs=xs: e.activation(out=ssx[:, xs:xs + 1], in_=ssx[:, xs:xs + 1], func=AF.Sqrt, bias=EPS, scale=1.0 / D),
                     ["ssx%d" % xs], ["ssx%d" % xs])
                S.op("dve", lambda e, xs=xs: e.reciprocal(out=ssx[:, xs:xs + 1], in_=ssx[:, xs:xs + 1]), ["ssx%d" % xs], ["ssx%d" % xs])
                S.op("dve", lambda e, xs=xs: e.scalar_tensor_tensor(out=tmp[:], in0=xt[:, xs], scalar=ssx[:, xs:xs + 1], in1=A2[:], op0=ALU.mult, op1=ALU.mult),
                     ["xt%d" % xs, "ssx%d" % xs, "A2"], ["tmp"])
                S.op("pool", lambda e, hb_=hb_: e.tensor_tensor(out=hb[:, hb_], in0=tmp[:], in1=B2[:], op=ALU.add), ["tmp", "B2"], ["hb%d" % hb_])
                for kc in range(8):
                    S.op("pe", lambda e, hb_=hb_, kc=kc: e.transpose(tr[:, kc, :], hb[:, hb_, kc * 128:(kc + 1) * 128], g.ident[:]),
                         ["hb%d" % hb_, "ident"], ["tr"])
                S.op("act", lambda e, gb=gb, tt=tt: e.copy(out=h2T[:, gb, :, tt * 128:(tt + 1) * 128], in_=tr[:]), ["tr"], ["h2T%d" % gb])
            for fc in range(32):
                fb = fc % 2
                for kc in range(8):
                    S.op("pe", lambda e, fb=fb, kc=kc, fc=fc, gb=gb: e.matmul(fp[fb][:], w1[:, kc, fc * 128:(fc + 1) * 128], h2T[:, gb, kc, :],
                                                                          start=(kc == 0), stop=(kc == 7)),
                         ["w1", "h2T%d" % gb], ["fp%d" % fb])
                S.op("act", lambda e, fb=fb: e.activation(out=r32[:, fb], in_=fp[fb][:], func=AF.Relu), ["fp%d" % fb], ["r32_%d" % fb])
                S.op("pool" if fc % 2 else "dve", lambda e, fb=fb, fc=fc: e.tensor_tensor(out=aT[:, fc, :], in0=r32[:, fb], in1=r32[:, fb], op=ALU.mult),
                     ["r32_%d" % fb], ["aT"])
            for tt in range(2):
                xs = 2 * gb + tt
                for nn in range(2):
                    yb_ = 2 * tt + nn
                    for fc in range(32):
                        S.op("pe", lambda e, yb_=yb_, fc=fc, tt=tt, nn=nn: e.matmul(yp[yb_][:], aT[:, fc, tt * 128:(tt + 1) * 128], w2[:, fc, nn * 512:(nn + 1) * 512],
                                                                               start=(fc == 0), stop=(fc == 31)),
                             ["aT", "w2"], ["yp%d" % yb_])
                for nn in range(2):
                    yb_ = 2 * tt + nn
                    S.op("dve", lambda e, yb_=yb_, nn=nn: e.tensor_tensor(out=tmp[:, nn * 512:(nn + 1) * 512], in0=yp[yb_][:], in1=g2[:, nn * 512:(nn + 1) * 512], op=ALU.mult),
                         ["yp%d" % yb_, "g2"], ["tmp"])
                row0 = gi * GW + tt * 128
                S.op("pool", lambda e, xs=xs: e.tensor_tensor(out=xt[:, xs], in0=xt[:, xs], in1=tmp[:], op=ALU.add), ["tmp", "xt%d" % xs], ["xt%d" % xs])
                S.dma("sp", xout[row0:row0 + 128, :], xt[:, xs], reads=["xt%d" % xs])
        S.flush()


def _band_mats(n):
    out = np.zeros((4, 5, 128, 128), np.float32)
    tl = np.arange(128)

    def mat(w, tile, src_tile):
        t = tile * 128 + tl
        lo = np.clip(t - w // 2, 0, n)
        hi = np.clip(t + w // 2, 0, n)
        cnt = (hi - lo).astype(np.float32)
        sg = src_tile * 128 + tl
        m = ((sg[:, None] >= lo[None, :]) & (sg[:, None] < hi[None, :])).astype(np.float32) / cnt[None, :]
        if tile == src_tile:
            m = m - np.eye(128, dtype=np.float32)
        return m
    nt = n // 128
    mid = 1 if nt > 2 else 0
    for wi, w in enumerate(POOL_WINDOWS):
        if nt > 2:
            out[wi, 0] = mat(w, mid, mid - 1)
            out[wi, 1] = mat(w, mid, mid)
            out[wi, 2] = mat(w, mid, mid + 1)
        else:
            out[wi, 0] = mat(w, 1, 0)
            out[wi, 2] = mat(w, 0, 1)
        out[wi, 3] = mat(w, 0, 0)
        out[wi, 4] = mat(w, nt - 1, nt - 1)
    return out.astype(ml_dtypes.bfloat16)


def make_consts(T):
    rows = T // 64
    row = np.repeat(np.arange(rows, dtype=np.float32), 64)
    col = np.tile(np.arange(64, dtype=np.float32), rows)
    inv = (10000.0 ** (-np.arange(16, dtype=np.float32) / 16)).astype(np.float32)
    ang = np.concatenate([row[:, None] * inv, col[:, None] * inv], axis=-1).astype(np.float32)
    j = np.arange(128)[:, None]
    a = np.arange(128)[None, :]
    return {
        "cos": np.cos(ang).astype(np.float32), "sin": np.sin(ang).astype(np.float32),
        "cosc": np.ones((CTX, 32), np.float32), "sinc": np.zeros((CTX, 32), np.float32),
        "ident": np.eye(128, dtype=np.float32).astype(ml_dtypes.bfloat16),
        "ones": np.ones((128, 128), np.float32).astype(ml_dtypes.bfloat16),
        "band": _band_mats(T), "bandc": _band_mats(CTX),
        "mask_prev": (j >= a).astype(np.float32).astype(ml_dtypes.bfloat16),
        "mask_next": (j <= a).astype(np.float32).astype(ml_dtypes.bfloat16),
    }


def core_inputs(inp, b, consts):
    f = lambda a: np.ascontiguousarray(np.asarray(a, dtype=np.float32))
    m = dict(consts)
    m["x"] = f(inp["x"][b])
    m["ctx"] = f(inp["ctx"][b])
    cv = np.stack([f(inp["c"][b]), f(inp["c_ctx"])], 0)
    m["cvec"] = np.ascontiguousarray(cv.reshape(2, 8, 128).transpose(2, 0, 1).reshape(128, 16))
    for k in ("w_ada", "b_ada", "norm1", "norm2", "w_in", "w_pool", "pool_scale", "swa_q_norm", "swa_k_norm",
              "swa_sink", "diff_q_norm", "diff_k_norm", "diff_subln", "w_br_pool", "w_br_swa", "w_br_diff",
              "w_out", "w_ff1", "w_ff2"):
        m[k] = f(inp[k])
    m["diff_lambda"] = f(inp["diff_lambda"]).reshape(DEPTH, 256)
    return m


_NC_CACHE = {}


def kernel(**inputs):
    B, T, _ = inputs["x"].shape
    if T not in _NC_CACHE:
        _NC_CACHE[T] = build_program(T)
    nc = _NC_CACHE[T]
    consts = make_consts(T)
    n_cores = 8
    in_maps = [core_inputs(inputs, c % B, consts) for c in range(n_cores)]
    res = run_bass_kernel_spmd(nc, in_maps, core_ids=list(range(n_cores)))
    return np.stack([np.asarray(res.results[b]["y"], dtype=np.float32) for b in range(B)], 0)
```

```python
import math
import numpy as np
import ml_dtypes
import concourse.bass as bass
import concourse.mybir as mybir
from concourse.bass_utils import run_bass_kernel_spmd

F32 = mybir.dt.float32
BF16 = mybir.dt.bfloat16
AF = mybir.ActivationFunctionType
ALU = mybir.AluOpType
AX = mybir.AxisListType

D = 1024
DEPTH = 2
CTX = 256
HD = 64
EPS = 1e-6
DFF = 4096
POOL_WINDOWS = (2, 4, 8, 16)
OFF_U, OFF_QS, OFF_KS, OFF_VS, OFF_QD, OFF_KD, OFF_VD, OFF_G = 0, 512, 1024, 1152, 1280, 1792, 2304, 2816
P1_BLOCKS = [(OFF_QS, 512), (OFF_KS, 128), (OFF_QD, 512), (OFF_KD, 512), (OFF_VS, 128), (OFF_VD, 512), (OFF_U, 512)]
NH = 26
C_VS, C_VD, C_U = 1664, 1792, 2304
P1_COLS = 2816


SKIP_SAME_ENGINE = True


class Sched:
    COMPUTE = ("pe", "act", "dve", "pool")

    def __init__(self, nc, stack):
        self.nc = nc
        self.eng = {"pe": nc.tensor, "act": nc.scalar, "dve": nc.vector, "pool": nc.gpsimd, "sp": nc.sync}
        self.tsem = {e: stack.enter_context(nc.semaphore("tl_" + e)) for e in self.COMPUTE}
        self.tick = {e: 0 for e in self.COMPUTE}
        self.nds = {"sp": 8, "pool": 6, "act": 2}
        self.dsem = {q: [stack.enter_context(nc.semaphore("d_%s%d" % (q, i))) for i in range(n)]
                     for q, n in self.nds.items()}
        self.dcnt = {q: 0 for q in self.nds}
        self.ccsem = stack.enter_context(nc.semaphore("ccsem"))
        self.cccnt = 0
        self.ops = []
        self.last_w = {}
        self.readers = {}
        self.waited = {e: {} for e in self.eng}
        self.sems = {}
        self.n_inst = 0
        self.skip_pe = True

    def op(self, eng, fn, reads=(), writes=(), kind="c"):
        idx = len(self.ops)
        deps = {}
        for k in reads:
            if k in self.last_w:
                deps[self.last_w[k]] = True
        for k in writes:
            if k in self.last_w:
                deps.setdefault(self.last_w[k], False)
            for rd in self.readers.get(k, ()):
                deps.setdefault(rd, False)
        if kind == "c" and SKIP_SAME_ENGINE and self.skip_pe:
            for d in list(deps):
                od = self.ops[d]
                if od["kind"] == "c" and od["eng"] == eng and eng == "pe":
                    del deps[d]
        for k in reads:
            self.readers.setdefault(k, []).append(idx)
        for k in writes:
            self.last_w[k] = idx
            self.readers[k] = []
        self.ops.append({"eng": eng, "fn": fn, "deps": deps, "kind": kind, "sig": kind != "c", "signal": None})
        return idx

    def dma(self, q, out, in_, reads=(), writes=()):
        return self.op(q, lambda e: e.dma_start(out=out, in_=in_), reads, writes, kind="d")

    def _wait(self, e, sem, val):
        w = self.waited[e]
        key = id(sem)
        if w.get(key, 0) < val:
            self.eng[e].wait_ge(sem, val)
            w[key] = val
            self.sems[key] = sem

    def flush(self):
        ops = self.ops
        for o in ops:
            for d in o["deps"]:
                ops[d]["sig"] = True
        last = {}
        for i, o in enumerate(ops):
            if o["kind"] == "c":
                last[o["eng"]] = i
        for i in last.values():
            ops[i]["sig"] = True
        final = {}
        for o in ops:
            e = o["eng"]
            for d in sorted(o["deps"]):
                sem, val = ops[d]["signal"]
                self._wait(e, sem, val)
            if o["kind"] == "d":
                j = self.dcnt[e]
                self.dcnt[e] += 1
                sem = self.dsem[e][j % self.nds[e]]
                val = 16 * (j // self.nds[e] + 1)
                self._wait(e, sem, val - 16)
                o["fn"](self.eng[e]).then_inc(sem, 16)
                o["signal"] = (sem, val)
                final[id(sem)] = (sem, val)
            elif o["kind"] == "cc":
                self.cccnt += 1
                o["fn"](self.eng[e]).then_inc(self.ccsem)
                o["signal"] = (self.ccsem, self.cccnt)
                final[id(self.ccsem)] = (self.ccsem, self.cccnt)
            else:
                ins = o["fn"](self.eng[e])
                if o["sig"]:
                    self.tick[e] += 1
                    ins.then_inc(self.tsem[e], 1)
                    o["signal"] = (self.tsem[e], self.tick[e])
                    final[id(self.tsem[e])] = (self.tsem[e], self.tick[e])
            self.n_inst += 1
        for e in self.eng:
            for sem, val in final.values():
                self._wait(e, sem, val)
        self.ops = []
        self.last_w = {}
        self.readers = {}


def dram_bcast(ap_1d, parts, mid=None):
    n = ap_1d.shape[-1]
    dims = [[0, parts]]
    if mid is not None:
        dims.append([0, mid])
    dims.append([1, n])
    return bass.AP(ap_1d.tensor, ap_1d.offset, dims)


class Ctx:
    pass


_UID = [0]


def mk_alloc(g, st):
    nc = g.nc
    _UID[0] += 1
    u = "_%d" % _UID[0]

    def sb(name, shape, dt=F32):
        return st.enter_context(nc.sbuf_tensor(name + u, list(shape), dt))

    def ps(name, shape, dt=F32):
        return st.enter_context(nc.psum_tensor(name + u, list(shape), dt))
    return sb, ps


def build_program(T, debug_outs=(), depth=DEPTH, phases=None):
    assert T % 512 == 0
    NT = T // 128
    NTC = CTX // 128
    TK = T + CTX
    NKT = TK // 128
    nc = bass.Bass("TRN2", target_bir_lowering=False)
    g = Ctx()
    g.nc = nc
    g.T, g.NT, g.NTC, g.TK, g.NKT = T, NT, NTC, TK, NKT

    def din(name, shape, dt=F32):
        return nc.dram_tensor(name, list(shape), dt, kind="ExternalInput").ap()

    def dscr(name, shape, dt=BF16):
        kind = "ExternalOutput" if name in debug_outs else "Internal"
        return nc.dram_tensor(name, list(shape), dt, kind=kind).ap()

    I = {}
    I["x"] = din("x", [T, D])
    I["ctx"] = din("ctx", [CTX, D])
    I["cvec"] = din("cvec", [128, 16])
    I["w_ada"] = din("w_ada", [DEPTH, D, 6 * D])
    I["b_ada"] = din("b_ada", [DEPTH, 6 * D])
    I["norm1"] = din("norm1", [DEPTH, D])
    I["norm2"] = din("norm2", [DEPTH, D])
    I["w_in"] = din("w_in", [DEPTH, D, 5888])
    I["w_pool"] = din("w_pool", [DEPTH, 4, 128, 128])
    I["pool_scale"] = din("pool_scale", [DEPTH, 512])
    I["swa_q_norm"] = din("swa_q_norm", [DEPTH, 64])
    I["swa_k_norm"] = din("swa_k_norm", [DEPTH, 64])
    I["swa_sink"] = din("swa_sink", [DEPTH, 8])
    I["diff_q_norm"] = din("diff_q_norm", [DEPTH, 64])
    I["diff_k_norm"] = din("diff_k_norm", [DEPTH, 64])
    I["diff_lambda"] = din("diff_lambda", [DEPTH, 256])
    I["diff_subln"] = din("diff_subln", [DEPTH, 128])
    I["w_br_pool"] = din("w_br_pool", [DEPTH, 512, D])
    I["w_br_swa"] = din("w_br_swa", [DEPTH, 512, D])
    I["w_br_diff"] = din("w_br_diff", [DEPTH, 512, D])
    I["w_out"] = din("w_out", [DEPTH, D, D])
    I["w_ff1"] = din("w_ff1", [DEPTH, D, DFF])
    I["w_ff2"] = din("w_ff2", [DEPTH, DFF, D])
    I["cos"] = din("cos", [T, 32])
    I["sin"] = din("sin", [T, 32])
    I["cosc"] = din("cosc", [CTX, 32])
    I["sinc"] = din("sinc", [CTX, 32])
    I["ident"] = din("ident", [128, 128], BF16)
    I["ones"] = din("ones", [128, 128], BF16)
    I["band"] = din("band", [4, 5, 128, 128], BF16)
    I["bandc"] = din("bandc", [4, 5, 128, 128], BF16)
    I["mask_prev"] = din("mask_prev", [128, 128], BF16)
    I["mask_next"] = din("mask_next", [128, 128], BF16)
    g.I = I
    y = nc.dram_tensor("y", [T, D], F32, kind="ExternalOutput").ap()
    g.y = y

    Sc = {}
    Sc["mod"] = dscr("mod", [DEPTH, 2, 6 * D], F32)
    for s, n in (("L", T), ("C", CTX)):
        Sc["hT" + s] = dscr("hT" + s, [8, 128, n])
        Sc["QsT" + s] = dscr("QsT" + s, [64, 8, n])
        Sc["QdT" + s] = dscr("QdT" + s, [4, 128, n])
        Sc["u" + s] = dscr("u" + s, [n, 512])
        Sc["ypT" + s] = dscr("ypT" + s, [4, 128, n])
        Sc["ysT" + s] = dscr("ysT" + s, [4, 128, n])
        Sc["ydT" + s] = dscr("ydT" + s, [4, 128, n])
        Sc["x1" + s] = dscr("x1" + s, [n, D], F32)
        Sc["xo" + s] = dscr("xo" + s, [n, D], F32)
    Sc["KsT"] = dscr("KsT", [64, 2, TK])
    Sc["KdT"] = dscr("KdT", [4, 128, TK])
    Sc["Vs"] = dscr("Vs", [TK, 128])
    Sc["Vd"] = dscr("Vd", [TK, 512])
    g.Sc = Sc

    import contextlib
    with contextlib.ExitStack() as stack:
        stack.enter_context(nc.allow_non_contiguous_dma(reason="small strided parameter / layout loads"))
        S = Sched(nc, stack)
        g.S = S
        ident = stack.enter_context(nc.sbuf_tensor("ident_sb", [128, 128], BF16))
        ones = stack.enter_context(nc.sbuf_tensor("ones_sb", [128, 128], BF16))
        S.dma("sp", ident[:], I["ident"], writes=["ident"])
        S.dma("sp", ones[:], I["ones"], writes=["ones"])
        g.ident, g.ones = ident, ones
        S.flush()

        for l in range(depth):
            last = l == DEPTH - 1
            phase_mod(g, l)
            xin_L = I["x"] if l == 0 else Sc["xoL"]
            xin_C = I["ctx"] if l == 0 else Sc["xoC"]
            phase_p1(g, l, "C", xin_C, NTC, I["cosc"], I["sinc"], T)
            phase_p1(g, l, "L", xin_L, NT, I["cos"], I["sin"], 0)
            streams = [("L", T, xin_L, (y if last else Sc["xoL"]))]
            if not last:
                streams.append(("C", CTX, xin_C, Sc["xoC"]))
            for s, n, xin, xout in streams:
                if phases is None or "pool" in phases:
                    phase_pool(g, l, s, n)
                if phases is None or "swa" in phases:
                    phase_swa(g, l, s, n)
                if phases is None or "diff" in phases:
                    phase_diff(g, l, s, n)
                if phases is None or "merge" in phases:
                    phase_merge(g, l, s, n, xin)
                if phases is None or "ffn" in phases:
                    phase_ffn(g, l, s, n, xout)
    return nc


def phase_mod(g, l):
    nc, S, I, Sc = g.nc, g.S, g.I, g.Sc
    import contextlib
    with contextlib.ExitStack() as st:
        sb, ps = mk_alloc(g, st)
        cv = sb("cv", [128, 16])
        cact = sb("cact", [128, 16])
        wada = sb("wada", [128, 2, 8, 512])
        bada = sb("bada", [2, 2, 512])
        modo = sb("modo", [2, 2, 512])
        modps = ps("modps", [2, 2, 512])
        S.dma("sp", cv[:], I["cvec"], writes=["cv"])
        S.op("act", lambda e: e.activation(out=cact[:], in_=cv[:], func=AF.Silu), ["cv"], ["cact"])
        for n in range(12):
            b = n % 2
            S.dma("sp", wada[:, b], I["w_ada"][l, :, n * 512:(n + 1) * 512].rearrange("(kc p) n -> p kc n", p=128),
                  writes=["wada%d" % b])
            S.dma("sp", bada[:, b], dram_bcast(I["b_ada"][l, n * 512:(n + 1) * 512], 2), writes=["bada%d" % b])
            for kc in range(8):
                lhsT = cact[:].rearrange("p (r k) -> p k r", r=2)[:, kc, :]
                S.op("pe", lambda e, lhsT=lhsT, kc=kc, b=b: e.matmul(modps[:, b], lhsT, wada[:, b, kc, :],
                                                                  start=(kc == 0), stop=(kc == 7)),
                     ["cact", "wada%d" % b], ["modps%d" % b])
            S.op("dve", lambda e, b=b: e.tensor_tensor(out=modo[:, b], in0=modps[:, b], in1=bada[:, b], op=ALU.add),
                 ["modps%d" % b, "bada%d" % b], ["modo%d" % b])
            S.dma("sp", Sc["mod"][l, :, n * 512:(n + 1) * 512], modo[:, b], reads=["modo%d" % b])
        S.flush()


def load_bc(S, q, tile_ap, vec_ap, key):
    S.dma(q, tile_ap, dram_bcast(vec_ap, 128), writes=[key])


def phase_p1(g, l, s, xin, ntiles, cos_d, sin_d, koff):
    nc, S, I, Sc = g.nc, g.S, g.I, g.Sc
    r = 0 if s == "L" else 1
    n = ntiles * 128
    import contextlib
    with contextlib.ExitStack() as st:
        sb, ps = mk_alloc(g, st)
        w = sb("p1w", [128, 8, P1_COLS], BF16)
        A1 = sb("A1", [128, D])
        B1 = sb("B1", [128, D])
        nrm = sb("nrm", [128, D])
        G = sb("G", [128, NH, 64])
        xt = sb("xt", [128, 2, D])
        junk = sb("junk", [128, D])
        ssx = sb("ssx", [128, 2])
        hb = sb("hb", [128, 2, D], BF16)
        hT = sb("hT", [128, 2, 8, 128], BF16)
        zs = sb("zs", [128, 2, 1664])
        sq = sb("sq", [128, 1664])
        ssh = sb("ssh", [128, 2, NH])
        qn = sb("qn", [128, NH, 64])
        t1 = sb("t1", [128, NH, 32])
        t2 = sb("t2", [128, NH, 32])
        t3 = sb("t3", [128, NH, 32])
        t4 = sb("t4", [128, NH, 32])
        qo = sb("qo", [128, 2, NH, 64], BF16)
        vu = sb("vu", [128, 2, 1152], BF16)
        cs = sb("cs", [128, 2, 2, 32])
        stQs = sb("stQs", [64, 2, 8, 512], BF16)
        stKs = sb("stKs", [64, 2, 2, 512], BF16)
        stQd = sb("stQd", [128, 2, 4, 512], BF16)
        stKd = sb("stKd", [128, 2, 4, 512], BF16)
        zp = [ps("zp%d" % i, [128, 512]) for i in range(6)]
        tr1 = ps("tr1", [128, 8, 128], BF16)
        tr2 = ps("tr2", [128, 8, 128], BF16)

        c0 = 0
        for (off, wd) in P1_BLOCKS:
            for kh in range(2):
                S.dma("pool", w[:, kh * 4:(kh + 1) * 4, c0:c0 + wd],
                      I["w_in"][l, kh * 512:(kh + 1) * 512, off:off + wd].rearrange("(kc p) n -> p kc n", p=128),
                      writes=["w"])
            c0 += wd
        load_bc(S, "sp", B1[:], Sc["mod"][l, r, 0:D], "B1")
        load_bc(S, "sp", A1[:], Sc["mod"][l, r, D:2 * D], "A1")
        load_bc(S, "sp", nrm[:], I["norm1"][l], "nrm")
        S.op("dve", lambda e: e.scalar_tensor_tensor(out=A1[:], in0=A1[:], scalar=1.0, in1=nrm[:],
                                                     op0=ALU.add, op1=ALU.mult), ["A1", "nrm"], ["A1"])
        for (h0, hn, nm) in ((0, 8, "swa_q_norm"), (8, 2, "swa_k_norm"), (10, 8, "diff_q_norm"), (18, 8, "diff_k_norm")):
            S.dma("sp", G[:, h0:h0 + hn, :], dram_bcast(I[nm][l], 128, hn), writes=["G"])

        for t in range(ntiles):
            b = t % 2
            gi = t // 4
            gb = gi % 2
            tq = t % 4
            S.dma("sp", xt[:, b], xin[t * 128:(t + 1) * 128, :], writes=["xt%d" % b])
            S.dma("sp", cs[:, b, 0], cos_d[t * 128:(t + 1) * 128, :], writes=["cs%d" % b])
            S.dma("sp", cs[:, b, 1], sin_d[t * 128:(t + 1) * 128, :], writes=["cs%d" % b])
            S.op("act", lambda e, b=b: e.activation(out=junk[:], in_=xt[:, b], func=AF.Square, accum_out=ssx[:, b:b + 1]),
                 ["xt%d" % b], ["junk", "ssx%d" % b])
            S.op("act", lambda e, b=b: e.activation(out=ssx[:, b:b + 1], in_=ssx[:, b:b + 1], func=AF.Sqrt, bias=EPS, scale=1.0 / D),
                 ["ssx%d" % b], ["ssx%d" % b])
            S.op("dve", lambda e, b=b: e.reciprocal(out=ssx[:, b:b + 1], in_=ssx[:, b:b + 1]), ["ssx%d" % b], ["ssx%d" % b])
            S.op("dve", lambda e, b=b: e.scalar_tensor_tensor(out=xt[:, b], in0=xt[:, b], scalar=ssx[:, b:b + 1], in1=A1[:],
                                                              op0=ALU.mult, op1=ALU.mult), ["xt%d" % b, "ssx%d" % b, "A1"], ["xt%d" % b])
            S.op("pool", lambda e, b=b: e.tensor_tensor(out=hb[:, b], in0=xt[:, b], in1=B1[:], op=ALU.add),
                 ["xt%d" % b, "B1"], ["hb%d" % b])
            for kc in range(8):
                S.op("pe", lambda e, b=b, kc=kc: e.transpose(tr1[:, kc, :], hb[:, b, kc * 128:(kc + 1) * 128], g.ident[:]),
                     ["hb%d" % b, "ident"], ["tr1"])
            S.op("act", lambda e, b=b: e.copy(out=hT[:, b], in_=tr1[:]), ["tr1"], ["hT%d" % b])
            S.dma("sp", Sc["hT" + s][:, :, t * 128:(t + 1) * 128].rearrange("k p t -> p k t"), hT[:, b], reads=["hT%d" % b])
            for ci in range(6):
                cw = min(512, P1_COLS - ci * 512)
                for kc in range(8):
                    S.op("pe", lambda e, b=b, kc=kc, ci=ci, cw=cw: e.matmul(zp[ci][:, 0:cw], hT[:, b, kc, :], w[:, kc, ci * 512:ci * 512 + cw],
                                                                       start=(kc == 0), stop=(kc == 7)),
                         ["hT%d" % b, "w"], ["zp%d" % ci])
            for ci in range(6):
                lo, hi = ci * 512, min(P1_COLS, ci * 512 + 512)
                if hi <= 1664 or lo < 1664:
                    h2 = min(hi, 1664)
                    S.op("act", lambda e, b=b, ci=ci, lo=lo, h2=h2: e.copy(out=zs[:, b, lo:h2], in_=zp[ci][:, 0:h2 - lo]),
                         ["zp%d" % ci], ["zs%d" % b])
                    S.op("act", lambda e, ci=ci, lo=lo, h2=h2: e.activation(out=sq[:, lo:h2], in_=zp[ci][:, 0:h2 - lo], func=AF.Square),
                         ["zp%d" % ci], ["sq"])
                if hi > 1664:
                    l2 = max(lo, 1664)
                    S.op("act", lambda e, b=b, ci=ci, lo=lo, l2=l2, hi=hi: e.copy(out=vu[:, b, l2 - 1664:hi - 1664], in_=zp[ci][:, l2 - lo:hi - lo]),
                         ["zp%d" % ci], ["vu%d" % b])
            S.dma("sp", Sc["Vs"][koff + t * 128:koff + (t + 1) * 128, :], vu[:, b, 0:128], reads=["vu%d" % b])
            S.dma("sp", Sc["Vd"][koff + t * 128:koff + (t + 1) * 128, :], vu[:, b, 128:640], reads=["vu%d" % b])
            S.dma("sp", Sc["u" + s][t * 128:(t + 1) * 128, :], vu[:, b, 640:1152], reads=["vu%d" % b])
            S.op("dve", lambda e, b=b: e.tensor_reduce(out=ssh[:, b], in_=sq[:].rearrange("p (h d) -> p h d", d=64), axis=AX.X, op=ALU.add),
                 ["sq"], ["ssh%d" % b])
            S.op("act", lambda e, b=b: e.activation(out=ssh[:, b], in_=ssh[:, b], func=AF.Sqrt, bias=EPS, scale=1.0 / 64),
                 ["ssh%d" % b], ["ssh%d" % b])
            S.op("dve", lambda e, b=b: e.reciprocal(out=ssh[:, b], in_=ssh[:, b]), ["ssh%d" % b], ["ssh%d" % b])
            zh = zs[:, b].rearrange("p (h d) -> p h d", d=64)
            S.op("dve", lambda e, b=b, zh=zh: e.tensor_tensor(out=qn[:], in0=zh, in1=ssh[:, b].unsqueeze(2).to_broadcast([128, NH, 64]), op=ALU.mult),
                 ["zs%d" % b, "ssh%d" % b], ["qn"])
            S.op("pool", lambda e: e.tensor_tensor(out=qn[:], in0=qn[:], in1=G[:], op=ALU.mult), ["qn", "G"], ["qn"])
            cb = cs[:, b, 0].unsqueeze(1).to_broadcast([128, NH, 32])
            sb_ = cs[:, b, 1].unsqueeze(1).to_broadcast([128, NH, 32])
            x1 = qn[:, :, 0:32]
            x2 = qn[:, :, 32:64]
            S.op("dve", lambda e, cb=cb, x1=x1: e.tensor_tensor(out=t1[:], in0=x1, in1=cb, op=ALU.mult), ["qn", "cs%d" % b], ["t1"])
            S.op("dve", lambda e, sb_=sb_, x2=x2: e.tensor_tensor(out=t2[:], in0=x2, in1=sb_, op=ALU.mult), ["qn", "cs%d" % b], ["t2"])
            S.op("dve", lambda e, b=b: e.tensor_tensor(out=qo[:, b, :, 0:32], in0=t1[:], in1=t2[:], op=ALU.subtract), ["t1", "t2"], ["qo%d" % b])
            S.op("pool", lambda e, cb=cb, x2=x2: e.tensor_tensor(out=t3[:], in0=x2, in1=cb, op=ALU.mult), ["qn", "cs%d" % b], ["t3"])
            S.op("pool", lambda e, sb_=sb_, x1=x1: e.tensor_tensor(out=t4[:], in0=x1, in1=sb_, op=ALU.mult), ["qn", "cs%d" % b], ["t4"])
            S.op("pool", lambda e, b=b: e.tensor_tensor(out=qo[:, b, :, 32:64], in0=t3[:], in1=t4[:], op=ALU.add), ["t3", "t4"], ["qo%d" % b])
            for h in range(8):
                S.op("pe", lambda e, b=b, h=h: e.transpose(tr2[0:64, h, :], qo[:, b, h, :], g.ident[:]), ["qo%d" % b, "ident"], ["tr2"])
            S.op("act", lambda e, gb=gb, tq=tq: e.copy(out=stQs[:, gb, :, tq * 128:(tq + 1) * 128], in_=tr2[0:64]), ["tr2"], ["stQs%d" % gb])
            for h in range(2):
                S.op("pe", lambda e, b=b, h=h: e.transpose(tr2[0:64, h, :], qo[:, b, 8 + h, :], g.ident[:]), ["qo%d" % b, "ident"], ["tr2"])
            S.op("act", lambda e, gb=gb, tq=tq: e.copy(out=stKs[:, gb, :, tq * 128:(tq + 1) * 128], in_=tr2[0:64, 0:2, :]), ["tr2"], ["stKs%d" % gb])
            for j in range(8):
                src = qo[:, b, 10 + 2 * j:12 + 2 * j, :].rearrange("p h d -> p (h d)")
                S.op("pe", lambda e, j=j, src=src: e.transpose(tr1[:, j, :], src, g.ident[:]), ["qo%d" % b, "ident"], ["tr1"])
            S.op("act", lambda e, gb=gb, tq=tq: e.copy(out=stQd[:, gb, :, tq * 128:(tq + 1) * 128], in_=tr1[:, 0:4, :]), ["tr1"], ["stQd%d" % gb])
            S.op("act", lambda e, gb=gb, tq=tq: e.copy(out=stKd[:, gb, :, tq * 128:(tq + 1) * 128], in_=tr1[:, 4:8, :]), ["tr1"], ["stKd%d" % gb])
            if tq == 3 or t == ntiles - 1:
                nt = (tq + 1) * 128
                t0 = gi * 512
                S.dma("sp", Sc["QsT" + s][:, :, t0:t0 + nt], stQs[:, gb, :, 0:nt], reads=["stQs%d" % gb])
                S.dma("sp", Sc["KsT"][:, :, koff + t0:koff + t0 + nt], stKs[:, gb, :, 0:nt], reads=["stKs%d" % gb])
                S.dma("sp", Sc["QdT" + s][:, :, t0:t0 + nt].rearrange("h p t -> p h t"), stQd[:, gb, :, 0:nt], reads=["stQd%d" % gb])
                S.dma("sp", Sc["KdT"][:, :, koff + t0:koff + t0 + nt].rearrange("h p t -> p h t"), stKd[:, gb, :, 0:nt], reads=["stKd%d" % gb])
        S.flush()


def phase_pool(g, l, s, n):
    nc, S, I, Sc = g.nc, g.S, g.I, g.Sc
    nt = n // 128
    band_d = I["band"] if s == "L" else I["bandc"]
    import contextlib
    with contextlib.ExitStack() as st:
        sb, ps = mk_alloc(g, st)
        band = sb("band", [128, 4, 5, 128], BF16)
        wp = sb("wp", [128, 4, 128], BF16)
        psc = sb("psc", [128, 4])
        ut = sb("ut", [128, 3, 512], BF16)
        pl = sb("pl", [128, 2, 128], BF16)
        yo = sb("yo", [128, 2, 4, 128], BF16)
        pp = [ps("pp%d" % i, [128, 128]) for i in range(2)]
        yp = [ps("yp%d" % i, [128, 128]) for i in range(2)]
        S.dma("sp", band[:], band_d.rearrange("w k s t -> s w k t"), writes=["band"])
        S.dma("pool", wp[:], I["w_pool"][l].rearrange("g c d -> c g d"), writes=["wp"])
        S.dma("sp", psc[:], I["pool_scale"][l].rearrange("(g d) -> d g", g=4), writes=["psc"])
        S.dma("sp", ut[:, 0], Sc["u" + s][0:128, :], writes=["ut0"])
        for t in range(nt):
            b = t % 2
            if t + 1 < nt:
                S.dma("sp", ut[:, (t + 1) % 3], Sc["u" + s][(t + 1) * 128:(t + 2) * 128, :], writes=["ut%d" % ((t + 1) % 3)])
            srcs = []
            if t > 0:
                srcs.append(((t - 1) % 3, 0))
            srcs.append((t % 3, 3 if t == 0 else (4 if t == nt - 1 else 1)))
            if t + 1 < nt:
                srcs.append(((t + 1) % 3, 2))
            for gq in range(4):
                pb = gq % 2
                for i, (slot, kind) in enumerate(srcs):
                    S.op("pe", lambda e, pb=pb, slot=slot, kind=kind, gq=gq, i=i, ns=len(srcs): e.matmul(
                        pp[pb][:], ut[:, slot, gq * 128:(gq + 1) * 128], band[:, gq, kind, :], start=(i == 0), stop=(i == ns - 1)),
                        ["ut%d" % slot, "band"], ["pp%d" % pb])
                S.op("act", lambda e, pb=pb: e.copy(out=pl[:, pb], in_=pp[pb][:]), ["pp%d" % pb], ["pl%d" % pb])
                S.op("pe", lambda e, pb=pb, gq=gq: e.matmul(yp[pb][:], wp[:, gq, :], pl[:, pb], start=True, stop=True),
                     ["pl%d" % pb, "wp"], ["yp%d" % pb])
                S.op("dve", lambda e, pb=pb, gq=gq, b=b: e.tensor_scalar(out=yo[:, b, gq, :], in0=yp[pb][:], scalar1=psc[:, gq:gq + 1],
                                                                         scalar2=None, op0=ALU.mult),
                     ["yp%d" % pb, "psc"], ["yo%d" % b])
            S.dma("sp", Sc["ypT" + s][:, :, t * 128:(t + 1) * 128].rearrange("g p t -> p g t"), yo[:, b], reads=["yo%d" % b])
        S.flush()


def phase_swa(g, l, s, n):
    nc, S, I, Sc = g.nc, g.S, g.I, g.Sc
    T, TK = g.T, g.TK
    nt = n // 128
    nkt_all = TK // 128
    ctx_kts = [T // 128, T // 128 + 1]
    import contextlib
    with contextlib.ExitStack() as st:
        sb, ps = mk_alloc(g, st)
        kst = sb("kst", [64, 2, TK], BF16)
        vs = sb("vs", [128, nkt_all, 128], BF16)
        qb_ = sb("qb", [64, 2, 8, 128], BF16)
        mk = sb("mk", [128, 2, 128], BF16)
        esk = sb("esk", [64, 8])
        pt = sb("pt", [128, 3, 512], BF16)
        den = sb("den", [64, 512])
        yo = sb("yo", [64, 2, 4, 128], BF16)
        spp = [ps("spp%d" % i, [128, 512]) for i in range(2)]
        opp = [ps("opp%d" % i, [64, 512]) for i in range(2)]
        rpp = [ps("rpp%d" % i, [64, 512]) for i in range(2)]
        if s == "L":
            S.dma("sp", kst[:], Sc["KsT"], writes=["kst"])
            S.dma("sp", vs[:], Sc["Vs"].rearrange("(kt p) d -> p kt d", p=128), writes=["vs"])
        else:
            S.dma("sp", kst[:, :, T:TK], Sc["KsT"][:, :, T:TK], writes=["kst"])
            S.dma("sp", vs[:, T // 128:, :], Sc["Vs"][T:TK, :].rearrange("(kt p) d -> p kt d", p=128), writes=["vs"])
        S.dma("sp", mk[:, 0], I["mask_prev"], writes=["mk"])
        S.dma("sp", mk[:, 1], I["mask_next"], writes=["mk"])
        S.dma("sp", esk[:], dram_bcast(I["swa_sink"][l], 64), writes=["esk"])
        S.op("act", lambda e: e.activation(out=esk[:], in_=esk[:], func=AF.Exp), ["esk"], ["esk"])
        pcnt = 0
        for qi in range(nt):
            b = qi % 2
            S.dma("sp", qb_[:, b], Sc["QsT" + s][:, :, qi * 128:(qi + 1) * 128], writes=["qb%d" % b])
            kts = []
            if s == "L":
                if qi > 0:
                    kts.append((qi - 1, 0))
                kts.append((qi, None))
                if qi + 1 < nt:
                    kts.append((qi + 1, 1))
            kts += [(k, None) for k in ctx_kts]
            for gk in range(2):
                ob = gk
                for i, (kt, m) in enumerate(kts):
                    sl = pcnt % 3
                    sp_ = pcnt % 2
                    pcnt += 1
                    S.op("pe", lambda e, sp_=sp_, gk=gk, kt=kt, b=b: e.matmul(
                        spp[sp_][:], kst[:, gk, kt * 128:(kt + 1) * 128], qb_[:, b, 4 * gk:4 * gk + 4, :], start=True, stop=True),
                        ["kst", "qb%d" % b], ["spp%d" % sp_])
                    S.op("act", lambda e, sp_=sp_, sl=sl: e.activation(out=pt[:, sl], in_=spp[sp_][:], func=AF.Exp, scale=0.125),
                         ["spp%d" % sp_], ["pt%d" % sl])
                    if m is not None:
                        pv = pt[:, sl].rearrange("p (h q) -> p h q", h=4)
                        S.op("dve", lambda e, pv=pv, m=m: e.tensor_tensor(out=pv, in0=pv, in1=mk[:, m].unsqueeze(1).to_broadcast([128, 4, 128]),
                                                                          op=ALU.mult), ["pt%d" % sl, "mk"], ["pt%d" % sl])
                    S.op("pe", lambda e, ob=ob, kt=kt, gk=gk, sl=sl, i=i, nk=len(kts): e.matmul(
                        opp[ob][:], vs[:, kt, gk * 64:(gk + 1) * 64], pt[:, sl], start=(i == 0), stop=(i == nk - 1)),
                        ["vs", "pt%d" % sl], ["opp%d" % ob])
                    S.op("pe", lambda e, ob=ob, sl=sl, i=i, nk=len(kts): e.matmul(
                        rpp[ob][:], g.ones[:, 0:64], pt[:, sl], start=(i == 0), stop=(i == nk - 1)),
                        ["ones", "pt%d" % sl], ["rpp%d" % ob])
                dv = den[:].rearrange("p (h q) -> p h q", h=4)
                S.op("dve", lambda e, ob=ob, gk=gk, dv=dv: e.tensor_tensor(
                    out=dv, in0=rpp[ob][:].rearrange("p (h q) -> p h q", h=4),
                    in1=esk[:, 4 * gk:4 * gk + 4].unsqueeze(2).to_broadcast([64, 4, 128]), op=ALU.add),
                    ["rpp%d" % ob, "esk"], ["den"])
                S.op("dve", lambda e: e.reciprocal(out=den[:], in_=den[:]), ["den"], ["den"])
                S.op("dve", lambda e, ob=ob, gk=gk: e.tensor_tensor(out=yo[:, gk].rearrange("p h q -> p (h q)"), in0=opp[ob][:], in1=den[:], op=ALU.mult),
                     ["opp%d" % ob, "den"], ["yo%d" % gk])
                for jj in range(2):
                    j = 2 * gk + jj
                    dst = Sc["ysT" + s][j].rearrange("(hh d) t -> d hh t", hh=2)[:, :, qi * 128:(qi + 1) * 128]
                    S.dma("sp", dst, yo[:, gk, 2 * jj:2 * jj + 2, :], reads=["yo%d" % gk])
        S.flush()


def phase_diff(g, l, s, n):
    nc, S, I, Sc = g.nc, g.S, g.I, g.Sc
    T, TK = g.T, g.TK
    lam_init = 0.8 - 0.6 * math.exp(-0.3 * l)
    k0 = 0 if s == "L" else T
    nk = TK - k0
    nkt = nk // 128
    GW = min(512, n)
    ng = n // GW
    S.skip_pe = False
    import contextlib
    with contextlib.ExitStack() as st:
        sb, ps = mk_alloc(g, st)
        kt_ = sb("dk", [128, 2, nk], BF16)
        vt = sb("dv", [128, 2, nkt, 128], BF16)
        qt = sb("dq", [128, 2, n], BF16)
        dl = sb("dl", [128, 256])
        pr = sb("pr", [128, 2, 64])
        ee = sb("ee", [128, 2])
        nlam = sb("nlam", [128, 1])
        sg = sb("sg", [128, 1])
        hi = sb("hi", [128, GW], BF16)
        lo = sb("lo", [128, GW], BF16)
        pt = sb("pt", [128, 3, 2, GW], BF16)
        acc = sb("acc", [128, 2, GW])
        ra = sb("ra", [128, GW])
        oa = sb("oa", [128, GW])
        ob = sb("ob", [128, GW])
        sq = sb("sq", [128, GW])
        yb = sb("yb", [128, 2, GW], BF16)
        sps = [ps("sps%d" % i, [128, 2, GW]) for i in range(2)]
        ops_ = [ps("ops%d" % i, [128, GW]) for i in range(2)]
        rps = ps("rps", [128, GW])
        S.dma("sp", dl[:], dram_bcast(I["diff_lambda"][l], 128), writes=["dl"])
        dl4 = dl[:].rearrange("p (a b d) -> p a b d", a=2, b=2)
        S.op("dve", lambda e: e.tensor_tensor(out=pr[:], in0=dl4[:, :, 0, :], in1=dl4[:, :, 1, :], op=ALU.mult), ["dl"], ["pr"])
        S.op("dve", lambda e: e.tensor_reduce(out=ee[:], in_=pr[:], axis=AX.X, op=ALU.add), ["pr"], ["ee"])
        S.op("act", lambda e: e.activation(out=ee[:], in_=ee[:], func=AF.Exp), ["ee"], ["ee"])
        S.op("dve", lambda e: e.tensor_tensor(out=nlam[:], in0=ee[:, 1:2], in1=ee[:, 0:1], op=ALU.subtract), ["ee"], ["nlam"])
        S.op("dve", lambda e: e.tensor_scalar(out=nlam[:], in0=nlam[:], scalar1=-lam_init, scalar2=None, op0=ALU.add), ["nlam"], ["nlam"])
        S.dma("sp", sg[:], I["diff_subln"][l].rearrange("(p o) -> p o", o=1), writes=["sg"])
        S.op("dve", lambda e: e.tensor_scalar(out=sg[:], in0=sg[:], scalar1=(1.0 - lam_init), scalar2=None, op0=ALU.mult), ["sg"], ["sg"])

        def load_head(h):
            hb = h % 2
            S.dma("sp", kt_[:, hb], Sc["KdT"][h][:, k0:TK], writes=["dk%d" % hb])
            S.dma("sp", vt[:, hb], Sc["Vd"][k0:TK, h * 128:(h + 1) * 128].rearrange("(kt p) d -> p kt d", p=128), writes=["dv%d" % hb])
            S.dma("sp", qt[:, hb], Sc["QdT" + s][h], writes=["dq%d" % hb])
        load_head(0)
        cnt = 0
        for h in range(4):
            hb = h % 2
            if h + 1 < 4:
                load_head(h + 1)
            for gq in range(ng):
                qs_ = slice(gq * GW, (gq + 1) * GW)

                def scores(kt, c, hb=hb, qs_=qs_):
                    pb = c % 2
                    for hf in range(2):
                        S.op("pe", lambda e, pb=pb, hf=hf, kt=kt, hb=hb, qs_=qs_: e.matmul(
                            sps[pb][:, hf, :], kt_[hf * 64:(hf + 1) * 64, hb, kt * 128:(kt + 1) * 128], qt[hf * 64:(hf + 1) * 64, hb, qs_],
                            start=True, stop=True), ["dk%d" % hb, "dq%d" % hb], ["sps%d" % pb])
                scores(0, cnt)
                for kt in range(nkt):
                    c = cnt + kt
                    if kt + 1 < nkt:
                        scores(kt + 1, c + 1)
                    pb = c % 2
                    sl = c % 3
                    for hf in range(2):
                        S.op("act", lambda e, pb=pb, sl=sl, hf=hf: e.activation(out=pt[:, sl, hf, :], in_=sps[pb][:, hf, :], func=AF.Exp, scale=0.125),
                             ["sps%d" % pb], ["pt%d" % sl])
                    for hf in range(2):
                        S.op("pe", lambda e, hf=hf, sl=sl, kt=kt, hb=hb: e.matmul(ops_[hf][:], vt[:, hb, kt, :], pt[:, sl, hf, :],
                                                                               start=(kt == 0), stop=(kt == nkt - 1)),
                             ["dv%d" % hb, "pt%d" % sl], ["ops%d" % hf])
                    for hf, en in ((0, "dve"), (1, "pool")):
                        if kt == 0:
                            S.op(en, lambda e, hf=hf, sl=sl: e.tensor_copy(out=acc[:, hf], in_=pt[:, sl, hf, :]), ["pt%d" % sl], ["acc%d" % hf])
                        else:
                            S.op(en, lambda e, hf=hf, sl=sl: e.tensor_tensor(out=acc[:, hf], in0=acc[:, hf], in1=pt[:, sl, hf, :], op=ALU.add),
                                 ["pt%d" % sl, "acc%d" % hf], ["acc%d" % hf])
                cnt += nkt
                def colsum(src_ap, src_key):
                    S.op("dve", lambda e: e.tensor_copy(out=hi[:], in_=src_ap), [src_key], ["hi"])
                    S.op("pool", lambda e: e.tensor_tensor(out=sq[:], in0=src_ap, in1=hi[:], op=ALU.subtract), [src_key, "hi"], ["sq"])
                    S.op("pool", lambda e: e.tensor_copy(out=lo[:], in_=sq[:]), ["sq"], ["lo"])
                    S.op("pe", lambda e: e.matmul(rps[:], g.ones[:], hi[:], start=True, stop=False), ["ones", "hi"], ["rps"])
                    S.op("pe", lambda e: e.matmul(rps[:], g.ones[:], lo[:], start=False, stop=True), ["ones", "lo"], ["rps"])
                for hf, dst in ((0, oa), (1, ob)):
                    colsum(acc[:, hf], "acc%d" % hf)
                    S.op("dve", lambda e: e.reciprocal(out=ra[:], in_=rps[:]), ["rps"], ["ra"])
                    S.op("dve", lambda e, hf=hf, dst=dst: e.tensor_tensor(out=dst[:], in0=ops_[hf][:], in1=ra[:], op=ALU.mult),
                         ["ops%d" % hf, "ra"], ["o%d" % hf])
                S.op("dve", lambda e: e.scalar_tensor_tensor(out=oa[:], in0=ob[:], scalar=nlam[:, 0:1], in1=oa[:], op0=ALU.mult, op1=ALU.add),
                     ["o0", "o1", "nlam"], ["o0"])
                S.op("pool", lambda e: e.tensor_tensor(out=ob[:], in0=oa[:], in1=oa[:], op=ALU.mult), ["o0"], ["o1"])
                colsum(ob[:], "o1")
                S.op("act", lambda e: e.activation(out=ra[:], in_=rps[:], func=AF.Sqrt, bias=EPS, scale=1.0 / 128), ["rps"], ["ra"])
                S.op("dve", lambda e: e.reciprocal(out=ra[:], in_=ra[:]), ["ra"], ["ra"])
                yb_ = gq % 2
                S.op("dve", lambda e, yb_=yb_: e.scalar_tensor_tensor(out=yb[:, yb_], in0=oa[:], scalar=sg[:, 0:1], in1=ra[:], op0=ALU.mult, op1=ALU.mult),
                     ["o0", "sg", "ra"], ["yb%d" % yb_])
                S.dma("sp", Sc["ydT" + s][h][:, qs_], yb[:, yb_], reads=["yb%d" % yb_])
        S.flush()
    S.skip_pe = True


def phase_merge(g, l, s, n, xin):
    nc, S, I, Sc = g.nc, g.S, g.I, g.Sc
    r = 0 if s == "L" else 1
    GW = min(512, n)
    ng = n // GW
    import contextlib
    with contextlib.ExitStack() as st:
        sb, ps = mk_alloc(g, st)
        wg = sb("wg", [128, 8, 3072], BF16)
        wb = sb("wb", [128, 3, 4, D], BF16)
        wo = sb("wo", [128, 8, D], BF16)
        g1 = sb("g1", [128, D])
        hTg = sb("hTg", [128, 2, 8, GW], BF16)
        ybr = sb("ybr", [128, 2, 3, 4, GW], BF16)
        sig = sb("sig", [128, 2, 3, GW])
        m0 = sb("m0", [128, GW])
        m1 = sb("m1", [128, GW])
        m2 = sb("m2", [128, GW])
        mT = sb("mT", [128, 8, GW], BF16)
        xt = sb("xt", [128, 2, D])
        tmp = sb("tmp", [128, D])
        gp = [ps("gp%d" % i, [128, GW]) for i in range(3)]
        bp = [ps("bp%d" % i, [128, GW]) for i in range(3)]
        ao = [ps("ao%d" % i, [128, 512]) for i in range(2)]
        for kh in range(4):
            for cb in range(3):
                S.dma("pool", wg[:, 2 * kh:2 * kh + 2, cb * 1024:(cb + 1) * 1024],
                      I["w_in"][l, kh * 256:(kh + 1) * 256, OFF_G + cb * 1024:OFF_G + (cb + 1) * 1024].rearrange("(kc p) n -> p kc n", p=128),
                      writes=["wg"])
        for bi, nm in enumerate(("w_br_pool", "w_br_swa", "w_br_diff")):
            S.dma("pool", wb[:, bi], I[nm][l].rearrange("(kc p) n -> p kc n", p=128), writes=["wb"])
        for kh in range(2):
            S.dma("pool", wo[:, 4 * kh:4 * kh + 4, :], I["w_out"][l, kh * 512:(kh + 1) * 512, :].rearrange("(kc p) n -> p kc n", p=128), writes=["wo"])
        load_bc(S, "sp", g1[:], Sc["mod"][l, r, 2 * D:3 * D], "g1")
        xcnt = 0
        for gi in range(ng):
            b = gi % 2
            cs_ = slice(gi * GW, (gi + 1) * GW)
            S.dma("sp", hTg[:, b], Sc["hT" + s][:, :, cs_].rearrange("k p t -> p k t"), writes=["hTg%d" % b])
            for bi, nm in enumerate(("ypT", "ysT", "ydT")):
                S.dma("sp", ybr[:, b, bi], Sc[nm + s][:, :, cs_].rearrange("c p t -> p c t"), writes=["ybr%d" % b])
            for j in range(8):
                sb_ = j % 2
                for bi in range(3):
                    for kc in range(8):
                        S.op("pe", lambda e, bi=bi, kc=kc, j=j, b=b: e.matmul(
                            gp[bi][:], wg[:, kc, bi * 1024 + j * 128:bi * 1024 + (j + 1) * 128], hTg[:, b, kc, :], start=(kc == 0), stop=(kc == 7)),
                            ["wg", "hTg%d" % b], ["gp%d" % bi])
                    for kc in range(4):
                        S.op("pe", lambda e, bi=bi, kc=kc, j=j, b=b: e.matmul(
                            bp[bi][:], wb[:, bi, kc, j * 128:(j + 1) * 128], ybr[:, b, bi, kc, :], start=(kc == 0), stop=(kc == 3)),
                            ["wb", "ybr%d" % b], ["bp%d" % bi])
                for bi in range(3):
                    S.op("act", lambda e, bi=bi, sb_=sb_: e.activation(out=sig[:, sb_, bi], in_=gp[bi][:], func=AF.Sigmoid),
                         ["gp%d" % bi], ["sig%d_%d" % (sb_, bi)])
                for bi, mm_ in enumerate((m0, m1, m2)):
                    S.op("dve", lambda e, bi=bi, mm_=mm_, sb_=sb_: e.tensor_tensor(out=mm_[:], in0=bp[bi][:], in1=sig[:, sb_, bi], op=ALU.mult),
                         ["bp%d" % bi, "sig%d_%d" % (sb_, bi)], ["m%d" % bi])
                S.op("pool", lambda e: e.tensor_tensor(out=m0[:], in0=m0[:], in1=m1[:], op=ALU.add), ["m0", "m1"], ["m0"])
                S.op("pool", lambda e, j=j: e.tensor_tensor(out=mT[:, j, :], in0=m0[:], in1=m2[:], op=ALU.add), ["m0", "m2"], ["mT"])
            for tt in range(GW // 128):
                xb = xcnt % 2
                xcnt += 1
                row0 = gi * GW + tt * 128
                S.dma("sp", xt[:, xb], xin[row0:row0 + 128, :], writes=["xt%d" % xb])
                for nn in range(2):
                    for kc in range(8):
                        S.op("pe", lambda e, nn=nn, kc=kc, tt=tt: e.matmul(
                            ao[nn][:], mT[:, kc, tt * 128:(tt + 1) * 128], wo[:, kc, nn * 512:(nn + 1) * 512], start=(kc == 0), stop=(kc == 7)),
                            ["mT", "wo"], ["ao%d" % nn])
                for nn in range(2):
                    S.op("dve", lambda e, nn=nn: e.tensor_tensor(out=tmp[:, nn * 512:(nn + 1) * 512], in0=ao[nn][:], in1=g1[:, nn * 512:(nn + 1) * 512], op=ALU.mult),
                         ["ao%d" % nn, "g1"], ["tmp"])
                S.op("pool", lambda e, xb=xb: e.tensor_tensor(out=xt[:, xb], in0=xt[:, xb], in1=tmp[:], op=ALU.add), ["tmp", "xt%d" % xb], ["xt%d" % xb])
                S.dma("sp", Sc["x1" + s][row0:row0 + 128, :], xt[:, xb], reads=["xt%d" % xb])
        S.flush()


def phase_ffn(g, l, s, n, xout):
    nc, S, I, Sc = g.nc, g.S, g.I, g.Sc
    r = 0 if s == "L" else 1
    GW = 256
    ng = n // GW
    import contextlib
    with contextlib.ExitStack() as st:
        sb, ps = mk_alloc(g, st)
        w1 = sb("w1", [128, 8, DFF], BF16)
        w2 = sb("w2", [128, 32, D], BF16)
        A2 = sb("A2", [128, D])
        B2 = sb("B2", [128, D])
        g2 = sb("g2", [128, D])
        tmp = sb("tmp", [128, D])
        xt = sb("xt", [128, 4, D])
        ssx = sb("ssx", [128, 4])
        hb = sb("hb", [128, 2, D], BF16)
        h2T = sb("h2T", [128, 2, 8, GW], BF16)
        aT = sb("aT", [128, 32, GW], BF16)
        r32 = sb("r32", [128, 2, GW])
        tr = ps("tr", [128, 8, 128], BF16)
        fp = [ps("fp%d" % i, [128, GW]) for i in range(2)]
        yp = [ps("yp%d" % i, [128, 512]) for i in range(4)]
        for kc in range(8):
            for cb in range(2):
                S.dma("pool", w1[:, kc, cb * 2048:(cb + 1) * 2048], I["w_ff1"][l, kc * 128:(kc + 1) * 128, cb * 2048:(cb + 1) * 2048], writes=["w1"])
        for fh in range(8):
            S.dma("pool", w2[:, 4 * fh:4 * fh + 4, :], I["w_ff2"][l, fh * 512:(fh + 1) * 512, :].rearrange("(kc p) n -> p kc n", p=128), writes=["w2"])
        load_bc(S, "sp", B2[:], Sc["mod"][l, r, 3 * D:4 * D], "B2")
        load_bc(S, "sp", A2[:], Sc["mod"][l, r, 4 * D:5 * D], "A2")
        load_bc(S, "sp", g2[:], Sc["mod"][l, r, 5 * D:6 * D], "g2")
        load_bc(S, "sp", tmp[:], I["norm2"][l], "tmp")
        S.op("dve", lambda e: e.scalar_tensor_tensor(out=A2[:], in0=A2[:], scalar=1.0, in1=tmp[:], op0=ALU.add, op1=ALU.mult), ["A2", "tmp"], ["A2"])
        tcnt = 0
        for gi in range(ng):
            gb = gi % 2
            for tt in range(2):
                xs = 2 * gb + tt
                hb_ = tcnt % 2
                tcnt += 1
                row0 = gi * GW + tt * 128
                S.dma("sp", xt[:, xs], Sc["x1" + s][row0:row0 + 128, :], writes=["xt%d" % xs])
                S.op("act", lambda e, xs=xs: e.activation(out=tmp[:], in_=xt[:, xs], func=AF.Square, accum_out=ssx[:, xs:xs + 1]),
                     ["xt%d" % xs], ["tmp", "ssx%d" % xs])
                S.op("act", lambda e, xs=xs: e.activation(out=ssx[:, xs:xs + 1], in_=ssx[:, xs:xs + 1], func=AF.Sqrt, bias=EPS, scale=1.0 / D),
                     ["ssx%d" % xs], ["ssx%d" % xs])
                S.op("dve", lambda e, xs=xs: e.reciprocal(out=ssx[:, xs:xs + 1], in_=ssx[:, xs:xs + 1]), ["ssx%d" % xs], ["ssx%d" % xs])
                S.op("dve", lambda e, xs=xs: e.scalar_tensor_tensor(out=tmp[:], in0=xt[:, xs], scalar=ssx[:, xs:xs + 1], in1=A2[:], op0=ALU.mult, op1=ALU.mult),
                     ["xt%d" % xs, "ssx%d" % xs, "A2"], ["tmp"])
                S.op("pool", lambda e, hb_=hb_: e.tensor_tensor(out=hb[:, hb_], in0=tmp[:], in1=B2[:], op=ALU.add), ["tmp", "B2"], ["hb%d" % hb_])
                for kc in range(8):
                    S.op("pe", lambda e, hb_=hb_, kc=kc: e.transpose(tr[:, kc, :], hb[:, hb_, kc * 128:(kc + 1) * 128], g.ident[:]),
                         ["hb%d" % hb_, "ident"], ["tr"])
                S.op("act", lambda e, gb=gb, tt=tt: e.copy(out=h2T[:, gb, :, tt * 128:(tt + 1) * 128], in_=tr[:]), ["tr"], ["h2T%d" % gb])
            for fc in range(32):
                fb = fc % 2
                for kc in range(8):
                    S.op("pe", lambda e, fb=fb, kc=kc, fc=fc, gb=gb: e.matmul(fp[fb][:], w1[:, kc, fc * 128:(fc + 1) * 128], h2T[:, gb, kc, :],
                                                                          start=(kc == 0), stop=(kc == 7)),
                         ["w1", "h2T%d" % gb], ["fp%d" % fb])
                S.op("act", lambda e, fb=fb: e.activation(out=r32[:, fb], in_=fp[fb][:], func=AF.Relu), ["fp%d" % fb], ["r32_%d" % fb])
                S.op("pool" if fc % 2 else "dve", lambda e, fb=fb, fc=fc: e.tensor_tensor(out=aT[:, fc, :], in0=r32[:, fb], in1=r32[:, fb], op=ALU.mult),
                     ["r32_%d" % fb], ["aT"])
            for tt in range(2):
                xs = 2 * gb + tt
                for nn in range(2):
                    yb_ = 2 * tt + nn
                    for fc in range(32):
                        S.op("pe", lambda e, yb_=yb_, fc=fc, tt=tt, nn=nn: e.matmul(yp[yb_][:], aT[:, fc, tt * 128:(tt + 1) * 128], w2[:, fc, nn * 512:(nn + 1) * 512],
                                                                               start=(fc == 0), stop=(fc == 31)),
                             ["aT", "w2"], ["yp%d" % yb_])
                for nn in range(2):
                    yb_ = 2 * tt + nn
                    S.op("dve", lambda e, yb_=yb_, nn=nn: e.tensor_tensor(out=tmp[:, nn * 512:(nn + 1) * 512], in0=yp[yb_][:], in1=g2[:, nn * 512:(nn + 1) * 512], op=ALU.mult),
                         ["yp%d" % yb_, "g2"], ["tmp"])
                row0 = gi * GW + tt * 128
                S.op("pool", lambda e, xs=xs: e.tensor_tensor(out=xt[:, xs], in0=xt[:, xs], in1=tmp[:], op=ALU.add), ["tmp", "xt%d" % xs], ["xt%d" % xs])
                S.dma("sp", xout[row0:row0 + 128, :], xt[:, xs], reads=["xt%d" % xs])
        S.flush()


def _band_mats(n):
    out = np.zeros((4, 5, 128, 128), np.float32)
    tl = np.arange(128)

    def mat(w, tile, src_tile):
        t = tile * 128 + tl
        lo = np.clip(t - w // 2, 0, n)
        hi = np.clip(t + w // 2, 0, n)
        cnt = (hi - lo).astype(np.float32)
        sg = src_tile * 128 + tl
        m = ((sg[:, None] >= lo[None, :]) & (sg[:, None] < hi[None, :])).astype(np.float32) / cnt[None, :]
        if tile == src_tile:
            m = m - np.eye(128, dtype=np.float32)
        return m
    nt = n // 128
    mid = 1 if nt > 2 else 0
    for wi, w in enumerate(POOL_WINDOWS):
        if nt > 2:
            out[wi, 0] = mat(w, mid, mid - 1)
            out[wi, 1] = mat(w, mid, mid)
            out[wi, 2] = mat(w, mid, mid + 1)
        else:
            out[wi, 0] = mat(w, 1, 0)
            out[wi, 2] = mat(w, 0, 1)
        out[wi, 3] = mat(w, 0, 0)
        out[wi, 4] = mat(w, nt - 1, nt - 1)
    return out.astype(ml_dtypes.bfloat16)


def make_consts(T):
    rows = T // 64
    row = np.repeat(np.arange(rows, dtype=np.float32), 64)
    col = np.tile(np.arange(64, dtype=np.float32), rows)
    inv = (10000.0 ** (-np.arange(16, dtype=np.float32) / 16)).astype(np.float32)
    ang = np.concatenate([row[:, None] * inv, col[:, None] * inv], axis=-1).astype(np.float32)
    j = np.arange(128)[:, None]
    a = np.arange(128)[None, :]
    return {
        "cos": np.cos(ang).astype(np.float32), "sin": np.sin(ang).astype(np.float32),
        "cosc": np.ones((CTX, 32), np.float32), "sinc": np.zeros((CTX, 32), np.float32),
        "ident": np.eye(128, dtype=np.float32).astype(ml_dtypes.bfloat16),
        "ones": np.ones((128, 128), np.float32).astype(ml_dtypes.bfloat16),
        "band": _band_mats(T), "bandc": _band_mats(CTX),
        "mask_prev": (j >= a).astype(np.float32).astype(ml_dtypes.bfloat16),
        "mask_next": (j <= a).astype(np.float32).astype(ml_dtypes.bfloat16),
    }


def core_inputs(inp, b, consts):
    f = lambda a: np.ascontiguousarray(np.asarray(a, dtype=np.float32))
    m = dict(consts)
    m["x"] = f(inp["x"][b])
    m["ctx"] = f(inp["ctx"][b])
    cv = np.stack([f(inp["c"][b]), f(inp["c_ctx"])], 0)
    m["cvec"] = np.ascontiguousarray(cv.reshape(2, 8, 128).transpose(2, 0, 1).reshape(128, 16))
    for k in ("w_ada", "b_ada", "norm1", "norm2", "w_in", "w_pool", "pool_scale", "swa_q_norm", "swa_k_norm",
              "swa_sink", "diff_q_norm", "diff_k_norm", "diff_subln", "w_br_pool", "w_br_swa", "w_br_diff",
              "w_out", "w_ff1", "w_ff2"):
        m[k] = f(inp[k])
    m["diff_lambda"] = f(inp["diff_lambda"]).reshape(DEPTH, 256)
    return m


_NC_CACHE = {}


def kernel(**inputs):
    B, T, _ = inputs["x"].shape
    if T not in _NC_CACHE:
        _NC_CACHE[T] = build_program(T)
    nc = _NC_CACHE[T]
    consts = make_consts(T)
    n_cores = 8
    in_maps = [core_inputs(inputs, c % B, consts) for c in range(n_cores)]
    res = run_bass_kernel_spmd(nc, in_maps, core_ids=list(range(n_cores)))
    return np.stack([np.asarray(res.results[b]["y"], dtype=np.float32) for b in range(B)], 0)
```

```python
import math
import numpy as np
import ml_dtypes
import concourse.bass as bass
import concourse.mybir as mybir
from concourse.bass_utils import run_bass_kernel_spmd

F32 = mybir.dt.float32
BF16 = mybir.dt.bfloat16
AF = mybir.ActivationFunctionType
ALU = mybir.AluOpType
AX = mybir.AxisListType

D = 1024
DEPTH = 2
CTX = 256
HD = 64
EPS = 1e-6
DFF = 4096
POOL_WINDOWS = (2, 4, 8, 16)
OFF_U, OFF_QS, OFF_KS, OFF_VS, OFF_QD, OFF_KD, OFF_VD, OFF_G = 0, 512, 1024, 1152, 1280, 1792, 2304, 2816
P1_BLOCKS = [(OFF_QS, 512), (OFF_KS, 128), (OFF_QD, 512), (OFF_KD, 512), (OFF_VS, 128), (OFF_VD, 512), (OFF_U, 512)]
NH = 26
C_VS, C_VD, C_U = 1664, 1792, 2304
P1_COLS = 2816


SKIP_SAME_ENGINE = True


class Sched:
    COMPUTE = ("pe", "act", "dve", "pool")

    def __init__(self, nc, stack):
        self.nc = nc
        self.eng = {"pe": nc.tensor, "act": nc.scalar, "dve": nc.vector, "pool": nc.gpsimd, "sp": nc.sync}
        self.tsem = {e: stack.enter_context(nc.semaphore("tl_" + e)) for e in self.COMPUTE}
        self.tick = {e: 0 for e in self.COMPUTE}
        self.nds = {"sp": 8, "pool": 6, "act": 2}
        self.dsem = {q: [stack.enter_context(nc.semaphore("d_%s%d" % (q, i))) for i in range(n)]
                     for q, n in self.nds.items()}
        self.dcnt = {q: 0 for q in self.nds}
        self.ccsem = stack.enter_context(nc.semaphore("ccsem"))
        self.cccnt = 0
        self.ops = []
        self.last_w = {}
        self.readers = {}
        self.waited = {e: {} for e in self.eng}
        self.sems = {}
        self.n_inst = 0
        self.skip_pe = True

    def op(self, eng, fn, reads=(), writes=(), kind="c"):
        idx = len(self.ops)
        deps = {}
        for k in reads:
            if k in self.last_w:
                deps[self.last_w[k]] = True
        for k in writes:
            if k in self.last_w:
                deps.setdefault(self.last_w[k], False)
            for rd in self.readers.get(k, ()):
                deps.setdefault(rd, False)
        if kind == "c" and SKIP_SAME_ENGINE and self.skip_pe:
            for d in list(deps):
                od = self.ops[d]
                if od["kind"] == "c" and od["eng"] == eng and eng == "pe":
                    del deps[d]
        for k in reads:
            self.readers.setdefault(k, []).append(idx)
        for k in writes:
            self.last_w[k] = idx
            self.readers[k] = []
        self.ops.append({"eng": eng, "fn": fn, "deps": deps, "kind": kind, "sig": kind != "c", "signal": None})
        return idx

    def dma(self, q, out, in_, reads=(), writes=()):
        return self.op(q, lambda e: e.dma_start(out=out, in_=in_), reads, writes, kind="d")

    def _wait(self, e, sem, val):
        w = self.waited[e]
        key = id(sem)
        if w.get(key, 0) < val:
            self.eng[e].wait_ge(sem, val)
            w[key] = val
            self.sems[key] = sem

    def flush(self):
        ops = self.ops
        for o in ops:
            for d in o["deps"]:
                ops[d]["sig"] = True
        last = {}
        for i, o in enumerate(ops):
            if o["kind"] == "c":
                last[o["eng"]] = i
        for i in last.values():
            ops[i]["sig"] = True
        final = {}
        for o in ops:
            e = o["eng"]
            for d in sorted(o["deps"]):
                sem, val = ops[d]["signal"]
                self._wait(e, sem, val)
            if o["kind"] == "d":
                j = self.dcnt[e]
                self.dcnt[e] += 1
                sem = self.dsem[e][j % self.nds[e]]
                val = 16 * (j // self.nds[e] + 1)
                self._wait(e, sem, val - 16)
                o["fn"](self.eng[e]).then_inc(sem, 16)
                o["signal"] = (sem, val)
                final[id(sem)] = (sem, val)
            elif o["kind"] == "cc":
                self.cccnt += 1
                o["fn"](self.eng[e]).then_inc(self.ccsem)
                o["signal"] = (self.ccsem, self.cccnt)
                final[id(self.ccsem)] = (self.ccsem, self.cccnt)
            else:
                ins = o["fn"](self.eng[e])
                if o["sig"]:
                    self.tick[e] += 1
                    ins.then_inc(self.tsem[e], 1)
                    o["signal"] = (self.tsem[e], self.tick[e])
                    final[id(self.tsem[e])] = (self.tsem[e], self.tick[e])
            self.n_inst += 1
        for e in self.eng:
            for sem, val in final.values():
                self._wait(e, sem, val)
        self.ops = []
        self.last_w = {}
        self.readers = {}


def dram_bcast(ap_1d, parts, mid=None):
    n = ap_1d.shape[-1]
    dims = [[0, parts]]
    if mid is not None:
        dims.append([0, mid])
    dims.append([1, n])
    return bass.AP(ap_1d.tensor, ap_1d.offset, dims)


class Ctx:
    pass


_UID = [0]


def mk_alloc(g, st):
    nc = g.nc
    _UID[0] += 1
    u = "_%d" % _UID[0]

    def sb(name, shape, dt=F32):
        return st.enter_context(nc.sbuf_tensor(name + u, list(shape), dt))

    def ps(name, shape, dt=F32):
        return st.enter_context(nc.psum_tensor(name + u, list(shape), dt))
    return sb, ps


def build_program(T, debug_outs=(), depth=DEPTH, phases=None):
    assert T % 256 == 0
    NT = T // 128
    NTC = CTX // 128
    TK = T + CTX
    NKT = TK // 128
    nc = bass.Bass("TRN2", target_bir_lowering=False)
    g = Ctx()
    g.nc = nc
    g.T, g.NT, g.NTC, g.TK, g.NKT = T, NT, NTC, TK, NKT

    def din(name, shape, dt=F32):
        return nc.dram_tensor(name, list(shape), dt, kind="ExternalInput").ap()

    def dscr(name, shape, dt=BF16):
        kind = "ExternalOutput" if name in debug_outs else "Internal"
        return nc.dram_tensor(name, list(shape), dt, kind=kind).ap()

    I = {}
    I["x"] = din("x", [T, D])
    I["ctx"] = din("ctx", [CTX, D])
    I["cvec"] = din("cvec", [128, 16])
    I["w_ada"] = din("w_ada", [DEPTH, D, 6 * D])
    I["b_ada"] = din("b_ada", [DEPTH, 6 * D])
    I["norm1"] = din("norm1", [DEPTH, D])
    I["norm2"] = din("norm2", [DEPTH, D])
    I["w_in"] = din("w_in", [DEPTH, D, 5888])
    I["w_pool"] = din("w_pool", [DEPTH, 4, 128, 128])
    I["pool_scale"] = din("pool_scale", [DEPTH, 512])
    I["swa_q_norm"] = din("swa_q_norm", [DEPTH, 64])
    I["swa_k_norm"] = din("swa_k_norm", [DEPTH, 64])
    I["swa_sink"] = din("swa_sink", [DEPTH, 8])
    I["diff_q_norm"] = din("diff_q_norm", [DEPTH, 64])
    I["diff_k_norm"] = din("diff_k_norm", [DEPTH, 64])
    I["diff_lambda"] = din("diff_lambda", [DEPTH, 256])
    I["diff_subln"] = din("diff_subln", [DEPTH, 128])
    I["w_br_pool"] = din("w_br_pool", [DEPTH, 512, D])
    I["w_br_swa"] = din("w_br_swa", [DEPTH, 512, D])
    I["w_br_diff"] = din("w_br_diff", [DEPTH, 512, D])
    I["w_out"] = din("w_out", [DEPTH, D, D])
    I["w_ff1"] = din("w_ff1", [DEPTH, D, DFF])
    I["w_ff2"] = din("w_ff2", [DEPTH, DFF, D])
    I["cos"] = din("cos", [T, 32])
    I["sin"] = din("sin", [T, 32])
    I["cosc"] = din("cosc", [CTX, 32])
    I["sinc"] = din("sinc", [CTX, 32])
    I["ident"] = din("ident", [128, 128], BF16)
    I["ones"] = din("ones", [128, 128], BF16)
    I["band"] = din("band", [4, 7, 128, 128], BF16)
    I["bandc"] = din("bandc", [4, 7, 128, 128], BF16)
    I["masks"] = din("masks", [4, 128, 128], BF16)
    g.I = I
    y = nc.dram_tensor("y", [T, D], F32, kind="ExternalOutput").ap()
    g.y = y

    Sc = {}
    Sc["mod"] = dscr("mod", [DEPTH, 2, 6 * D], F32)
    for s, n in (("L", T), ("C", CTX)):
        Sc["hT" + s] = dscr("hT" + s, [8, 128, n])
        Sc["QsT" + s] = dscr("QsT" + s, [64, 8, n])
        Sc["QdT" + s] = dscr("QdT" + s, [4, 128, n])
        Sc["u" + s] = dscr("u" + s, [n, 512])
        Sc["ypT" + s] = dscr("ypT" + s, [4, 128, n])
        Sc["ysT" + s] = dscr("ysT" + s, [4, 128, n])
        Sc["ydT" + s] = dscr("ydT" + s, [4, 128, n])
        Sc["x1" + s] = dscr("x1" + s, [n, D], F32)
        Sc["xo" + s] = dscr("xo" + s, [n, D], F32)
    g.NCH = T // 256
    for j in range(g.NCH):
        Sc["KdTo%d" % j] = dscr("KdTo%d" % j, [512, 256])
        Sc["KdTa%d" % j] = dscr("KdTa%d" % j, [1024, 256])
        Sc["Vdo%d" % j] = dscr("Vdo%d" % j, [256, 512])
        Sc["Vda%d" % j] = dscr("Vda%d" % j, [512, 512])
    Sc["KsTo"] = dscr("KsTo", [64, 2, T])
    Sc["Vso"] = dscr("Vso", [T, 128])
    Sc["KsHo"] = dscr("KsHo", [64, 512])
    Sc["KsHa"] = dscr("KsHa", [128, 512])
    Sc["VsHo"] = dscr("VsHo", [256, 128])
    Sc["VsHa"] = dscr("VsHa", [512, 128])
    Sc["uHo"] = dscr("uHo", [256, 512])
    Sc["uHa"] = dscr("uHa", [512, 512])
    Sc["KsTc"] = dscr("KsTc", [64, 2, CTX])
    Sc["KdTc"] = dscr("KdTc", [4, 128, CTX])
    Sc["Vsc"] = dscr("Vsc", [CTX, 128])
    Sc["Vdc"] = dscr("Vdc", [CTX, 512])
    g.Sc = Sc

    import contextlib
    with contextlib.ExitStack() as stack:
        stack.enter_context(nc.allow_non_contiguous_dma(reason="small strided parameter / layout loads"))
        S = Sched(nc, stack)
        g.S = S
        ident = stack.enter_context(nc.sbuf_tensor("ident_sb", [128, 128], BF16))
        ones = stack.enter_context(nc.sbuf_tensor("ones_sb", [128, 128], BF16))
        S.dma("sp", ident[:], I["ident"], writes=["ident"])
        S.dma("sp", ones[:], I["ones"], writes=["ones"])
        g.ident, g.ones = ident, ones
        S.flush()

        for l in range(depth):
            last = l == DEPTH - 1
            phase_mod(g, l)
            xin_L = I["x"] if l == 0 else Sc["xoL"]
            xin_C = I["ctx"] if l == 0 else Sc["xoC"]
            phase_p1(g, l, "C", xin_C, NTC, I["cosc"], I["sinc"],
                     dict(KsT=Sc["KsTc"], Vs=Sc["Vsc"],
                          KdT=lambda t0, nt: [(Sc["KdTc"][:, :, t0:t0 + nt].rearrange("h p t -> p h t"), 0, nt)],
                          Vd=lambda t: Sc["Vdc"][t * 128:(t + 1) * 128, :]))
            phase_p1(g, l, "L", xin_L, NT, I["cos"], I["sin"],
                     dict(KsT=Sc["KsTo"], Vs=Sc["Vso"],
                          KdT=lambda t0, nt: [(Sc["KdTo%d" % ((t0 + c) // 256)].rearrange("(h p) t -> p h t", h=4), c, c + 256)
                                              for c in range(0, nt, 256)],
                          Vd=lambda t: Sc["Vdo%d" % (t // 2)][(t % 2) * 128:(t % 2 + 1) * 128, :]))
            phase_exchange(g)
            streams = [("L", T, xin_L, (y if last else Sc["xoL"]))]
            if not last:
                streams.append(("C", CTX, xin_C, Sc["xoC"]))
            for s, n, xin, xout in streams:
                if phases is None or "pool" in phases:
                    phase_pool(g, l, s, n)
                if phases is None or "swa" in phases:
                    phase_swa(g, l, s, n)
                if phases is None or "diff" in phases:
                    phase_diff(g, l, s, n)
                if phases is None or "merge" in phases:
                    phase_merge(g, l, s, n, xin)
                if phases is None or "ffn" in phases:
                    phase_ffn(g, l, s, n, xout)
    return nc


def phase_mod(g, l):
    nc, S, I, Sc = g.nc, g.S, g.I, g.Sc
    import contextlib
    with contextlib.ExitStack() as st:
        sb, ps = mk_alloc(g, st)
        cv = sb("cv", [128, 16])
        cact = sb("cact", [128, 16])
        wada = sb("wada", [128, 2, 8, 512])
        bada = sb("bada", [2, 2, 512])
        modo = sb("modo", [2, 2, 512])
        modps = ps("modps", [2, 2, 512])
        S.dma("sp", cv[:], I["cvec"], writes=["cv"])
        S.op("act", lambda e: e.activation(out=cact[:], in_=cv[:], func=AF.Silu), ["cv"], ["cact"])
        for n in range(12):
            b = n % 2
            S.dma("sp", wada[:, b], I["w_ada"][l, :, n * 512:(n + 1) * 512].rearrange("(kc p) n -> p kc n", p=128),
                  writes=["wada%d" % b])
            S.dma("sp", bada[:, b], dram_bcast(I["b_ada"][l, n * 512:(n + 1) * 512], 2), writes=["bada%d" % b])
            for kc in range(8):
                lhsT = cact[:].rearrange("p (r k) -> p k r", r=2)[:, kc, :]
                S.op("pe", lambda e, lhsT=lhsT, kc=kc, b=b: e.matmul(modps[:, b], lhsT, wada[:, b, kc, :],
                                                                  start=(kc == 0), stop=(kc == 7)),
                     ["cact", "wada%d" % b], ["modps%d" % b])
            S.op("dve", lambda e, b=b: e.tensor_tensor(out=modo[:, b], in0=modps[:, b], in1=bada[:, b], op=ALU.add),
                 ["modps%d" % b, "bada%d" % b], ["modo%d" % b])
            S.dma("sp", Sc["mod"][l, :, n * 512:(n + 1) * 512], modo[:, b], reads=["modo%d" % b])
        S.flush()


def load_bc(S, q, tile_ap, vec_ap, key):
    S.dma(q, tile_ap, dram_bcast(vec_ap, 128), writes=[key])


def phase_p1(g, l, s, xin, ntiles, cos_d, sin_d, dst):
    nc, S, I, Sc = g.nc, g.S, g.I, g.Sc
    r = 0 if s == "L" else 1
    n = ntiles * 128
    import contextlib
    with contextlib.ExitStack() as st:
        sb, ps = mk_alloc(g, st)
        w = sb("p1w", [128, 8, P1_COLS], BF16)
        A1 = sb("A1", [128, D])
        B1 = sb("B1", [128, D])
        nrm = sb("nrm", [128, D])
        G = sb("G", [128, NH, 64])
        xt = sb("xt", [128, 2, D])
        junk = sb("junk", [128, D])
        ssx = sb("ssx", [128, 2])
        hb = sb("hb", [128, 2, D], BF16)
        hT = sb("hT", [128, 2, 8, 128], BF16)
        zs = sb("zs", [128, 2, 1664])
        sq = sb("sq", [128, 1664])
        ssh = sb("ssh", [128, 2, NH])
        qn = sb("qn", [128, NH, 64])
        t1 = sb("t1", [128, NH, 32])
        t2 = sb("t2", [128, NH, 32])
        t3 = sb("t3", [128, NH, 32])
        t4 = sb("t4", [128, NH, 32])
        qo = sb("qo", [128, 2, NH, 64], BF16)
        vu = sb("vu", [128, 2, 1152], BF16)
        cs = sb("cs", [128, 2, 2, 32])
        stQs = sb("stQs", [64, 2, 8, 512], BF16)
        stKs = sb("stKs", [64, 2, 2, 512], BF16)
        stQd = sb("stQd", [128, 2, 4, 512], BF16)
        stKd = sb("stKd", [128, 2, 4, 512], BF16)
        zp = [ps("zp%d" % i, [128, 512]) for i in range(6)]
        tr1 = ps("tr1", [128, 8, 128], BF16)
        tr2 = ps("tr2", [128, 8, 128], BF16)

        c0 = 0
        for (off, wd) in P1_BLOCKS:
            for kh in range(2):
                S.dma("pool", w[:, kh * 4:(kh + 1) * 4, c0:c0 + wd],
                      I["w_in"][l, kh * 512:(kh + 1) * 512, off:off + wd].rearrange("(kc p) n -> p kc n", p=128),
                      writes=["w"])
            c0 += wd
        load_bc(S, "sp", B1[:], Sc["mod"][l, r, 0:D], "B1")
        load_bc(S, "sp", A1[:], Sc["mod"][l, r, D:2 * D], "A1")
        load_bc(S, "sp", nrm[:], I["norm1"][l], "nrm")
        S.op("dve", lambda e: e.scalar_tensor_tensor(out=A1[:], in0=A1[:], scalar=1.0, in1=nrm[:],
                                                     op0=ALU.add, op1=ALU.mult), ["A1", "nrm"], ["A1"])
        for (h0, hn, nm) in ((0, 8, "swa_q_norm"), (8, 2, "swa_k_norm"), (10, 8, "diff_q_norm"), (18, 8, "diff_k_norm")):
            S.dma("sp", G[:, h0:h0 + hn, :], dram_bcast(I[nm][l], 128, hn), writes=["G"])

        for t in range(ntiles):
            b = t % 2
            gi = t // 4
            gb = gi % 2
            tq = t % 4
            S.dma("sp", xt[:, b], xin[t * 128:(t + 1) * 128, :], writes=["xt%d" % b])
            S.dma("sp", cs[:, b, 0], cos_d[t * 128:(t + 1) * 128, :], writes=["cs%d" % b])
            S.dma("sp", cs[:, b, 1], sin_d[t * 128:(t + 1) * 128, :], writes=["cs%d" % b])
            S.op("act", lambda e, b=b: e.activation(out=junk[:], in_=xt[:, b], func=AF.Square, accum_out=ssx[:, b:b + 1]),
                 ["xt%d" % b], ["junk", "ssx%d" % b])
            S.op("act", lambda e, b=b: e.activation(out=ssx[:, b:b + 1], in_=ssx[:, b:b + 1], func=AF.Sqrt, bias=EPS, scale=1.0 / D),
                 ["ssx%d" % b], ["ssx%d" % b])
            S.op("dve", lambda e, b=b: e.reciprocal(out=ssx[:, b:b + 1], in_=ssx[:, b:b + 1]), ["ssx%d" % b], ["ssx%d" % b])
            S.op("dve", lambda e, b=b: e.scalar_tensor_tensor(out=xt[:, b], in0=xt[:, b], scalar=ssx[:, b:b + 1], in1=A1[:],
                                                              op0=ALU.mult, op1=ALU.mult), ["xt%d" % b, "ssx%d" % b, "A1"], ["xt%d" % b])
            S.op("pool", lambda e, b=b: e.tensor_tensor(out=hb[:, b], in0=xt[:, b], in1=B1[:], op=ALU.add),
                 ["xt%d" % b, "B1"], ["hb%d" % b])
            for kc in range(8):
                S.op("pe", lambda e, b=b, kc=kc: e.transpose(tr1[:, kc, :], hb[:, b, kc * 128:(kc + 1) * 128], g.ident[:]),
                     ["hb%d" % b, "ident"], ["tr1"])
            S.op("act", lambda e, b=b: e.copy(out=hT[:, b], in_=tr1[:]), ["tr1"], ["hT%d" % b])
            S.dma("sp", Sc["hT" + s][:, :, t * 128:(t + 1) * 128].rearrange("k p t -> p k t"), hT[:, b], reads=["hT%d" % b])
            for ci in range(6):
                cw = min(512, P1_COLS - ci * 512)
                for kc in range(8):
                    S.op("pe", lambda e, b=b, kc=kc, ci=ci, cw=cw: e.matmul(zp[ci][:, 0:cw], hT[:, b, kc, :], w[:, kc, ci * 512:ci * 512 + cw],
                                                                       start=(kc == 0), stop=(kc == 7)),
                         ["hT%d" % b, "w"], ["zp%d" % ci])
            for ci in range(6):
                lo, hi = ci * 512, min(P1_COLS, ci * 512 + 512)
                if hi <= 1664 or lo < 1664:
                    h2 = min(hi, 1664)
                    S.op("act", lambda e, b=b, ci=ci, lo=lo, h2=h2: e.copy(out=zs[:, b, lo:h2], in_=zp[ci][:, 0:h2 - lo]),
                         ["zp%d" % ci], ["zs%d" % b])
                    S.op("act", lambda e, ci=ci, lo=lo, h2=h2: e.activation(out=sq[:, lo:h2], in_=zp[ci][:, 0:h2 - lo], func=AF.Square),
                         ["zp%d" % ci], ["sq"])
                if hi > 1664:
                    l2 = max(lo, 1664)
                    S.op("act", lambda e, b=b, ci=ci, lo=lo, l2=l2, hi=hi: e.copy(out=vu[:, b, l2 - 1664:hi - 1664], in_=zp[ci][:, l2 - lo:hi - lo]),
                         ["zp%d" % ci], ["vu%d" % b])
            S.dma("sp", dst["Vs"][t * 128:(t + 1) * 128, :], vu[:, b, 0:128], reads=["vu%d" % b])
            S.dma("sp", dst["Vd"](t), vu[:, b, 128:640], reads=["vu%d" % b])
            S.dma("sp", Sc["u" + s][t * 128:(t + 1) * 128, :], vu[:, b, 640:1152], reads=["vu%d" % b])
            if s == "L" and t in (0, ntiles - 1):
                fl = 0 if t == 0 else 1
                S.dma("sp", Sc["VsHo"][fl * 128:(fl + 1) * 128, :], vu[:, b, 0:128], reads=["vu%d" % b])
                S.dma("sp", Sc["uHo"][fl * 128:(fl + 1) * 128, :], vu[:, b, 640:1152], reads=["vu%d" % b])
            S.op("dve", lambda e, b=b: e.tensor_reduce(out=ssh[:, b], in_=sq[:].rearrange("p (h d) -> p h d", d=64), axis=AX.X, op=ALU.add),
                 ["sq"], ["ssh%d" % b])
            S.op("act", lambda e, b=b: e.activation(out=ssh[:, b], in_=ssh[:, b], func=AF.Sqrt, bias=EPS, scale=1.0 / 64),
                 ["ssh%d" % b], ["ssh%d" % b])
            S.op("dve", lambda e, b=b: e.reciprocal(out=ssh[:, b], in_=ssh[:, b]), ["ssh%d" % b], ["ssh%d" % b])
            zh = zs[:, b].rearrange("p (h d) -> p h d", d=64)
            S.op("dve", lambda e, b=b, zh=zh: e.tensor_tensor(out=qn[:], in0=zh, in1=ssh[:, b].unsqueeze(2).to_broadcast([128, NH, 64]), op=ALU.mult),
                 ["zs%d" % b, "ssh%d" % b], ["qn"])
            S.op("pool", lambda e: e.tensor_tensor(out=qn[:], in0=qn[:], in1=G[:], op=ALU.mult), ["qn", "G"], ["qn"])
            cb = cs[:, b, 0].unsqueeze(1).to_broadcast([128, NH, 32])
            sb_ = cs[:, b, 1].unsqueeze(1).to_broadcast([128, NH, 32])
            x1 = qn[:, :, 0:32]
            x2 = qn[:, :, 32:64]
            S.op("dve", lambda e, cb=cb, x1=x1: e.tensor_tensor(out=t1[:], in0=x1, in1=cb, op=ALU.mult), ["qn", "cs%d" % b], ["t1"])
            S.op("dve", lambda e, sb_=sb_, x2=x2: e.tensor_tensor(out=t2[:], in0=x2, in1=sb_, op=ALU.mult), ["qn", "cs%d" % b], ["t2"])
            S.op("dve", lambda e, b=b: e.tensor_tensor(out=qo[:, b, :, 0:32], in0=t1[:], in1=t2[:], op=ALU.subtract), ["t1", "t2"], ["qo%d" % b])
            S.op("pool", lambda e, cb=cb, x2=x2: e.tensor_tensor(out=t3[:], in0=x2, in1=cb, op=ALU.mult), ["qn", "cs%d" % b], ["t3"])
            S.op("pool", lambda e, sb_=sb_, x1=x1: e.tensor_tensor(out=t4[:], in0=x1, in1=sb_, op=ALU.mult), ["qn", "cs%d" % b], ["t4"])
            S.op("pool", lambda e, b=b: e.tensor_tensor(out=qo[:, b, :, 32:64], in0=t3[:], in1=t4[:], op=ALU.add), ["t3", "t4"], ["qo%d" % b])
            for h in range(8):
                S.op("pe", lambda e, b=b, h=h: e.transpose(tr2[0:64, h, :], qo[:, b, h, :], g.ident[:]), ["qo%d" % b, "ident"], ["tr2"])
            S.op("act", lambda e, gb=gb, tq=tq: e.copy(out=stQs[:, gb, :, tq * 128:(tq + 1) * 128], in_=tr2[0:64]), ["tr2"], ["stQs%d" % gb])
            for h in range(2):
                S.op("pe", lambda e, b=b, h=h: e.transpose(tr2[0:64, h, :], qo[:, b, 8 + h, :], g.ident[:]), ["qo%d" % b, "ident"], ["tr2"])
            S.op("act", lambda e, gb=gb, tq=tq: e.copy(out=stKs[:, gb, :, tq * 128:(tq + 1) * 128], in_=tr2[0:64, 0:2, :]), ["tr2"], ["stKs%d" % gb])
            for j in range(8):
                src = qo[:, b, 10 + 2 * j:12 + 2 * j, :].rearrange("p h d -> p (h d)")
                S.op("pe", lambda e, j=j, src=src: e.transpose(tr1[:, j, :], src, g.ident[:]), ["qo%d" % b, "ident"], ["tr1"])
            S.op("act", lambda e, gb=gb, tq=tq: e.copy(out=stQd[:, gb, :, tq * 128:(tq + 1) * 128], in_=tr1[:, 0:4, :]), ["tr1"], ["stQd%d" % gb])
            S.op("act", lambda e, gb=gb, tq=tq: e.copy(out=stKd[:, gb, :, tq * 128:(tq + 1) * 128], in_=tr1[:, 4:8, :]), ["tr1"], ["stKd%d" % gb])
            if tq == 3 or t == ntiles - 1:
                nt = (tq + 1) * 128
                t0 = gi * 512
                S.dma("sp", Sc["QsT" + s][:, :, t0:t0 + nt], stQs[:, gb, :, 0:nt], reads=["stQs%d" % gb])
                S.dma("sp", dst["KsT"][:, :, t0:t0 + nt], stKs[:, gb, :, 0:nt], reads=["stKs%d" % gb])
                S.dma("sp", Sc["QdT" + s][:, :, t0:t0 + nt].rearrange("h p t -> p h t"), stQd[:, gb, :, 0:nt], reads=["stQd%d" % gb])
                for (kap, c0_, c1_) in dst["KdT"](t0, nt):
                    S.dma("sp", kap, stKd[:, gb, :, c0_:c1_], reads=["stKd%d" % gb])
                if s == "L":
                    ksh = Sc["KsHo"].rearrange("p (g f t) -> p g f t", g=2, f=2)
                    if gi == 0:
                        S.dma("sp", ksh[:, :, 0, :], stKs[:, gb, :, 0:128], reads=["stKs%d" % gb])
                    if t == ntiles - 1:
                        S.dma("sp", ksh[:, :, 1, :], stKs[:, gb, :, nt - 128:nt], reads=["stKs%d" % gb])
        S.flush()


PAIRS = [[0, 1], [2, 3], [4, 5], [6, 7]]


def phase_exchange(g):
    S, Sc = g.S, g.Sc
    pairs_ = [("KsHo", "KsHa"), ("VsHo", "VsHa"), ("uHo", "uHa")]
    for j in range(g.NCH):
        pairs_ += [("KdTo%d" % j, "KdTa%d" % j), ("Vdo%d" % j, "Vda%d" % j)]
    for a, b in pairs_:
        S.op("pool", lambda e, a=a, b=b: e.collective_compute("AllGather", ALU.bypass, replica_groups=PAIRS,
                                                               ins=[Sc[a][:, :]], outs=[Sc[b][:, :]]), kind="cc")
    S.flush()


def phase_pool(g, l, s, n):
    nc, S, I, Sc = g.nc, g.S, g.I, g.Sc
    nt = n // 128
    band_d = I["band"] if s == "L" else I["bandc"]
    import contextlib
    with contextlib.ExitStack() as st:
        sb, ps = mk_alloc(g, st)
        band = sb("band", [128, 4, 7, 128], BF16)
        wp = sb("wp", [128, 4, 128], BF16)
        psc = sb("psc", [128, 4])
        ut = sb("ut", [128, 5, 512], BF16)
        pl = sb("pl", [128, 2, 128], BF16)
        yo = sb("yo", [128, 2, 4, 128], BF16)
        pp = [ps("pp%d" % i, [128, 128]) for i in range(2)]
        yp = [ps("yp%d" % i, [128, 128]) for i in range(2)]
        S.dma("sp", band[:], band_d.rearrange("w k s t -> s w k t"), writes=["band"])
        S.dma("pool", wp[:], I["w_pool"][l].rearrange("g c d -> c g d"), writes=["wp"])
        S.dma("sp", psc[:], I["pool_scale"][l].rearrange("(g d) -> d g", g=4), writes=["psc"])
        S.dma("sp", ut[:, 0], Sc["u" + s][0:128, :], writes=["ut0"])
        if s == "L":
            S.dma("sp", ut[:, 3], Sc["uHa"][128:256, :], writes=["ut3"])
            S.dma("sp", ut[:, 4], Sc["uHa"][256:384, :], writes=["ut4"])
        for t in range(nt):
            b = t % 2
            if t + 1 < nt:
                S.dma("sp", ut[:, (t + 1) % 3], Sc["u" + s][(t + 1) * 128:(t + 2) * 128, :], writes=["ut%d" % ((t + 1) % 3)])
            srcs = []
            if t > 0:
                srcs.append(((t - 1) % 3, 0))
            elif s == "L":
                srcs.append((3, 3))
            srcs.append((t % 3, 4 if t == 0 else (5 if t == nt - 1 else 1)))
            if t + 1 < nt:
                srcs.append(((t + 1) % 3, 2))
            elif s == "L":
                srcs.append((4, 6))
            for gq in range(4):
                pb = gq % 2
                for i, (slot, kind) in enumerate(srcs):
                    S.op("pe", lambda e, pb=pb, slot=slot, kind=kind, gq=gq, i=i, ns=len(srcs): e.matmul(
                        pp[pb][:], ut[:, slot, gq * 128:(gq + 1) * 128], band[:, gq, kind, :], start=(i == 0), stop=(i == ns - 1)),
                        ["ut%d" % slot, "band"], ["pp%d" % pb])
                S.op("act", lambda e, pb=pb: e.copy(out=pl[:, pb], in_=pp[pb][:]), ["pp%d" % pb], ["pl%d" % pb])
                S.op("pe", lambda e, pb=pb, gq=gq: e.matmul(yp[pb][:], wp[:, gq, :], pl[:, pb], start=True, stop=True),
                     ["pl%d" % pb, "wp"], ["yp%d" % pb])
                S.op("dve", lambda e, pb=pb, gq=gq, b=b: e.tensor_scalar(out=yo[:, b, gq, :], in0=yp[pb][:], scalar1=psc[:, gq:gq + 1],
                                                                         scalar2=None, op0=ALU.mult),
                     ["yp%d" % pb, "psc"], ["yo%d" % b])
            S.dma("sp", Sc["ypT" + s][:, :, t * 128:(t + 1) * 128].rearrange("g p t -> p g t"), yo[:, b], reads=["yo%d" % b])
        S.flush()


def phase_swa(g, l, s, n):
    nc, S, I, Sc = g.nc, g.S, g.I, g.Sc
    T = g.T
    nt = n // 128
    NT = T // 128
    W = T + 512
    nkt_all = W // 128
    ctx_kts = [NT + 2, NT + 3]
    import contextlib
    with contextlib.ExitStack() as st:
        sb, ps = mk_alloc(g, st)
        kst = sb("kst", [64, 2, W], BF16)
        vs = sb("vs", [128, nkt_all, 128], BF16)
        qb_ = sb("qb", [64, 2, 8, 128], BF16)
        mk = sb("mk", [128, 4, 128], BF16)
        esk = sb("esk", [64, 8])
        pt = sb("pt", [128, 3, 512], BF16)
        den = sb("den", [64, 512])
        yo = sb("yo", [64, 2, 4, 128], BF16)
        spp = [ps("spp%d" % i, [128, 512]) for i in range(2)]
        opp = [ps("opp%d" % i, [64, 512]) for i in range(2)]
        rpp = [ps("rpp%d" % i, [64, 512]) for i in range(2)]
        S.dma("sp", kst[:, :, T + 256:W], Sc["KsTc"], writes=["kst"])
        S.dma("sp", vs[:, NT + 2:NT + 4, :], Sc["Vsc"].rearrange("(kt p) d -> p kt d", p=128), writes=["vs"])
        if s == "L":
            ksha = Sc["KsHa"].rearrange("(r p) (g f t) -> r p g f t", r=2, g=2, f=2)
            S.dma("sp", kst[:, :, 0:128], ksha[0, :, :, 1, :], writes=["kst"])
            S.dma("sp", kst[:, :, 128:128 + T], Sc["KsTo"], writes=["kst"])
            S.dma("sp", kst[:, :, 128 + T:256 + T], ksha[1, :, :, 0, :], writes=["kst"])
            S.dma("sp", vs[:, 0, :], Sc["VsHa"][128:256, :], writes=["vs"])
            S.dma("sp", vs[:, 1:NT + 1, :], Sc["Vso"].rearrange("(kt p) d -> p kt d", p=128), writes=["vs"])
            S.dma("sp", vs[:, NT + 1, :], Sc["VsHa"][256:384, :], writes=["vs"])
        S.dma("sp", mk[:], I["masks"].rearrange("m j a -> j m a"), writes=["mk"])
        S.dma("sp", esk[:], dram_bcast(I["swa_sink"][l], 64), writes=["esk"])
        S.op("act", lambda e: e.activation(out=esk[:], in_=esk[:], func=AF.Exp), ["esk"], ["esk"])
        pcnt = 0
        for qi in range(nt):
            b = qi % 2
            S.dma("sp", qb_[:, b], Sc["QsT" + s][:, :, qi * 128:(qi + 1) * 128], writes=["qb%d" % b])
            kts = []
            if s == "L":
                kts.append((qi, 2 if qi == 0 else 0))
                kts.append((qi + 1, None))
                kts.append((qi + 2, 3 if qi == nt - 1 else 1))
            kts += [(k, None) for k in ctx_kts]
            for gk in range(2):
                ob = gk
                for i, (kt, m) in enumerate(kts):
                    sl = pcnt % 3
                    sp_ = pcnt % 2
                    pcnt += 1
                    S.op("pe", lambda e, sp_=sp_, gk=gk, kt=kt, b=b: e.matmul(
                        spp[sp_][:], kst[:, gk, kt * 128:(kt + 1) * 128], qb_[:, b, 4 * gk:4 * gk + 4, :], start=True, stop=True),
                        ["kst", "qb%d" % b], ["spp%d" % sp_])
                    S.op("act", lambda e, sp_=sp_, sl=sl: e.activation(out=pt[:, sl], in_=spp[sp_][:], func=AF.Exp, scale=0.125),
                         ["spp%d" % sp_], ["pt%d" % sl])
                    if m is not None:
                        pv = pt[:, sl].rearrange("p (h q) -> p h q", h=4)
                        S.op("dve", lambda e, pv=pv, m=m: e.tensor_tensor(out=pv, in0=pv, in1=mk[:, m].unsqueeze(1).to_broadcast([128, 4, 128]),
                                                                          op=ALU.mult), ["pt%d" % sl, "mk"], ["pt%d" % sl])
                    S.op("pe", lambda e, ob=ob, kt=kt, gk=gk, sl=sl, i=i, nk=len(kts): e.matmul(
                        opp[ob][:], vs[:, kt, gk * 64:(gk + 1) * 64], pt[:, sl], start=(i == 0), stop=(i == nk - 1)),
                        ["vs", "pt%d" % sl], ["opp%d" % ob])
                    S.op("pe", lambda e, ob=ob, sl=sl, i=i, nk=len(kts): e.matmul(
                        rpp[ob][:], g.ones[:, 0:64], pt[:, sl], start=(i == 0), stop=(i == nk - 1)),
                        ["ones", "pt%d" % sl], ["rpp%d" % ob])
                dv = den[:].rearrange("p (h q) -> p h q", h=4)
                S.op("dve", lambda e, ob=ob, gk=gk, dv=dv: e.tensor_tensor(
                    out=dv, in0=rpp[ob][:].rearrange("p (h q) -> p h q", h=4),
                    in1=esk[:, 4 * gk:4 * gk + 4].unsqueeze(2).to_broadcast([64, 4, 128]), op=ALU.add),
                    ["rpp%d" % ob, "esk"], ["den"])
                S.op("dve", lambda e: e.reciprocal(out=den[:], in_=den[:]), ["den"], ["den"])
                S.op("dve", lambda e, ob=ob, gk=gk: e.tensor_tensor(out=yo[:, gk].rearrange("p h q -> p (h q)"), in0=opp[ob][:], in1=den[:], op=ALU.mult),
                     ["opp%d" % ob, "den"], ["yo%d" % gk])
                for jj in range(2):
                    j = 2 * gk + jj
                    dst = Sc["ysT" + s][j].rearrange("(hh d) t -> d hh t", hh=2)[:, :, qi * 128:(qi + 1) * 128]
                    S.dma("sp", dst, yo[:, gk, 2 * jj:2 * jj + 2, :], reads=["yo%d" % gk])
        S.flush()


def phase_diff(g, l, s, n):
    nc, S, I, Sc = g.nc, g.S, g.I, g.Sc
    T = g.T
    lam_init = 0.8 - 0.6 * math.exp(-0.3 * l)
    nk = (2 * T + CTX) if s == "L" else CTX
    nkt = nk // 128
    GW = min(512, n)
    ng = n // GW
    S.skip_pe = False
    import contextlib
    with contextlib.ExitStack() as st:
        sb, ps = mk_alloc(g, st)
        kt_ = sb("dk", [128, 2, nk], BF16)
        vt = sb("dv", [128, 2, nkt, 128], BF16)
        qt = sb("dq", [128, 2, n], BF16)
        dl = sb("dl", [128, 256])
        pr = sb("pr", [128, 2, 64])
        ee = sb("ee", [128, 2])
        nlam = sb("nlam", [128, 1])
        sg = sb("sg", [128, 1])
        hi = sb("hi", [128, GW], BF16)
        lo = sb("lo", [128, GW], BF16)
        pt = sb("pt", [128, 3, 2, GW], BF16)
        acc = sb("acc", [128, 2, GW])
        ra = sb("ra", [128, GW])
        oa = sb("oa", [128, GW])
        ob = sb("ob", [128, GW])
        sq = sb("sq", [128, GW])
        yb = sb("yb", [128, 2, GW], BF16)
        sps = [ps("sps%d" % i, [128, 2, GW]) for i in range(2)]
        ops_ = [ps("ops%d" % i, [128, GW]) for i in range(2)]
        rps = ps("rps", [128, GW])
        S.dma("sp", dl[:], dram_bcast(I["diff_lambda"][l], 128), writes=["dl"])
        dl4 = dl[:].rearrange("p (a b d) -> p a b d", a=2, b=2)
        S.op("dve", lambda e: e.tensor_tensor(out=pr[:], in0=dl4[:, :, 0, :], in1=dl4[:, :, 1, :], op=ALU.mult), ["dl"], ["pr"])
        S.op("dve", lambda e: e.tensor_reduce(out=ee[:], in_=pr[:], axis=AX.X, op=ALU.add), ["pr"], ["ee"])
        S.op("act", lambda e: e.activation(out=ee[:], in_=ee[:], func=AF.Exp), ["ee"], ["ee"])
        S.op("dve", lambda e: e.tensor_tensor(out=nlam[:], in0=ee[:, 1:2], in1=ee[:, 0:1], op=ALU.subtract), ["ee"], ["nlam"])
        S.op("dve", lambda e: e.tensor_scalar(out=nlam[:], in0=nlam[:], scalar1=-lam_init, scalar2=None, op0=ALU.add), ["nlam"], ["nlam"])
        S.dma("sp", sg[:], I["diff_subln"][l].rearrange("(p o) -> p o", o=1), writes=["sg"])
        S.op("dve", lambda e: e.tensor_scalar(out=sg[:], in0=sg[:], scalar1=(1.0 - lam_init), scalar2=None, op0=ALU.mult), ["sg"], ["sg"])

        def load_head(h):
            hb = h % 2
            c0 = nk - CTX
            if s == "L":
                for r_ in range(2):
                    for j in range(g.NCH):
                        kda = Sc["KdTa%d" % j].rearrange("(r h p) t -> r h p t", r=2, h=4)
                        k_off = r_ * T + j * 256
                        S.dma("sp", kt_[:, hb, k_off:k_off + 256], kda[r_, h], writes=["dk%d" % hb])
                        S.dma("sp", vt[:, hb, k_off // 128:k_off // 128 + 2, :],
                              Sc["Vda%d" % j][r_ * 256:(r_ + 1) * 256, h * 128:(h + 1) * 128].rearrange("(kt p) d -> p kt d", p=128),
                              writes=["dv%d" % hb])
            S.dma("sp", kt_[:, hb, c0:nk], Sc["KdTc"][h], writes=["dk%d" % hb])
            S.dma("sp", vt[:, hb, c0 // 128:nkt, :], Sc["Vdc"][:, h * 128:(h + 1) * 128].rearrange("(kt p) d -> p kt d", p=128),
                  writes=["dv%d" % hb])
            S.dma("sp", qt[:, hb], Sc["QdT" + s][h], writes=["dq%d" % hb])
        load_head(0)
        cnt = 0
        for h in range(4):
            hb = h % 2
            if h + 1 < 4:
                load_head(h + 1)
            for gq in range(ng):
                qs_ = slice(gq * GW, (gq + 1) * GW)

                def scores(kt, c, hb=hb, qs_=qs_):
                    pb = c % 2
                    for hf in range(2):
                        S.op("pe", lambda e, pb=pb, hf=hf, kt=kt, hb=hb, qs_=qs_: e.matmul(
                            sps[pb][:, hf, :], kt_[hf * 64:(hf + 1) * 64, hb, kt * 128:(kt + 1) * 128], qt[hf * 64:(hf + 1) * 64, hb, qs_],
                            start=True, stop=True), ["dk%d" % hb, "dq%d" % hb], ["sps%d" % pb])
                scores(0, cnt)
                for kt in range(nkt):
                    c = cnt + kt
                    if kt + 1 < nkt:
                        scores(kt + 1, c + 1)
                    pb = c % 2
                    sl = c % 3
                    for hf in range(2):
                        S.op("act", lambda e, pb=pb, sl=sl, hf=hf: e.activation(out=pt[:, sl, hf, :], in_=sps[pb][:, hf, :], func=AF.Exp, scale=0.125),
                             ["sps%d" % pb], ["pt%d" % sl])
                    for hf in range(2):
                        S.op("pe", lambda e, hf=hf, sl=sl, kt=kt, hb=hb: e.matmul(ops_[hf][:], vt[:, hb, kt, :], pt[:, sl, hf, :],
                                                                               start=(kt == 0), stop=(kt == nkt - 1)),
                             ["dv%d" % hb, "pt%d" % sl], ["ops%d" % hf])
                    for hf, en in ((0, "dve"), (1, "pool")):
                        if kt == 0:
                            S.op(en, lambda e, hf=hf, sl=sl: e.tensor_copy(out=acc[:, hf], in_=pt[:, sl, hf, :]), ["pt%d" % sl], ["acc%d" % hf])
                        else:
                            S.op(en, lambda e, hf=hf, sl=sl: e.tensor_tensor(out=acc[:, hf], in0=acc[:, hf], in1=pt[:, sl, hf, :], op=ALU.add),
                                 ["pt%d" % sl, "acc%d" % hf], ["acc%d" % hf])
                cnt += nkt
                def colsum(src_ap, src_key):
                    S.op("dve", lambda e: e.tensor_copy(out=hi[:], in_=src_ap), [src_key], ["hi"])
                    S.op("pool", lambda e: e.tensor_tensor(out=sq[:], in0=src_ap, in1=hi[:], op=ALU.subtract), [src_key, "hi"], ["sq"])
                    S.op("pool", lambda e: e.tensor_copy(out=lo[:], in_=sq[:]), ["sq"], ["lo"])
                    S.op("pe", lambda e: e.matmul(rps[:], g.ones[:], hi[:], start=True, stop=False), ["ones", "hi"], ["rps"])
                    S.op("pe", lambda e: e.matmul(rps[:], g.ones[:], lo[:], start=False, stop=True), ["ones", "lo"], ["rps"])
                for hf, dst in ((0, oa), (1, ob)):
                    colsum(acc[:, hf], "acc%d" % hf)
                    S.op("dve", lambda e: e.reciprocal(out=ra[:], in_=rps[:]), ["rps"], ["ra"])
                    S.op("dve", lambda e, hf=hf, dst=dst: e.tensor_tensor(out=dst[:], in0=ops_[hf][:], in1=ra[:], op=ALU.mult),
                         ["ops%d" % hf, "ra"], ["o%d" % hf])
                S.op("dve", lambda e: e.scalar_tensor_tensor(out=oa[:], in0=ob[:], scalar=nlam[:, 0:1], in1=oa[:], op0=ALU.mult, op1=ALU.add),
                     ["o0", "o1", "nlam"], ["o0"])
                S.op("pool", lambda e: e.tensor_tensor(out=ob[:], in0=oa[:], in1=oa[:], op=ALU.mult), ["o0"], ["o1"])
                colsum(ob[:], "o1")
                S.op("act", lambda e: e.activation(out=ra[:], in_=rps[:], func=AF.Sqrt, bias=EPS, scale=1.0 / 128), ["rps"], ["ra"])
                S.op("dve", lambda e: e.reciprocal(out=ra[:], in_=ra[:]), ["ra"], ["ra"])
                yb_ = gq % 2
                S.op("dve", lambda e, yb_=yb_: e.scalar_tensor_tensor(out=yb[:, yb_], in0=oa[:], scalar=sg[:, 0:1], in1=ra[:], op0=ALU.mult, op1=ALU.mult),
                     ["o0", "sg", "ra"], ["yb%d" % yb_])
                S.dma("sp", Sc["ydT" + s][h][:, qs_], yb[:, yb_], reads=["yb%d" % yb_])
        S.flush()
    S.skip_pe = True


def phase_merge(g, l, s, n, xin):
    nc, S, I, Sc = g.nc, g.S, g.I, g.Sc
    r = 0 if s == "L" else 1
    GW = min(512, n)
    ng = n // GW
    import contextlib
    with contextlib.ExitStack() as st:
        sb, ps = mk_alloc(g, st)
        wg = sb("wg", [128, 8, 3072], BF16)
        wb = sb("wb", [128, 3, 4, D], BF16)
        wo = sb("wo", [128, 8, D], BF16)
        g1 = sb("g1", [128, D])
        hTg = sb("hTg", [128, 2, 8, GW], BF16)
        ybr = sb("ybr", [128, 2, 3, 4, GW], BF16)
        sig = sb("sig", [128, 2, 3, GW])
        m0 = sb("m0", [128, GW])
        m1 = sb("m1", [128, GW])
        m2 = sb("m2", [128, GW])
        mT = sb("mT", [128, 8, GW], BF16)
        xt = sb("xt", [128, 2, D])
        tmp = sb("tmp", [128, D])
        gp = [ps("gp%d" % i, [128, GW]) for i in range(3)]
        bp = [ps("bp%d" % i, [128, GW]) for i in range(3)]
        ao = [ps("ao%d" % i, [128, 512]) for i in range(2)]
        for kh in range(4):
            for cb in range(3):
                S.dma("pool", wg[:, 2 * kh:2 * kh + 2, cb * 1024:(cb + 1) * 1024],
                      I["w_in"][l, kh * 256:(kh + 1) * 256, OFF_G + cb * 1024:OFF_G + (cb + 1) * 1024].rearrange("(kc p) n -> p kc n", p=128),
                      writes=["wg"])
        for bi, nm in enumerate(("w_br_pool", "w_br_swa", "w_br_diff")):
            S.dma("pool", wb[:, bi], I[nm][l].rearrange("(kc p) n -> p kc n", p=128), writes=["wb"])
        for kh in range(2):
            S.dma("pool", wo[:, 4 * kh:4 * kh + 4, :], I["w_out"][l, kh * 512:(kh + 1) * 512, :].rearrange("(kc p) n -> p kc n", p=128), writes=["wo"])
        load_bc(S, "sp", g1[:], Sc["mod"][l, r, 2 * D:3 * D], "g1")
        xcnt = 0
        for gi in range(ng):
            b = gi % 2
            cs_ = slice(gi * GW, (gi + 1) * GW)
            S.dma("sp", hTg[:, b], Sc["hT" + s][:, :, cs_].rearrange("k p t -> p k t"), writes=["hTg%d" % b])
            for bi, nm in enumerate(("ypT", "ysT", "ydT")):
                S.dma("sp", ybr[:, b, bi], Sc[nm + s][:, :, cs_].rearrange("c p t -> p c t"), writes=["ybr%d" % b])
            for j in range(8):
                sb_ = j % 2
                for bi in range(3):
                    for kc in range(8):
                        S.op("pe", lambda e, bi=bi, kc=kc, j=j, b=b: e.matmul(
                            gp[bi][:], wg[:, kc, bi * 1024 + j * 128:bi * 1024 + (j + 1) * 128], hTg[:, b, kc, :], start=(kc == 0), stop=(kc == 7)),
                            ["wg", "hTg%d" % b], ["gp%d" % bi])
                    for kc in range(4):
                        S.op("pe", lambda e, bi=bi, kc=kc, j=j, b=b: e.matmul(
                            bp[bi][:], wb[:, bi, kc, j * 128:(j + 1) * 128], ybr[:, b, bi, kc, :], start=(kc == 0), stop=(kc == 3)),
                            ["wb", "ybr%d" % b], ["bp%d" % bi])
                for bi in range(3):
                    S.op("act", lambda e, bi=bi, sb_=sb_: e.activation(out=sig[:, sb_, bi], in_=gp[bi][:], func=AF.Sigmoid),
                         ["gp%d" % bi], ["sig%d_%d" % (sb_, bi)])
                for bi, mm_ in enumerate((m0, m1, m2)):
                    S.op("dve", lambda e, bi=bi, mm_=mm_, sb_=sb_: e.tensor_tensor(out=mm_[:], in0=bp[bi][:], in1=sig[:, sb_, bi], op=ALU.mult),
                         ["bp%d" % bi, "sig%d_%d" % (sb_, bi)], ["m%d" % bi])
                S.op("pool", lambda e: e.tensor_tensor(out=m0[:], in0=m0[:], in1=m1[:], op=ALU.add), ["m0", "m1"], ["m0"])
                S.op("pool", lambda e, j=j: e.tensor_tensor(out=mT[:, j, :], in0=m0[:], in1=m2[:], op=ALU.add), ["m0", "m2"], ["mT"])
            for tt in range(GW // 128):
                xb = xcnt % 2
                xcnt += 1
                row0 = gi * GW + tt * 128
                S.dma("sp", xt[:, xb], xin[row0:row0 + 128, :], writes=["xt%d" % xb])
                for nn in range(2):
                    for kc in range(8):
                        S.op("pe", lambda e, nn=nn, kc=kc, tt=tt: e.matmul(
                            ao[nn][:], mT[:, kc, tt * 128:(tt + 1) * 128], wo[:, kc, nn * 512:(nn + 1) * 512], start=(kc == 0), stop=(kc == 7)),
                            ["mT", "wo"], ["ao%d" % nn])
                for nn in range(2):
                    S.op("dve", lambda e, nn=nn: e.tensor_tensor(out=tmp[:, nn * 512:(nn + 1) * 512], in0=ao[nn][:], in1=g1[:, nn * 512:(nn + 1) * 512], op=ALU.mult),
                         ["ao%d" % nn, "g1"], ["tmp"])
                S.op("pool", lambda e, xb=xb: e.tensor_tensor(out=xt[:, xb], in0=xt[:, xb], in1=tmp[:], op=ALU.add), ["tmp", "xt%d" % xb], ["xt%d" % xb])
                S.dma("sp", Sc["x1" + s][row0:row0 + 128, :], xt[:, xb], reads=["xt%d" % xb])
        S.flush()


def phase_ffn(g, l, s, n, xout):
    nc, S, I, Sc = g.nc, g.S, g.I, g.Sc
    r = 0 if s == "L" else 1
    GW = 256
    ng = n // GW
    import contextlib
    with contextlib.ExitStack() as st:
        sb, ps = mk_alloc(g, st)
        w1 = sb("w1", [128, 8, DFF], BF16)
        w2 = sb("w2", [128, 32, D], BF16)
        A2 = sb("A2", [128, D])
        B2 = sb("B2", [128, D])
        g2 = sb("g2", [128, D])
        tmp = sb("tmp", [128, D])
        xt = sb("xt", [128, 4, D])
        ssx = sb("ssx", [128, 4])
        hb = sb("hb", [128, 2, D], BF16)
        h2T = sb("h2T", [128, 2, 8, GW], BF16)
        aT = sb("aT", [128, 32, GW], BF16)
        r32 = sb("r32", [128, 2, GW])
        tr = ps("tr", [128, 8, 128], BF16)
        fp = [ps("fp%d" % i, [128, GW]) for i in range(2)]
        yp = [ps("yp%d" % i, [128, 512]) for i in range(4)]
        for kc in range(8):
            for cb in range(2):
                S.dma("pool", w1[:, kc, cb * 2048:(cb + 1) * 2048], I["w_ff1"][l, kc * 128:(kc + 1) * 128, cb * 2048:(cb + 1) * 2048], writes=["w1"])
        for fh in range(8):
            S.dma("pool", w2[:, 4 * fh:4 * fh + 4, :], I["w_ff2"][l, fh * 512:(fh + 1) * 512, :].rearrange("(kc p) n -> p kc n", p=128), writes=["w2"])
        load_bc(S, "sp", B2[:], Sc["mod"][l, r, 3 * D:4 * D], "B2")
        load_bc(S, "sp", A2[:], Sc["mod"][l, r, 4 * D:5 * D], "A2")
        load_bc(S, "sp", g2[:], Sc["mod"][l, r, 5 * D:6 * D], "g2")
        load_bc(S, "sp", tmp[:], I["norm2"][l], "tmp")
        S.op("dve", lambda e: e.scalar_tensor_tensor(out=A2[:], in0=A2[:], scalar=1.0, in1=tmp[:], op0=ALU.add, op1=ALU.mult), ["A2", "tmp"], ["A2"])
        tcnt = 0
        for gi in range(ng):
            gb = gi % 2
            for tt in range(2):
                xs = 2 * gb + tt
                hb_ = tcnt % 2
                tcnt += 1
                row0 = gi * GW + tt * 128
                S.dma("sp", xt[:, xs], Sc["x1" + s][row0:row0 + 128, :], writes=["xt%d" % xs])
                S.op("act", lambda e, xs=xs: e.activation(out=tmp[:], in_=xt[:, xs], func=AF.Square, accum_out=ssx[:, xs:xs + 1]),
                     ["xt%d" % xs], ["tmp", "ssx%d" % xs])
                S.op("act", lambda e, xs=xs: e.activation(out=ssx[:, xs:xs + 1], in_=ssx[:, xs:xs + 1], func=AF.Sqrt, bias=EPS, scale=1.0 / D),
                     ["ssx%d" % xs], ["ssx%d" % xs])
                S.op("dve", lambda e, xs=xs: e.reciprocal(out=ssx[:, xs:xs + 1], in_=ssx[:, xs:xs + 1]), ["ssx%d" % xs], ["ssx%d" % xs])
                S.op("dve", lambda e, xs=xs: e.scalar_tensor_tensor(out=tmp[:], in0=xt[:, xs], scalar=ssx[:, xs:xs + 1], in1=A2[:], op0=ALU.mult, op1=ALU.mult),
                     ["xt%d" % xs, "ssx%d" % xs, "A2"], ["tmp"])
                S.op("pool", lambda e, hb_=hb_: e.tensor_tensor(out=hb[:, hb_], in0=tmp[:], in1=B2[:], op=ALU.add), ["tmp", "B2"], ["hb%d" % hb_])
                for kc in range(8):
                    S.op("pe", lambda e, hb_=hb_, kc=kc: e.transpose(tr[:, kc, :], hb[:, hb_, kc * 128:(kc + 1) * 128], g.ident[:]),
                         ["hb%d" % hb_, "ident"], ["tr"])
                S.op("act", lambda e, gb=gb, tt=tt: e.copy(out=h2T[:, gb, :, tt * 128:(tt + 1) * 128], in_=tr[:]), ["tr"], ["h2T%d" % gb])
            for fc in range(32):
                fb = fc % 2
                for kc in range(8):
                    S.op("pe", lambda e, fb=fb, kc=kc, fc=fc, gb=gb: e.matmul(fp[fb][:], w1[:, kc, fc * 128:(fc + 1) * 128], h2T[:, gb, kc, :],
                                                                          start=(kc == 0), stop=(kc == 7)),
                         ["w1", "h2T%d" % gb], ["fp%d" % fb])
                S.op("act", lambda e, fb=fb: e.activation(out=r32[:, fb], in_=fp[fb][:], func=AF.Relu), ["fp%d" % fb], ["r32_%d" % fb])
                S.op("pool" if fc % 2 else "dve", lambda e, fb=fb, fc=fc: e.tensor_tensor(out=aT[:, fc, :], in0=r32[:, fb], in1=r32[:, fb], op=ALU.mult),
                     ["r32_%d" % fb], ["aT"])
            for tt in range(2):
                xs = 2 * gb + tt
                for nn in range(2):
                    yb_ = 2 * tt + nn
                    for fc in range(32):
                        S.op("pe", lambda e, yb_=yb_, fc=fc, tt=tt, nn=nn: e.matmul(yp[yb_][:], aT[:, fc, tt * 128:(tt + 1) * 128], w2[:, fc, nn * 512:(nn + 1) * 512],
                                                                               start=(fc == 0), stop=(fc == 31)),
                             ["aT", "w2"], ["yp%d" % yb_])
                for nn in range(2):
                    yb_ = 2 * tt + nn
                    S.op("dve", lambda e, yb_=yb_, nn=nn: e.tensor_tensor(out=tmp[:, nn * 512:(nn + 1) * 512], in0=yp[yb_][:], in1=g2[:, nn * 512:(nn + 1) * 512], op=ALU.mult),
                         ["yp%d" % yb_, "g2"], ["tmp"])
                row0 = gi * GW + tt * 128
                S.op("pool", lambda e, xs=xs: e.tensor_tensor(out=xt[:, xs], in0=xt[:, xs], in1=tmp[:], op=ALU.add), ["tmp", "xt%d" % xs], ["xt%d" % xs])
                S.dma("sp", xout[row0:row0 + 128, :], xt[:, xs], reads=["xt%d" % xs])
        S.flush()


def _band_mats(n_full, g0, gl):
    out = np.zeros((4, 7, 128, 128), np.float32)
    tl = np.arange(128)
    ntot = n_full // 128

    def mat(w, tile, src_tile):
        t = tile * 128 + tl
        lo = np.clip(t - w // 2, 0, n_full)
        hi = np.clip(t + w // 2, 0, n_full)
        cnt = (hi - lo).astype(np.float32)
        sg = src_tile * 128 + tl
        m = ((sg[:, None] >= lo[None, :]) & (sg[:, None] < hi[None, :])).astype(np.float32) / cnt[None, :]
        if tile == src_tile:
            m = m - np.eye(128, dtype=np.float32)
        return m
    for wi, w in enumerate(POOL_WINDOWS):
        if ntot > 2:
            out[wi, 0] = mat(w, 1, 0)
            out[wi, 1] = mat(w, 1, 1)
            out[wi, 2] = mat(w, 1, 2)
        else:
            out[wi, 0] = mat(w, 1, 0)
            out[wi, 2] = mat(w, 0, 1)
        if g0 > 0:
            out[wi, 3] = mat(w, g0, g0 - 1)
        out[wi, 4] = mat(w, g0, g0)
        out[wi, 5] = mat(w, gl, gl)
        if gl + 1 < ntot:
            out[wi, 6] = mat(w, gl, gl + 1)
    return out.astype(ml_dtypes.bfloat16)


def make_consts(T_full, hf):
    T = T_full // 2
    rows = T_full // 64
    row = np.repeat(np.arange(rows, dtype=np.float32), 64)
    col = np.tile(np.arange(64, dtype=np.float32), rows)
    inv = (10000.0 ** (-np.arange(16, dtype=np.float32) / 16)).astype(np.float32)
    ang = np.concatenate([row[:, None] * inv, col[:, None] * inv], axis=-1).astype(np.float32)[hf * T:(hf + 1) * T]
    j = np.arange(128)[:, None]
    a = np.arange(128)[None, :]
    mp = (j >= a).astype(np.float32)
    mn = (j <= a).astype(np.float32)
    masks = np.stack([mp, mn, mp * (1.0 if hf == 1 else 0.0), mn * (1.0 if hf == 0 else 0.0)], 0)
    NT = T // 128
    return {
        "cos": np.ascontiguousarray(np.cos(ang).astype(np.float32)), "sin": np.ascontiguousarray(np.sin(ang).astype(np.float32)),
        "cosc": np.ones((CTX, 32), np.float32), "sinc": np.zeros((CTX, 32), np.float32),
        "ident": np.eye(128, dtype=np.float32).astype(ml_dtypes.bfloat16),
        "ones": np.ones((128, 128), np.float32).astype(ml_dtypes.bfloat16),
        "band": _band_mats(T_full, hf * NT, hf * NT + NT - 1), "bandc": _band_mats(CTX, 0, CTX // 128 - 1),
        "masks": masks.astype(ml_dtypes.bfloat16),
    }


def core_inputs(inp, b, hf, consts):
    f = lambda a: np.ascontiguousarray(np.asarray(a, dtype=np.float32))
    m = dict(consts)
    T = inp["x"].shape[1] // 2
    m["x"] = f(inp["x"][b][hf * T:(hf + 1) * T])
    m["ctx"] = f(inp["ctx"][b])
    cv = np.stack([f(inp["c"][b]), f(inp["c_ctx"])], 0)
    m["cvec"] = np.ascontiguousarray(cv.reshape(2, 8, 128).transpose(2, 0, 1).reshape(128, 16))
    for k in ("w_ada", "b_ada", "norm1", "norm2", "w_in", "w_pool", "pool_scale", "swa_q_norm", "swa_k_norm",
              "swa_sink", "diff_q_norm", "diff_k_norm", "diff_subln", "w_br_pool", "w_br_swa", "w_br_diff",
              "w_out", "w_ff1", "w_ff2"):
        m[k] = f(inp[k])
    m["diff_lambda"] = f(inp["diff_lambda"]).reshape(DEPTH, 256)
    return m


_NC_CACHE = {}


def kernel(**inputs):
    B, TF, _ = inputs["x"].shape
    T = TF // 2
    if T not in _NC_CACHE:
        _NC_CACHE[T] = build_program(T)
    nc = _NC_CACHE[T]
    consts = [make_consts(TF, hf) for hf in range(2)]
    n_cores = 2 * B
    in_maps = [core_inputs(inputs, c // 2, c % 2, consts[c % 2]) for c in range(n_cores)]
    res = run_bass_kernel_spmd(nc, in_maps, core_ids=list(range(n_cores)))
    out = np.empty((B, TF, D), np.float32)
    for c in range(n_cores):
        out[c // 2, (c % 2) * T:(c % 2 + 1) * T] = np.asarray(res.results[c]["y"], dtype=np.float32)
    return out
```

```python
import math
import numpy as np
import ml_dtypes
import concourse.bass as bass
import concourse.mybir as mybir
from concourse.bass_utils import run_bass_kernel_spmd

F32 = mybir.dt.float32
BF16 = mybir.dt.bfloat16
AF = mybir.ActivationFunctionType
ALU = mybir.AluOpType
AX = mybir.AxisListType

D = 1024
DEPTH = 2
CTX = 256
HD = 64
EPS = 1e-6
DFF = 4096
POOL_WINDOWS = (2, 4, 8, 16)
OFF_U, OFF_QS, OFF_KS, OFF_VS, OFF_QD, OFF_KD, OFF_VD, OFF_G = 0, 512, 1024, 1152, 1280, 1792, 2304, 2816
P1_BLOCKS = [(OFF_QS, 512), (OFF_KS, 128), (OFF_QD, 512), (OFF_KD, 512), (OFF_VS, 128), (OFF_VD, 512), (OFF_U, 512)]
NH = 26
C_VS, C_VD, C_U = 1664, 1792, 2304
P1_COLS = 2816


SKIP_SAME_ENGINE = True


class Sched:
    COMPUTE = ("pe", "act", "dve", "pool")

    def __init__(self, nc, stack):
        self.nc = nc
        self.eng = {"pe": nc.tensor, "act": nc.scalar, "dve": nc.vector, "pool": nc.gpsimd, "sp": nc.sync}
        self.tsem = {e: stack.enter_context(nc.semaphore("tl_" + e)) for e in self.COMPUTE}
        self.tick = {e: 0 for e in self.COMPUTE}
        self.nds = {"sp": 8, "pool": 6, "act": 2}
        self.dsem = {q: [stack.enter_context(nc.semaphore("d_%s%d" % (q, i))) for i in range(n)]
                     for q, n in self.nds.items()}
        self.dcnt = {q: 0 for q in self.nds}
        self.ccsem = stack.enter_context(nc.semaphore("ccsem"))
        self.cccnt = 0
        self.ops = []
        self.last_w = {}
        self.readers = {}
        self.waited = {e: {} for e in self.eng}
        self.sems = {}
        self.n_inst = 0
        self.skip_pe = True

    def op(self, eng, fn, reads=(), writes=(), kind="c"):
        idx = len(self.ops)
        deps = {}
        for k in reads:
            if k in self.last_w:
                deps[self.last_w[k]] = True
        for k in writes:
            if k in self.last_w:
                deps.setdefault(self.last_w[k], False)
            for rd in self.readers.get(k, ()):
                deps.setdefault(rd, False)
        if kind == "c" and SKIP_SAME_ENGINE and self.skip_pe:
            for d in list(deps):
                od = self.ops[d]
                if od["kind"] == "c" and od["eng"] == eng and (eng == "pe" or not deps[d]):
                    del deps[d]
        for k in reads:
            self.readers.setdefault(k, []).append(idx)
        for k in writes:
            self.last_w[k] = idx
            self.readers[k] = []
        self.ops.append({"eng": eng, "fn": fn, "deps": deps, "kind": kind, "sig": kind != "c", "signal": None})
        return idx

    def dma(self, q, out, in_, reads=(), writes=()):
        return self.op(q, lambda e: e.dma_start(out=out, in_=in_), reads, writes, kind="d")

    def _wait(self, e, sem, val):
        w = self.waited[e]
        key = id(sem)
        if w.get(key, 0) < val:
            self.eng[e].wait_ge(sem, val)
            w[key] = val
            self.sems[key] = sem

    def flush(self):
        ops = self.ops
        for o in ops:
            for d in o["deps"]:
                ops[d]["sig"] = True
        last = {}
        for i, o in enumerate(ops):
            if o["kind"] == "c":
                last[o["eng"]] = i
        for i in last.values():
            ops[i]["sig"] = True
        final = {}
        for o in ops:
            e = o["eng"]
            for d in sorted(o["deps"]):
                sem, val = ops[d]["signal"]
                self._wait(e, sem, val)
            if o["kind"] == "d":
                j = self.dcnt[e]
                self.dcnt[e] += 1
                sem = self.dsem[e][j % self.nds[e]]
                val = 16 * (j // self.nds[e] + 1)
                self._wait(e, sem, val - 16)
                o["fn"](self.eng[e]).then_inc(sem, 16)
                o["signal"] = (sem, val)
                final[id(sem)] = (sem, val)
            elif o["kind"] == "cc":
                self.cccnt += 1
                o["fn"](self.eng[e]).then_inc(self.ccsem)
                o["signal"] = (self.ccsem, self.cccnt)
                final[id(self.ccsem)] = (self.ccsem, self.cccnt)
            else:
                ins = o["fn"](self.eng[e])
                if o["sig"]:
                    self.tick[e] += 1
                    ins.then_inc(self.tsem[e], 1)
                    o["signal"] = (self.tsem[e], self.tick[e])
                    final[id(self.tsem[e])] = (self.tsem[e], self.tick[e])
            self.n_inst += 1
        for e in self.eng:
            for sem, val in final.values():
                self._wait(e, sem, val)
        self.ops = []
        self.last_w = {}
        self.readers = {}


def dram_bcast(ap_1d, parts, mid=None):
    n = ap_1d.shape[-1]
    dims = [[0, parts]]
    if mid is not None:
        dims.append([0, mid])
    dims.append([1, n])
    return bass.AP(ap_1d.tensor, ap_1d.offset, dims)


class Ctx:
    pass


_UID = [0]


def mk_alloc(g, st):
    nc = g.nc
    _UID[0] += 1
    u = "_%d" % _UID[0]

    def sb(name, shape, dt=F32):
        return st.enter_context(nc.sbuf_tensor(name + u, list(shape), dt))

    def ps(name, shape, dt=F32):
        return st.enter_context(nc.psum_tensor(name + u, list(shape), dt))
    return sb, ps


def build_program(T, debug_outs=(), depth=DEPTH, phases=None):
    assert T % 256 == 0
    NT = T // 128
    NTC = CTX // 128
    TK = T + CTX
    NKT = TK // 128
    nc = bass.Bass("TRN2", target_bir_lowering=False)
    g = Ctx()
    g.nc = nc
    g.T, g.NT, g.NTC, g.TK, g.NKT = T, NT, NTC, TK, NKT

    def din(name, shape, dt=F32):
        return nc.dram_tensor(name, list(shape), dt, kind="ExternalInput").ap()

    def dscr(name, shape, dt=BF16):
        kind = "ExternalOutput" if name in debug_outs else "Internal"
        return nc.dram_tensor(name, list(shape), dt, kind=kind).ap()

    I = {}
    I["x"] = din("x", [T, D])
    I["ctx"] = din("ctx", [CTX, D])
    I["cvec"] = din("cvec", [128, 16])
    I["w_ada"] = din("w_ada", [DEPTH, D, 6 * D])
    I["b_ada"] = din("b_ada", [DEPTH, 6 * D])
    I["norm1"] = din("norm1", [DEPTH, D])
    I["norm2"] = din("norm2", [DEPTH, D])
    I["w_in"] = din("w_in", [DEPTH, D, 5888])
    I["w_pool"] = din("w_pool", [DEPTH, 4, 128, 128])
    I["pool_scale"] = din("pool_scale", [DEPTH, 512])
    I["swa_q_norm"] = din("swa_q_norm", [DEPTH, 64])
    I["swa_k_norm"] = din("swa_k_norm", [DEPTH, 64])
    I["swa_sink"] = din("swa_sink", [DEPTH, 8])
    I["diff_q_norm"] = din("diff_q_norm", [DEPTH, 64])
    I["diff_k_norm"] = din("diff_k_norm", [DEPTH, 64])
    I["diff_lambda"] = din("diff_lambda", [DEPTH, 256])
    I["diff_subln"] = din("diff_subln", [DEPTH, 128])
    I["w_br_pool"] = din("w_br_pool", [DEPTH, 512, D])
    I["w_br_swa"] = din("w_br_swa", [DEPTH, 512, D])
    I["w_br_diff"] = din("w_br_diff", [DEPTH, 512, D])
    I["w_out"] = din("w_out", [DEPTH, D, D])
    I["w_ff1"] = din("w_ff1", [DEPTH, D, DFF])
    I["w_ff2"] = din("w_ff2", [DEPTH, DFF, D])
    I["cos"] = din("cos", [T, 32])
    I["sin"] = din("sin", [T, 32])
    I["cosc"] = din("cosc", [CTX, 32])
    I["sinc"] = din("sinc", [CTX, 32])
    I["ident"] = din("ident", [128, 128], BF16)
    I["ones"] = din("ones", [128, 128], BF16)
    I["band"] = din("band", [4, 7, 128, 128], BF16)
    I["bandc"] = din("bandc", [4, 7, 128, 128], BF16)
    I["masks"] = din("masks", [4, 128, 128], BF16)
    g.I = I
    y = nc.dram_tensor("y", [T, D], F32, kind="ExternalOutput").ap()
    g.y = y

    Sc = {}
    Sc["mod"] = dscr("mod", [DEPTH, 2, 6 * D], F32)
    for s, n in (("L", T), ("C", CTX)):
        Sc["hT" + s] = dscr("hT" + s, [8, 128, n])
        Sc["QsT" + s] = dscr("QsT" + s, [64, 8, n])
        Sc["QdT" + s] = dscr("QdT" + s, [4, 128, n])
        Sc["u" + s] = dscr("u" + s, [n, 512])
        Sc["ypT" + s] = dscr("ypT" + s, [4, 128, n])
        Sc["ysT" + s] = dscr("ysT" + s, [4, 128, n])
        Sc["ydT" + s] = dscr("ydT" + s, [4, 128, n])
        Sc["x1" + s] = dscr("x1" + s, [n, D], F32)
        Sc["xo" + s] = dscr("xo" + s, [n, D], F32)
    g.NCH = T // 256
    for j in range(g.NCH):
        Sc["KdTo%d" % j] = dscr("KdTo%d" % j, [512, 256])
        Sc["KdTa%d" % j] = dscr("KdTa%d" % j, [1024, 256])
        Sc["Vdo%d" % j] = dscr("Vdo%d" % j, [256, 512])
        Sc["Vda%d" % j] = dscr("Vda%d" % j, [512, 512])
    Sc["KsTo"] = dscr("KsTo", [64, 2, T])
    Sc["Vso"] = dscr("Vso", [T, 128])
    Sc["KsHo"] = dscr("KsHo", [64, 512])
    Sc["KsHa"] = dscr("KsHa", [128, 512])
    Sc["VsHo"] = dscr("VsHo", [256, 128])
    Sc["VsHa"] = dscr("VsHa", [512, 128])
    Sc["uHo"] = dscr("uHo", [256, 512])
    Sc["uHa"] = dscr("uHa", [512, 512])
    Sc["KsTc"] = dscr("KsTc", [64, 2, CTX])
    Sc["KdTc"] = dscr("KdTc", [4, 128, CTX])
    Sc["Vsc"] = dscr("Vsc", [CTX, 128])
    Sc["Vdc"] = dscr("Vdc", [CTX, 512])
    g.Sc = Sc

    import contextlib
    with contextlib.ExitStack() as stack:
        stack.enter_context(nc.allow_non_contiguous_dma(reason="small strided parameter / layout loads"))
        S = Sched(nc, stack)
        g.S = S
        ident = stack.enter_context(nc.sbuf_tensor("ident_sb", [128, 128], BF16))
        ones = stack.enter_context(nc.sbuf_tensor("ones_sb", [128, 128], BF16))
        S.dma("sp", ident[:], I["ident"], writes=["ident"])
        S.dma("sp", ones[:], I["ones"], writes=["ones"])
        g.ident, g.ones = ident, ones
        S.flush()

        for l in range(depth):
            last = l == DEPTH - 1
            phase_mod(g, l)
            xin_L = I["x"] if l == 0 else Sc["xoL"]
            xin_C = I["ctx"] if l == 0 else Sc["xoC"]
            phase_p1(g, l, "C", xin_C, NTC, I["cosc"], I["sinc"],
                     dict(KsT=Sc["KsTc"], Vs=Sc["Vsc"],
                          KdT=lambda t0, nt: [(Sc["KdTc"][:, :, t0:t0 + nt].rearrange("h p t -> p h t"), 0, nt)],
                          Vd=lambda t: Sc["Vdc"][t * 128:(t + 1) * 128, :]))
            phase_p1(g, l, "L", xin_L, NT, I["cos"], I["sin"],
                     dict(KsT=Sc["KsTo"], Vs=Sc["Vso"],
                          KdT=lambda t0, nt: [(Sc["KdTo%d" % ((t0 + c) // 256)].rearrange("(h p) t -> p h t", h=4), c, c + 256)
                                              for c in range(0, nt, 256)],
                          Vd=lambda t: Sc["Vdo%d" % (t // 2)][(t % 2) * 128:(t % 2 + 1) * 128, :]))
            phase_exchange(g)
            streams = [("L", T, xin_L, (y if last else Sc["xoL"]))]
            if not last:
                streams.append(("C", CTX, xin_C, Sc["xoC"]))
            for s, n, xin, xout in streams:
                if phases is None or "pool" in phases:
                    phase_pool(g, l, s, n)
                if phases is None or "swa" in phases:
                    phase_swa(g, l, s, n)
                if phases is None or "diff" in phases:
                    phase_diff(g, l, s, n)
                if phases is None or "merge" in phases:
                    phase_merge(g, l, s, n, xin)
                if phases is None or "ffn" in phases:
                    phase_ffn(g, l, s, n, xout)
    return nc


def phase_mod(g, l):
    nc, S, I, Sc = g.nc, g.S, g.I, g.Sc
    import contextlib
    with contextlib.ExitStack() as st:
        sb, ps = mk_alloc(g, st)
        cv = sb("cv", [128, 16])
        cact = sb("cact", [128, 16])
        wada = sb("wada", [128, 2, 8, 512])
        bada = sb("bada", [2, 2, 512])
        modo = sb("modo", [2, 2, 512])
        modps = ps("modps", [2, 2, 512])
        S.dma("sp", cv[:], I["cvec"], writes=["cv"])
        S.op("act", lambda e: e.activation(out=cact[:], in_=cv[:], func=AF.Silu), ["cv"], ["cact"])
        for n in range(12):
            b = n % 2
            S.dma("sp", wada[:, b], I["w_ada"][l, :, n * 512:(n + 1) * 512].rearrange("(kc p) n -> p kc n", p=128),
                  writes=["wada%d" % b])
            S.dma("sp", bada[:, b], dram_bcast(I["b_ada"][l, n * 512:(n + 1) * 512], 2), writes=["bada%d" % b])
            for kc in range(8):
                lhsT = cact[:].rearrange("p (r k) -> p k r", r=2)[:, kc, :]
                S.op("pe", lambda e, lhsT=lhsT, kc=kc, b=b: e.matmul(modps[:, b], lhsT, wada[:, b, kc, :],
                                                                  start=(kc == 0), stop=(kc == 7)),
                     ["cact", "wada%d" % b], ["modps%d" % b])
            S.op("dve", lambda e, b=b: e.tensor_tensor(out=modo[:, b], in0=modps[:, b], in1=bada[:, b], op=ALU.add),
                 ["modps%d" % b, "bada%d" % b], ["modo%d" % b])
            S.dma("sp", Sc["mod"][l, :, n * 512:(n + 1) * 512], modo[:, b], reads=["modo%d" % b])
        S.flush()


def load_bc(S, q, tile_ap, vec_ap, key):
    S.dma(q, tile_ap, dram_bcast(vec_ap, 128), writes=[key])


def phase_p1(g, l, s, xin, ntiles, cos_d, sin_d, dst):
    nc, S, I, Sc = g.nc, g.S, g.I, g.Sc
    r = 0 if s == "L" else 1
    n = ntiles * 128
    import contextlib
    with contextlib.ExitStack() as st:
        sb, ps = mk_alloc(g, st)
        w = sb("p1w", [128, 8, P1_COLS], BF16)
        A1 = sb("A1", [128, D])
        B1 = sb("B1", [128, D])
        nrm = sb("nrm", [128, D])
        G = sb("G", [128, NH, 64])
        xt = sb("xt", [128, 2, D])
        junk = sb("junk", [128, D])
        ssx = sb("ssx", [128, 2])
        hb = sb("hb", [128, 2, D], BF16)
        hT = sb("hT", [128, 2, 8, 128], BF16)
        zs = sb("zs", [128, 2, 1664])
        sq = sb("sq", [128, 1664])
        ssh = sb("ssh", [128, 2, NH])
        qn = sb("qn", [128, NH, 64])
        t1 = sb("t1", [128, NH, 32])
        t2 = sb("t2", [128, NH, 32])
        t3 = sb("t3", [128, NH, 32])
        t4 = sb("t4", [128, NH, 32])
        qo = sb("qo", [128, 2, NH, 64], BF16)
        vu = sb("vu", [128, 2, 1152], BF16)
        cs = sb("cs", [128, 2, 2, 32])
        stQs = sb("stQs", [64, 2, 8, 512], BF16)
        stKs = sb("stKs", [64, 2, 2, 512], BF16)
        stQd = sb("stQd", [128, 2, 4, 512], BF16)
        stKd = sb("stKd", [128, 2, 4, 512], BF16)
        zp = [ps("zp%d" % i, [128, 512]) for i in range(6)]
        tr1 = ps("tr1", [128, 8, 128], BF16)
        tr2 = ps("tr2", [128, 8, 128], BF16)

        c0 = 0
        for (off, wd) in P1_BLOCKS:
            for kh in range(2):
                S.dma("pool", w[:, kh * 4:(kh + 1) * 4, c0:c0 + wd],
                      I["w_in"][l, kh * 512:(kh + 1) * 512, off:off + wd].rearrange("(kc p) n -> p kc n", p=128),
                      writes=["w"])
            c0 += wd
        load_bc(S, "sp", B1[:], Sc["mod"][l, r, 0:D], "B1")
        load_bc(S, "sp", A1[:], Sc["mod"][l, r, D:2 * D], "A1")
        load_bc(S, "sp", nrm[:], I["norm1"][l], "nrm")
        S.op("dve", lambda e: e.scalar_tensor_tensor(out=A1[:], in0=A1[:], scalar=1.0, in1=nrm[:],
                                                     op0=ALU.add, op1=ALU.mult), ["A1", "nrm"], ["A1"])
        for (h0, hn, nm) in ((0, 8, "swa_q_norm"), (8, 2, "swa_k_norm"), (10, 8, "diff_q_norm"), (18, 8, "diff_k_norm")):
            S.dma("sp", G[:, h0:h0 + hn, :], dram_bcast(I[nm][l], 128, hn), writes=["G"])

        for t in range(ntiles):
            b = t % 2
            gi = t // 4
            gb = gi % 2
            tq = t % 4
            S.dma("sp", xt[:, b], xin[t * 128:(t + 1) * 128, :], writes=["xt%d" % b])
            S.dma("sp", cs[:, b, 0], cos_d[t * 128:(t + 1) * 128, :], writes=["cs%d" % b])
            S.dma("sp", cs[:, b, 1], sin_d[t * 128:(t + 1) * 128, :], writes=["cs%d" % b])
            S.op("act", lambda e, b=b: e.activation(out=junk[:], in_=xt[:, b], func=AF.Square, accum_out=ssx[:, b:b + 1]),
                 ["xt%d" % b], ["junk", "ssx%d" % b])
            S.op("act", lambda e, b=b: e.activation(out=ssx[:, b:b + 1], in_=ssx[:, b:b + 1], func=AF.Sqrt, bias=EPS, scale=1.0 / D),
                 ["ssx%d" % b], ["ssx%d" % b])
            S.op("dve", lambda e, b=b: e.reciprocal(out=ssx[:, b:b + 1], in_=ssx[:, b:b + 1]), ["ssx%d" % b], ["ssx%d" % b])
            S.op("dve", lambda e, b=b: e.scalar_tensor_tensor(out=xt[:, b], in0=xt[:, b], scalar=ssx[:, b:b + 1], in1=A1[:],
                                                              op0=ALU.mult, op1=ALU.mult), ["xt%d" % b, "ssx%d" % b, "A1"], ["xt%d" % b])
            S.op("pool", lambda e, b=b: e.tensor_tensor(out=hb[:, b], in0=xt[:, b], in1=B1[:], op=ALU.add),
                 ["xt%d" % b, "B1"], ["hb%d" % b])
            for kc in range(8):
                S.op("pe", lambda e, b=b, kc=kc: e.transpose(tr1[:, kc, :], hb[:, b, kc * 128:(kc + 1) * 128], g.ident[:]),
                     ["hb%d" % b, "ident"], ["tr1"])
            S.op("act", lambda e, b=b: e.copy(out=hT[:, b], in_=tr1[:]), ["tr1"], ["hT%d" % b])
            S.dma("sp", Sc["hT" + s][:, :, t * 128:(t + 1) * 128].rearrange("k p t -> p k t"), hT[:, b], reads=["hT%d" % b])
            for ci in range(6):
                cw = min(512, P1_COLS - ci * 512)
                for kc in range(8):
                    S.op("pe", lambda e, b=b, kc=kc, ci=ci, cw=cw: e.matmul(zp[ci][:, 0:cw], hT[:, b, kc, :], w[:, kc, ci * 512:ci * 512 + cw],
                                                                       start=(kc == 0), stop=(kc == 7)),
                         ["hT%d" % b, "w"], ["zp%d" % ci])
            for ci in range(6):
                lo, hi = ci * 512, min(P1_COLS, ci * 512 + 512)
                if hi <= 1664 or lo < 1664:
                    h2 = min(hi, 1664)
                    S.op("act", lambda e, b=b, ci=ci, lo=lo, h2=h2: e.copy(out=zs[:, b, lo:h2], in_=zp[ci][:, 0:h2 - lo]),
                         ["zp%d" % ci], ["zs%d" % b])
                    S.op("act", lambda e, ci=ci, lo=lo, h2=h2: e.activation(out=sq[:, lo:h2], in_=zp[ci][:, 0:h2 - lo], func=AF.Square),
                         ["zp%d" % ci], ["sq"])
                if hi > 1664:
                    l2 = max(lo, 1664)
                    S.op("act", lambda e, b=b, ci=ci, lo=lo, l2=l2, hi=hi: e.copy(out=vu[:, b, l2 - 1664:hi - 1664], in_=zp[ci][:, l2 - lo:hi - lo]),
                         ["zp%d" % ci], ["vu%d" % b])
            S.dma("sp", dst["Vs"][t * 128:(t + 1) * 128, :], vu[:, b, 0:128], reads=["vu%d" % b])
            S.dma("sp", dst["Vd"](t), vu[:, b, 128:640], reads=["vu%d" % b])
            S.dma("sp", Sc["u" + s][t * 128:(t + 1) * 128, :], vu[:, b, 640:1152], reads=["vu%d" % b])
            if s == "L" and t in (0, ntiles - 1):
                fl = 0 if t == 0 else 1
                S.dma("sp", Sc["VsHo"][fl * 128:(fl + 1) * 128, :], vu[:, b, 0:128], reads=["vu%d" % b])
                S.dma("sp", Sc["uHo"][fl * 128:(fl + 1) * 128, :], vu[:, b, 640:1152], reads=["vu%d" % b])
            S.op("dve", lambda e, b=b: e.tensor_reduce(out=ssh[:, b], in_=sq[:].rearrange("p (h d) -> p h d", d=64), axis=AX.X, op=ALU.add),
                 ["sq"], ["ssh%d" % b])
            S.op("act", lambda e, b=b: e.activation(out=ssh[:, b], in_=ssh[:, b], func=AF.Sqrt, bias=EPS, scale=1.0 / 64),
                 ["ssh%d" % b], ["ssh%d" % b])
            S.op("dve", lambda e, b=b: e.reciprocal(out=ssh[:, b], in_=ssh[:, b]), ["ssh%d" % b], ["ssh%d" % b])
            zh = zs[:, b].rearrange("p (h d) -> p h d", d=64)
            S.op("dve", lambda e, b=b, zh=zh: e.tensor_tensor(out=qn[:], in0=zh, in1=ssh[:, b].unsqueeze(2).to_broadcast([128, NH, 64]), op=ALU.mult),
                 ["zs%d" % b, "ssh%d" % b], ["qn"])
            S.op("pool", lambda e: e.tensor_tensor(out=qn[:], in0=qn[:], in1=G[:], op=ALU.mult), ["qn", "G"], ["qn"])
            cb = cs[:, b, 0].unsqueeze(1).to_broadcast([128, NH, 32])
            sb_ = cs[:, b, 1].unsqueeze(1).to_broadcast([128, NH, 32])
            x1 = qn[:, :, 0:32]
            x2 = qn[:, :, 32:64]
            S.op("dve", lambda e, cb=cb, x1=x1: e.tensor_tensor(out=t1[:], in0=x1, in1=cb, op=ALU.mult), ["qn", "cs%d" % b], ["t1"])
            S.op("dve", lambda e, sb_=sb_, x2=x2: e.tensor_tensor(out=t2[:], in0=x2, in1=sb_, op=ALU.mult), ["qn", "cs%d" % b], ["t2"])
            S.op("dve", lambda e, b=b: e.tensor_tensor(out=qo[:, b, :, 0:32], in0=t1[:], in1=t2[:], op=ALU.subtract), ["t1", "t2"], ["qo%d" % b])
            S.op("pool", lambda e, cb=cb, x2=x2: e.tensor_tensor(out=t3[:], in0=x2, in1=cb, op=ALU.mult), ["qn", "cs%d" % b], ["t3"])
            S.op("pool", lambda e, sb_=sb_, x1=x1: e.tensor_tensor(out=t4[:], in0=x1, in1=sb_, op=ALU.mult), ["qn", "cs%d" % b], ["t4"])
            S.op("pool", lambda e, b=b: e.tensor_tensor(out=qo[:, b, :, 32:64], in0=t3[:], in1=t4[:], op=ALU.add), ["t3", "t4"], ["qo%d" % b])
            for h in range(8):
                S.op("pe", lambda e, b=b, h=h: e.transpose(tr2[0:64, h, :], qo[:, b, h, :], g.ident[:]), ["qo%d" % b, "ident"], ["tr2"])
            S.op("act", lambda e, gb=gb, tq=tq: e.copy(out=stQs[:, gb, :, tq * 128:(tq + 1) * 128], in_=tr2[0:64]), ["tr2"], ["stQs%d" % gb])
            for h in range(2):
                S.op("pe", lambda e, b=b, h=h: e.transpose(tr2[0:64, h, :], qo[:, b, 8 + h, :], g.ident[:]), ["qo%d" % b, "ident"], ["tr2"])
            S.op("act", lambda e, gb=gb, tq=tq: e.copy(out=stKs[:, gb, :, tq * 128:(tq + 1) * 128], in_=tr2[0:64, 0:2, :]), ["tr2"], ["stKs%d" % gb])
            for j in range(8):
                src = qo[:, b, 10 + 2 * j:12 + 2 * j, :].rearrange("p h d -> p (h d)")
                S.op("pe", lambda e, j=j, src=src: e.transpose(tr1[:, j, :], src, g.ident[:]), ["qo%d" % b, "ident"], ["tr1"])
            S.op("act", lambda e, gb=gb, tq=tq: e.copy(out=stQd[:, gb, :, tq * 128:(tq + 1) * 128], in_=tr1[:, 0:4, :]), ["tr1"], ["stQd%d" % gb])
            S.op("act", lambda e, gb=gb, tq=tq: e.copy(out=stKd[:, gb, :, tq * 128:(tq + 1) * 128], in_=tr1[:, 4:8, :]), ["tr1"], ["stKd%d" % gb])
            if tq == 3 or t == ntiles - 1:
                nt = (tq + 1) * 128
                t0 = gi * 512
                S.dma("sp", Sc["QsT" + s][:, :, t0:t0 + nt], stQs[:, gb, :, 0:nt], reads=["stQs%d" % gb])
                S.dma("sp", dst["KsT"][:, :, t0:t0 + nt], stKs[:, gb, :, 0:nt], reads=["stKs%d" % gb])
                S.dma("sp", Sc["QdT" + s][:, :, t0:t0 + nt].rearrange("h p t -> p h t"), stQd[:, gb, :, 0:nt], reads=["stQd%d" % gb])
                for (kap, c0_, c1_) in dst["KdT"](t0, nt):
                    S.dma("sp", kap, stKd[:, gb, :, c0_:c1_], reads=["stKd%d" % gb])
                if s == "L":
                    ksh = Sc["KsHo"].rearrange("p (g f t) -> p g f t", g=2, f=2)
                    if gi == 0:
                        S.dma("sp", ksh[:, :, 0, :], stKs[:, gb, :, 0:128], reads=["stKs%d" % gb])
                    if t == ntiles - 1:
                        S.dma("sp", ksh[:, :, 1, :], stKs[:, gb, :, nt - 128:nt], reads=["stKs%d" % gb])
        S.flush()


PAIRS = [[0, 1], [2, 3], [4, 5], [6, 7]]


def phase_exchange(g):
    S, Sc = g.S, g.Sc
    pairs_ = [("KsHo", "KsHa"), ("VsHo", "VsHa"), ("uHo", "uHa")]
    for j in range(g.NCH):
        pairs_ += [("KdTo%d" % j, "KdTa%d" % j), ("Vdo%d" % j, "Vda%d" % j)]
    for a, b in pairs_:
        S.op("pool", lambda e, a=a, b=b: e.collective_compute("AllGather", ALU.bypass, replica_groups=PAIRS,
                                                               ins=[Sc[a][:, :]], outs=[Sc[b][:, :]]), kind="cc")
    S.flush()


def phase_pool(g, l, s, n):
    nc, S, I, Sc = g.nc, g.S, g.I, g.Sc
    nt = n // 128
    band_d = I["band"] if s == "L" else I["bandc"]
    import contextlib
    with contextlib.ExitStack() as st:
        sb, ps = mk_alloc(g, st)
        band = sb("band", [128, 4, 7, 128], BF16)
        wp = sb("wp", [128, 4, 128], BF16)
        psc = sb("psc", [128, 4])
        ut = sb("ut", [128, 5, 512], BF16)
        pl = sb("pl", [128, 2, 128], BF16)
        yo = sb("yo", [128, 2, 4, 128], BF16)
        pp = [ps("pp%d" % i, [128, 128]) for i in range(2)]
        yp = [ps("yp%d" % i, [128, 128]) for i in range(2)]
        S.dma("sp", band[:], band_d.rearrange("w k s t -> s w k t"), writes=["band"])
        S.dma("pool", wp[:], I["w_pool"][l].rearrange("g c d -> c g d"), writes=["wp"])
        S.dma("sp", psc[:], I["pool_scale"][l].rearrange("(g d) -> d g", g=4), writes=["psc"])
        S.dma("sp", ut[:, 0], Sc["u" + s][0:128, :], writes=["ut0"])
        if s == "L":
            S.dma("sp", ut[:, 3], Sc["uHa"][128:256, :], writes=["ut3"])
            S.dma("sp", ut[:, 4], Sc["uHa"][256:384, :], writes=["ut4"])
        for t in range(nt):
            b = t % 2
            if t + 1 < nt:
                S.dma("sp", ut[:, (t + 1) % 3], Sc["u" + s][(t + 1) * 128:(t + 2) * 128, :], writes=["ut%d" % ((t + 1) % 3)])
            srcs = []
            if t > 0:
                srcs.append(((t - 1) % 3, 0))
            elif s == "L":
                srcs.append((3, 3))
            srcs.append((t % 3, 4 if t == 0 else (5 if t == nt - 1 else 1)))
            if t + 1 < nt:
                srcs.append(((t + 1) % 3, 2))
            elif s == "L":
                srcs.append((4, 6))
            for gq in range(4):
                pb = gq % 2
                for i, (slot, kind) in enumerate(srcs):
                    S.op("pe", lambda e, pb=pb, slot=slot, kind=kind, gq=gq, i=i, ns=len(srcs): e.matmul(
                        pp[pb][:], ut[:, slot, gq * 128:(gq + 1) * 128], band[:, gq, kind, :], start=(i == 0), stop=(i == ns - 1)),
                        ["ut%d" % slot, "band"], ["pp%d" % pb])
                S.op("act", lambda e, pb=pb: e.copy(out=pl[:, pb], in_=pp[pb][:]), ["pp%d" % pb], ["pl%d" % pb])
                S.op("pe", lambda e, pb=pb, gq=gq: e.matmul(yp[pb][:], wp[:, gq, :], pl[:, pb], start=True, stop=True),
                     ["pl%d" % pb, "wp"], ["yp%d" % pb])
                S.op("dve", lambda e, pb=pb, gq=gq, b=b: e.tensor_scalar(out=yo[:, b, gq, :], in0=yp[pb][:], scalar1=psc[:, gq:gq + 1],
                                                                         scalar2=None, op0=ALU.mult),
                     ["yp%d" % pb, "psc"], ["yo%d" % b])
            S.dma("sp", Sc["ypT" + s][:, :, t * 128:(t + 1) * 128].rearrange("g p t -> p g t"), yo[:, b], reads=["yo%d" % b])
        S.flush()


def phase_swa(g, l, s, n):
    nc, S, I, Sc = g.nc, g.S, g.I, g.Sc
    T = g.T
    nt = n // 128
    NT = T // 128
    W = T + 512
    nkt_all = W // 128
    ctx_kts = [NT + 2, NT + 3]
    import contextlib
    with contextlib.ExitStack() as st:
        sb, ps = mk_alloc(g, st)
        kst = sb("kst", [64, 2, W], BF16)
        vs = sb("vs", [128, nkt_all, 128], BF16)
        qb_ = sb("qb", [64, 2, 8, 128], BF16)
        mk = sb("mk", [128, 4, 128], BF16)
        esk = sb("esk", [64, 8])
        pt = sb("pt", [128, 3, 512], BF16)
        den = sb("den", [64, 512])
        yo = sb("yo", [64, 2, 4, 128], BF16)
        spp = [ps("spp%d" % i, [128, 512]) for i in range(2)]
        opp = [ps("opp%d" % i, [64, 512]) for i in range(2)]
        rpp = [ps("rpp%d" % i, [64, 512]) for i in range(2)]
        S.dma("sp", kst[:, :, T + 256:W], Sc["KsTc"], writes=["kst"])
        S.dma("sp", vs[:, NT + 2:NT + 4, :], Sc["Vsc"].rearrange("(kt p) d -> p kt d", p=128), writes=["vs"])
        if s == "L":
            ksha = Sc["KsHa"].rearrange("(r p) (g f t) -> r p g f t", r=2, g=2, f=2)
            S.dma("sp", kst[:, :, 0:128], ksha[0, :, :, 1, :], writes=["kst"])
            S.dma("sp", kst[:, :, 128:128 + T], Sc["KsTo"], writes=["kst"])
            S.dma("sp", kst[:, :, 128 + T:256 + T], ksha[1, :, :, 0, :], writes=["kst"])
            S.dma("sp", vs[:, 0, :], Sc["VsHa"][128:256, :], writes=["vs"])
            S.dma("sp", vs[:, 1:NT + 1, :], Sc["Vso"].rearrange("(kt p) d -> p kt d", p=128), writes=["vs"])
            S.dma("sp", vs[:, NT + 1, :], Sc["VsHa"][256:384, :], writes=["vs"])
        S.dma("sp", mk[:], I["masks"].rearrange("m j a -> j m a"), writes=["mk"])
        S.dma("sp", esk[:], dram_bcast(I["swa_sink"][l], 64), writes=["esk"])
        S.op("act", lambda e: e.activation(out=esk[:], in_=esk[:], func=AF.Exp), ["esk"], ["esk"])
        pcnt = 0
        for qi in range(nt):
            b = qi % 2
            S.dma("sp", qb_[:, b], Sc["QsT" + s][:, :, qi * 128:(qi + 1) * 128], writes=["qb%d" % b])
            kts = []
            if s == "L":
                kts.append((qi, 2 if qi == 0 else 0))
                kts.append((qi + 1, None))
                kts.append((qi + 2, 3 if qi == nt - 1 else 1))
            kts += [(k, None) for k in ctx_kts]
            for gk in range(2):
                ob = gk
                for i, (kt, m) in enumerate(kts):
                    sl = pcnt % 3
                    sp_ = pcnt % 2
                    pcnt += 1
                    S.op("pe", lambda e, sp_=sp_, gk=gk, kt=kt, b=b: e.matmul(
                        spp[sp_][:], kst[:, gk, kt * 128:(kt + 1) * 128], qb_[:, b, 4 * gk:4 * gk + 4, :], start=True, stop=True),
                        ["kst", "qb%d" % b], ["spp%d" % sp_])
                    S.op("act", lambda e, sp_=sp_, sl=sl: e.activation(out=pt[:, sl], in_=spp[sp_][:], func=AF.Exp, scale=0.125),
                         ["spp%d" % sp_], ["pt%d" % sl])
                    if m is not None:
                        pv = pt[:, sl].rearrange("p (h q) -> p h q", h=4)
                        S.op("dve", lambda e, pv=pv, m=m: e.tensor_tensor(out=pv, in0=pv, in1=mk[:, m].unsqueeze(1).to_broadcast([128, 4, 128]),
                                                                          op=ALU.mult), ["pt%d" % sl, "mk"], ["pt%d" % sl])
                    S.op("pe", lambda e, ob=ob, kt=kt, gk=gk, sl=sl, i=i, nk=len(kts): e.matmul(
                        opp[ob][:], vs[:, kt, gk * 64:(gk + 1) * 64], pt[:, sl], start=(i == 0), stop=(i == nk - 1)),
                        ["vs", "pt%d" % sl], ["opp%d" % ob])
                    S.op("pe", lambda e, ob=ob, sl=sl, i=i, nk=len(kts): e.matmul(
                        rpp[ob][:], g.ones[:, 0:64], pt[:, sl], start=(i == 0), stop=(i == nk - 1)),
                        ["ones", "pt%d" % sl], ["rpp%d" % ob])
                dv = den[:].rearrange("p (h q) -> p h q", h=4)
                S.op("dve", lambda e, ob=ob, gk=gk, dv=dv: e.tensor_tensor(
                    out=dv, in0=rpp[ob][:].rearrange("p (h q) -> p h q", h=4),
                    in1=esk[:, 4 * gk:4 * gk + 4].unsqueeze(2).to_broadcast([64, 4, 128]), op=ALU.add),
                    ["rpp%d" % ob, "esk"], ["den"])
                S.op("dve", lambda e: e.reciprocal(out=den[:], in_=den[:]), ["den"], ["den"])
                S.op("dve", lambda e, ob=ob, gk=gk: e.tensor_tensor(out=yo[:, gk].rearrange("p h q -> p (h q)"), in0=opp[ob][:], in1=den[:], op=ALU.mult),
                     ["opp%d" % ob, "den"], ["yo%d" % gk])
                for jj in range(2):
                    j = 2 * gk + jj
                    dst = Sc["ysT" + s][j].rearrange("(hh d) t -> d hh t", hh=2)[:, :, qi * 128:(qi + 1) * 128]
                    S.dma("sp", dst, yo[:, gk, 2 * jj:2 * jj + 2, :], reads=["yo%d" % gk])
        S.flush()


def phase_diff(g, l, s, n):
    nc, S, I, Sc = g.nc, g.S, g.I, g.Sc
    T = g.T
    lam_init = 0.8 - 0.6 * math.exp(-0.3 * l)
    nk = (2 * T + CTX) if s == "L" else CTX
    nkt = nk // 128
    GW = min(512, n)
    ng = n // GW
    S.skip_pe = False
    import contextlib
    with contextlib.ExitStack() as st:
        sb, ps = mk_alloc(g, st)
        kt_ = sb("dk", [128, 2, nk], BF16)
        vt = sb("dv", [128, 2, nkt, 128], BF16)
        qt = sb("dq", [128, 2, n], BF16)
        dl = sb("dl", [128, 256])
        pr = sb("pr", [128, 2, 64])
        ee = sb("ee", [128, 2])
        nlam = sb("nlam", [128, 1])
        sg = sb("sg", [128, 1])
        hi = sb("hi", [128, GW], BF16)
        lo = sb("lo", [128, GW], BF16)
        pt = sb("pt", [128, 3, 2, GW], BF16)
        acc = sb("acc", [128, 2, GW])
        ra = sb("ra", [128, GW])
        oa = sb("oa", [128, GW])
        ob = sb("ob", [128, GW])
        sq = sb("sq", [128, GW])
        yb = sb("yb", [128, 2, GW], BF16)
        sps = [ps("sps%d" % i, [128, 2, GW]) for i in range(2)]
        ops_ = [ps("ops%d" % i, [128, GW]) for i in range(2)]
        rps = ps("rps", [128, GW])
        S.dma("sp", dl[:], dram_bcast(I["diff_lambda"][l], 128), writes=["dl"])
        dl4 = dl[:].rearrange("p (a b d) -> p a b d", a=2, b=2)
        S.op("dve", lambda e: e.tensor_tensor(out=pr[:], in0=dl4[:, :, 0, :], in1=dl4[:, :, 1, :], op=ALU.mult), ["dl"], ["pr"])
        S.op("dve", lambda e: e.tensor_reduce(out=ee[:], in_=pr[:], axis=AX.X, op=ALU.add), ["pr"], ["ee"])
        S.op("act", lambda e: e.activation(out=ee[:], in_=ee[:], func=AF.Exp), ["ee"], ["ee"])
        S.op("dve", lambda e: e.tensor_tensor(out=nlam[:], in0=ee[:, 1:2], in1=ee[:, 0:1], op=ALU.subtract), ["ee"], ["nlam"])
        S.op("dve", lambda e: e.tensor_scalar(out=nlam[:], in0=nlam[:], scalar1=-lam_init, scalar2=None, op0=ALU.add), ["nlam"], ["nlam"])
        S.dma("sp", sg[:], I["diff_subln"][l].rearrange("(p o) -> p o", o=1), writes=["sg"])
        S.op("dve", lambda e: e.tensor_scalar(out=sg[:], in0=sg[:], scalar1=(1.0 - lam_init), scalar2=None, op0=ALU.mult), ["sg"], ["sg"])

        def load_head(h):
            hb = h % 2
            c0 = nk - CTX
            if s == "L":
                for r_ in range(2):
                    for j in range(g.NCH):
                        kda = Sc["KdTa%d" % j].rearrange("(r h p) t -> r h p t", r=2, h=4)
                        k_off = r_ * T + j * 256
                        S.dma("sp", kt_[:, hb, k_off:k_off + 256], kda[r_, h], writes=["dk%d" % hb])
                        S.dma("sp", vt[:, hb, k_off // 128:k_off // 128 + 2, :],
                              Sc["Vda%d" % j][r_ * 256:(r_ + 1) * 256, h * 128:(h + 1) * 128].rearrange("(kt p) d -> p kt d", p=128),
                              writes=["dv%d" % hb])
            S.dma("sp", kt_[:, hb, c0:nk], Sc["KdTc"][h], writes=["dk%d" % hb])
            S.dma("sp", vt[:, hb, c0 // 128:nkt, :], Sc["Vdc"][:, h * 128:(h + 1) * 128].rearrange("(kt p) d -> p kt d", p=128),
                  writes=["dv%d" % hb])
            S.dma("sp", qt[:, hb], Sc["QdT" + s][h], writes=["dq%d" % hb])
        load_head(0)
        cnt = 0
        for h in range(4):
            hb = h % 2
            if h + 1 < 4:
                load_head(h + 1)
            for gq in range(ng):
                qs_ = slice(gq * GW, (gq + 1) * GW)

                def scores(kt, c, hb=hb, qs_=qs_):
                    pb = c % 2
                    for hf in range(2):
                        S.op("pe", lambda e, pb=pb, hf=hf, kt=kt, hb=hb, qs_=qs_: e.matmul(
                            sps[pb][:, hf, :], kt_[hf * 64:(hf + 1) * 64, hb, kt * 128:(kt + 1) * 128], qt[hf * 64:(hf + 1) * 64, hb, qs_],
                            start=True, stop=True), ["dk%d" % hb, "dq%d" % hb], ["sps%d" % pb])
                scores(0, cnt)
                for kt in range(nkt):
                    c = cnt + kt
                    if kt + 1 < nkt:
                        scores(kt + 1, c + 1)
                    pb = c % 2
                    sl = c % 3
                    for hf in range(2):
                        S.op("act", lambda e, pb=pb, sl=sl, hf=hf: e.activation(out=pt[:, sl, hf, :], in_=sps[pb][:, hf, :], func=AF.Exp, scale=0.125),
                             ["sps%d" % pb], ["pt%d" % sl])
                    for hf in range(2):
                        S.op("pe", lambda e, hf=hf, sl=sl, kt=kt, hb=hb: e.matmul(ops_[hf][:], vt[:, hb, kt, :], pt[:, sl, hf, :],
                                                                               start=(kt == 0), stop=(kt == nkt - 1)),
                             ["dv%d" % hb, "pt%d" % sl], ["ops%d" % hf])
                    for hf, en in ((0, "dve"), (1, "pool")):
                        if kt == 0:
                            S.op(en, lambda e, hf=hf, sl=sl: e.tensor_copy(out=acc[:, hf], in_=pt[:, sl, hf, :]), ["pt%d" % sl], ["acc%d" % hf])
                        else:
                            S.op(en, lambda e, hf=hf, sl=sl: e.tensor_tensor(out=acc[:, hf], in0=acc[:, hf], in1=pt[:, sl, hf, :], op=ALU.add),
                                 ["pt%d" % sl, "acc%d" % hf], ["acc%d" % hf])
                cnt += nkt
                def colsum(src_ap, src_key):
                    S.op("dve", lambda e: e.tensor_copy(out=hi[:], in_=src_ap), [src_key], ["hi"])
                    S.op("pool", lambda e: e.tensor_tensor(out=sq[:], in0=src_ap, in1=hi[:], op=ALU.subtract), [src_key, "hi"], ["sq"])
                    S.op("pool", lambda e: e.tensor_copy(out=lo[:], in_=sq[:]), ["sq"], ["lo"])
                    S.op("pe", lambda e: e.matmul(rps[:], g.ones[:], hi[:], start=True, stop=False), ["ones", "hi"], ["rps"])
                    S.op("pe", lambda e: e.matmul(rps[:], g.ones[:], lo[:], start=False, stop=True), ["ones", "lo"], ["rps"])
                for hf, dst in ((0, oa), (1, ob)):
                    colsum(acc[:, hf], "acc%d" % hf)
                    S.op("dve", lambda e: e.reciprocal(out=ra[:], in_=rps[:]), ["rps"], ["ra"])
                    S.op("dve", lambda e, hf=hf, dst=dst: e.tensor_tensor(out=dst[:], in0=ops_[hf][:], in1=ra[:], op=ALU.mult),
                         ["ops%d" % hf, "ra"], ["o%d" % hf])
                S.op("dve", lambda e: e.scalar_tensor_tensor(out=oa[:], in0=ob[:], scalar=nlam[:, 0:1], in1=oa[:], op0=ALU.mult, op1=ALU.add),
                     ["o0", "o1", "nlam"], ["o0"])
                S.op("pool", lambda e: e.tensor_tensor(out=ob[:], in0=oa[:], in1=oa[:], op=ALU.mult), ["o0"], ["o1"])
                colsum(ob[:], "o1")
                S.op("act", lambda e: e.activation(out=ra[:], in_=rps[:], func=AF.Sqrt, bias=EPS, scale=1.0 / 128), ["rps"], ["ra"])
                S.op("dve", lambda e: e.reciprocal(out=ra[:], in_=ra[:]), ["ra"], ["ra"])
                yb_ = gq % 2
                S.op("dve", lambda e, yb_=yb_: e.scalar_tensor_tensor(out=yb[:, yb_], in0=oa[:], scalar=sg[:, 0:1], in1=ra[:], op0=ALU.mult, op1=ALU.mult),
                     ["o0", "sg", "ra"], ["yb%d" % yb_])
                S.dma("sp", Sc["ydT" + s][h][:, qs_], yb[:, yb_], reads=["yb%d" % yb_])
        S.flush()
    S.skip_pe = True


def phase_merge(g, l, s, n, xin):
    nc, S, I, Sc = g.nc, g.S, g.I, g.Sc
    r = 0 if s == "L" else 1
    GW = min(512, n)
    ng = n // GW
    import contextlib
    with contextlib.ExitStack() as st:
        sb, ps = mk_alloc(g, st)
        wg = sb("wg", [128, 8, 3072], BF16)
        wb = sb("wb", [128, 3, 4, D], BF16)
        wo = sb("wo", [128, 8, D], BF16)
        g1 = sb("g1", [128, D])
        hTg = sb("hTg", [128, 2, 8, GW], BF16)
        ybr = sb("ybr", [128, 2, 3, 4, GW], BF16)
        sig = sb("sig", [128, 2, 3, GW])
        m0 = sb("m0", [128, GW])
        m1 = sb("m1", [128, GW])
        m2 = sb("m2", [128, GW])
        mT = sb("mT", [128, 8, GW], BF16)
        xt = sb("xt", [128, 2, D])
        tmp = sb("tmp", [128, D])
        gp = [ps("gp%d" % i, [128, GW]) for i in range(3)]
        bp = [ps("bp%d" % i, [128, GW]) for i in range(3)]
        ao = [ps("ao%d" % i, [128, 512]) for i in range(2)]
        for kh in range(4):
            for cb in range(3):
                S.dma("pool", wg[:, 2 * kh:2 * kh + 2, cb * 1024:(cb + 1) * 1024],
                      I["w_in"][l, kh * 256:(kh + 1) * 256, OFF_G + cb * 1024:OFF_G + (cb + 1) * 1024].rearrange("(kc p) n -> p kc n", p=128),
                      writes=["wg"])
        for bi, nm in enumerate(("w_br_pool", "w_br_swa", "w_br_diff")):
            S.dma("pool", wb[:, bi], I[nm][l].rearrange("(kc p) n -> p kc n", p=128), writes=["wb"])
        for kh in range(2):
            S.dma("pool", wo[:, 4 * kh:4 * kh + 4, :], I["w_out"][l, kh * 512:(kh + 1) * 512, :].rearrange("(kc p) n -> p kc n", p=128), writes=["wo"])
        load_bc(S, "sp", g1[:], Sc["mod"][l, r, 2 * D:3 * D], "g1")
        xcnt = 0
        for gi in range(ng):
            b = gi % 2
            cs_ = slice(gi * GW, (gi + 1) * GW)
            S.dma("sp", hTg[:, b], Sc["hT" + s][:, :, cs_].rearrange("k p t -> p k t"), writes=["hTg%d" % b])
            for bi, nm in enumerate(("ypT", "ysT", "ydT")):
                S.dma("sp", ybr[:, b, bi], Sc[nm + s][:, :, cs_].rearrange("c p t -> p c t"), writes=["ybr%d" % b])
            for j in range(8):
                sb_ = j % 2
                for bi in range(3):
                    for kc in range(8):
                        S.op("pe", lambda e, bi=bi, kc=kc, j=j, b=b: e.matmul(
                            gp[bi][:], wg[:, kc, bi * 1024 + j * 128:bi * 1024 + (j + 1) * 128], hTg[:, b, kc, :], start=(kc == 0), stop=(kc == 7)),
                            ["wg", "hTg%d" % b], ["gp%d" % bi])
                    for kc in range(4):
                        S.op("pe", lambda e, bi=bi, kc=kc, j=j, b=b: e.matmul(
                            bp[bi][:], wb[:, bi, kc, j * 128:(j + 1) * 128], ybr[:, b, bi, kc, :], start=(kc == 0), stop=(kc == 3)),
                            ["wb", "ybr%d" % b], ["bp%d" % bi])
                for bi in range(3):
                    S.op("act", lambda e, bi=bi, sb_=sb_: e.activation(out=sig[:, sb_, bi], in_=gp[bi][:], func=AF.Sigmoid),
                         ["gp%d" % bi], ["sig%d_%d" % (sb_, bi)])
                for bi, mm_ in enumerate((m0, m1, m2)):
                    S.op("dve", lambda e, bi=bi, mm_=mm_, sb_=sb_: e.tensor_tensor(out=mm_[:], in0=bp[bi][:], in1=sig[:, sb_, bi], op=ALU.mult),
                         ["bp%d" % bi, "sig%d_%d" % (sb_, bi)], ["m%d" % bi])
                S.op("pool", lambda e: e.tensor_tensor(out=m0[:], in0=m0[:], in1=m1[:], op=ALU.add), ["m0", "m1"], ["m0"])
                S.op("pool", lambda e, j=j: e.tensor_tensor(out=mT[:, j, :], in0=m0[:], in1=m2[:], op=ALU.add), ["m0", "m2"], ["mT"])
            for tt in range(GW // 128):
                xb = xcnt % 2
                xcnt += 1
                row0 = gi * GW + tt * 128
                S.dma("sp", xt[:, xb], xin[row0:row0 + 128, :], writes=["xt%d" % xb])
                for nn in range(2):
                    for kc in range(8):
                        S.op("pe", lambda e, nn=nn, kc=kc, tt=tt: e.matmul(
                            ao[nn][:], mT[:, kc, tt * 128:(tt + 1) * 128], wo[:, kc, nn * 512:(nn + 1) * 512], start=(kc == 0), stop=(kc == 7)),
                            ["mT", "wo"], ["ao%d" % nn])
                for nn in range(2):
                    S.op("dve", lambda e, nn=nn: e.tensor_tensor(out=tmp[:, nn * 512:(nn + 1) * 512], in0=ao[nn][:], in1=g1[:, nn * 512:(nn + 1) * 512], op=ALU.mult),
                         ["ao%d" % nn, "g1"], ["tmp"])
                S.op("pool", lambda e, xb=xb: e.tensor_tensor(out=xt[:, xb], in0=xt[:, xb], in1=tmp[:], op=ALU.add), ["tmp", "xt%d" % xb], ["xt%d" % xb])
                S.dma("sp", Sc["x1" + s][row0:row0 + 128, :], xt[:, xb], reads=["xt%d" % xb])
        S.flush()


def phase_ffn(g, l, s, n, xout):
    nc, S, I, Sc = g.nc, g.S, g.I, g.Sc
    r = 0 if s == "L" else 1
    GW = 256
    ng = n // GW
    import contextlib
    with contextlib.ExitStack() as st:
        sb, ps = mk_alloc(g, st)
        w1 = sb("w1", [128, 8, DFF], BF16)
        w2 = sb("w2", [128, 32, D], BF16)
        A2 = sb("A2", [128, D])
        B2 = sb("B2", [128, D])
        g2 = sb("g2", [128, D])
        tmp = sb("tmp", [128, D])
        xt = sb("xt", [128, 4, D])
        ssx = sb("ssx", [128, 4])
        hb = sb("hb", [128, 2, D], BF16)
        h2T = sb("h2T", [128, 2, 8, GW], BF16)
        aT = sb("aT", [128, 32, GW], BF16)
        r32 = sb("r32", [128, 2, GW])
        tr = ps("tr", [128, 8, 128], BF16)
        fp = [ps("fp%d" % i, [128, GW]) for i in range(2)]
        yp = [ps("yp%d" % i, [128, 512]) for i in range(4)]
        for kc in range(8):
            for cb in range(2):
                S.dma("pool", w1[:, kc, cb * 2048:(cb + 1) * 2048], I["w_ff1"][l, kc * 128:(kc + 1) * 128, cb * 2048:(cb + 1) * 2048], writes=["w1"])
        for fh in range(8):
            S.dma("pool", w2[:, 4 * fh:4 * fh + 4, :], I["w_ff2"][l, fh * 512:(fh + 1) * 512, :].rearrange("(kc p) n -> p kc n", p=128), writes=["w2"])
        load_bc(S, "sp", B2[:], Sc["mod"][l, r, 3 * D:4 * D], "B2")
        load_bc(S, "sp", A2[:], Sc["mod"][l, r, 4 * D:5 * D], "A2")
        load_bc(S, "sp", g2[:], Sc["mod"][l, r, 5 * D:6 * D], "g2")
        load_bc(S, "sp", tmp[:], I["norm2"][l], "tmp")
        S.op("dve", lambda e: e.scalar_tensor_tensor(out=A2[:], in0=A2[:], scalar=1.0, in1=tmp[:], op0=ALU.add, op1=ALU.mult), ["A2", "tmp"], ["A2"])
        tcnt = 0
        for gi in range(ng):
            gb = gi % 2
            for tt in range(2):
                xs = 2 * gb + tt
                hb_ = tcnt % 2
                tcnt += 1
                row0 = gi * GW + tt * 128
                S.dma("sp", xt[:, xs], Sc["x1" + s][row0:row0 + 128, :], writes=["xt%d" % xs])
                S.op("act", lambda e, xs=xs: e.activation(out=tmp[:], in_=xt[:, xs], func=AF.Square, accum_out=ssx[:, xs:xs + 1]),
                     ["xt%d" % xs], ["tmp", "ssx%d" % xs])
                S.op("act", lambda e, xs=xs: e.activation(out=ssx[:, xs:xs + 1], in_=ssx[:, xs:xs + 1], func=AF.Sqrt, bias=EPS, scale=1.0 / D),
                     ["ssx%d" % xs], ["ssx%d" % xs])
                S.op("dve", lambda e, xs=xs: e.reciprocal(out=ssx[:, xs:xs + 1], in_=ssx[:, xs:xs + 1]), ["ssx%d" % xs], ["ssx%d" % xs])
                S.op("dve", lambda e, xs=xs: e.scalar_tensor_tensor(out=tmp[:], in0=xt[:, xs], scalar=ssx[:, xs:xs + 1], in1=A2[:], op0=ALU.mult, op1=ALU.mult),
                     ["xt%d" % xs, "ssx%d" % xs, "A2"], ["tmp"])
                S.op("pool", lambda e, hb_=hb_: e.tensor_tensor(out=hb[:, hb_], in0=tmp[:], in1=B2[:], op=ALU.add), ["tmp", "B2"], ["hb%d" % hb_])
                for kc in range(8):
                    S.op("pe", lambda e, hb_=hb_, kc=kc: e.transpose(tr[:, kc, :], hb[:, hb_, kc * 128:(kc + 1) * 128], g.ident[:]),
                         ["hb%d" % hb_, "ident"], ["tr"])
                S.op("act", lambda e, gb=gb, tt=tt: e.copy(out=h2T[:, gb, :, tt * 128:(tt + 1) * 128], in_=tr[:]), ["tr"], ["h2T%d" % gb])
            for fc in range(32):
                fb = fc % 2
                for kc in range(8):
                    S.op("pe", lambda e, fb=fb, kc=kc, fc=fc, gb=gb: e.matmul(fp[fb][:], w1[:, kc, fc * 128:(fc + 1) * 128], h2T[:, gb, kc, :],
                                                                          start=(kc == 0), stop=(kc == 7)),
                         ["w1", "h2T%d" % gb], ["fp%d" % fb])
                S.op("act", lambda e, fb=fb: e.activation(out=r32[:, fb], in_=fp[fb][:], func=AF.Relu), ["fp%d" % fb], ["r32_%d" % fb])
                S.op("pool" if fc % 2 else "dve", lambda e, fb=fb, fc=fc: e.tensor_tensor(out=aT[:, fc, :], in0=r32[:, fb], in1=r32[:, fb], op=ALU.mult),
                     ["r32_%d" % fb], ["aT"])
            for tt in range(2):
                xs = 2 * gb + tt
                for nn in range(2):
                    yb_ = 2 * tt + nn
                    for fc in range(32):
                        S.op("pe", lambda e, yb_=yb_, fc=fc, tt=tt, nn=nn: e.matmul(yp[yb_][:], aT[:, fc, tt * 128:(tt + 1) * 128], w2[:, fc, nn * 512:(nn + 1) * 512],
                                                                               start=(fc == 0), stop=(fc == 31)),
                             ["aT", "w2"], ["yp%d" % yb_])
                for nn in range(2):
                    yb_ = 2 * tt + nn
                    S.op("dve", lambda e, yb_=yb_, nn=nn: e.tensor_tensor(out=tmp[:, nn * 512:(nn + 1) * 512], in0=yp[yb_][:], in1=g2[:, nn * 512:(nn + 1) * 512], op=ALU.mult),
                         ["yp%d" % yb_, "g2"], ["tmp"])
                row0 = gi * GW + tt * 128
                S.op("pool", lambda e, xs=xs: e.tensor_tensor(out=xt[:, xs], in0=xt[:, xs], in1=tmp[:], op=ALU.add), ["tmp", "xt%d" % xs], ["xt%d" % xs])
                S.dma("sp", xout[row0:row0 + 128, :], xt[:, xs], reads=["xt%d" % xs])
        S.flush()


def _band_mats(n_full, g0, gl):
    out = np.zeros((4, 7, 128, 128), np.float32)
    tl = np.arange(128)
    ntot = n_full // 128

    def mat(w, tile, src_tile):
        t = tile * 128 + tl
        lo = np.clip(t - w // 2, 0, n_full)
        hi = np.clip(t + w // 2, 0, n_full)
        cnt = (hi - lo).astype(np.float32)
        sg = src_tile * 128 + tl
        m = ((sg[:, None] >= lo[None, :]) & (sg[:, None] < hi[None, :])).astype(np.float32) / cnt[None, :]
        if tile == src_tile:
            m = m - np.eye(128, dtype=np.float32)
        return m
    for wi, w in enumerate(POOL_WINDOWS):
        if ntot > 2:
            out[wi, 0] = mat(w, 1, 0)
            out[wi, 1] = mat(w, 1, 1)
            out[wi, 2] = mat(w, 1, 2)
        else:
            out[wi, 0] = mat(w, 1, 0)
            out[wi, 2] = mat(w, 0, 1)
        if g0 > 0:
            out[wi, 3] = mat(w, g0, g0 - 1)
        out[wi, 4] = mat(w, g0, g0)
        out[wi, 5] = mat(w, gl, gl)
        if gl + 1 < ntot:
            out[wi, 6] = mat(w, gl, gl + 1)
    return out.astype(ml_dtypes.bfloat16)


def make_consts(T_full, hf):
    T = T_full // 2
    rows = T_full // 64
    row = np.repeat(np.arange(rows, dtype=np.float32), 64)
    col = np.tile(np.arange(64, dtype=np.float32), rows)
    inv = (10000.0 ** (-np.arange(16, dtype=np.float32) / 16)).astype(np.float32)
    ang = np.concatenate([row[:, None] * inv, col[:, None] * inv], axis=-1).astype(np.float32)[hf * T:(hf + 1) * T]
    j = np.arange(128)[:, None]
    a = np.arange(128)[None, :]
    mp = (j >= a).astype(np.float32)
    mn = (j <= a).astype(np.float32)
    masks = np.stack([mp, mn, mp * (1.0 if hf == 1 else 0.0), mn * (1.0 if hf == 0 else 0.0)], 0)
    NT = T // 128
    return {
        "cos": np.ascontiguousarray(np.cos(ang).astype(np.float32)), "sin": np.ascontiguousarray(np.sin(ang).astype(np.float32)),
        "cosc": np.ones((CTX, 32), np.float32), "sinc": np.zeros((CTX, 32), np.float32),
        "ident": np.eye(128, dtype=np.float32).astype(ml_dtypes.bfloat16),
        "ones": np.ones((128, 128), np.float32).astype(ml_dtypes.bfloat16),
        "band": _band_mats(T_full, hf * NT, hf * NT + NT - 1), "bandc": _band_mats(CTX, 0, CTX // 128 - 1),
        "masks": masks.astype(ml_dtypes.bfloat16),
    }


def core_inputs(inp, b, hf, consts):
    f = lambda a: np.ascontiguousarray(np.asarray(a, dtype=np.float32))
    m = dict(consts)
    T = inp["x"].shape[1] // 2
    m["x"] = f(inp["x"][b][hf * T:(hf + 1) * T])
    m["ctx"] = f(inp["ctx"][b])
    cv = np.stack([f(inp["c"][b]), f(inp["c_ctx"])], 0)
    m["cvec"] = np.ascontiguousarray(cv.reshape(2, 8, 128).transpose(2, 0, 1).reshape(128, 16))
    for k in ("w_ada", "b_ada", "norm1", "norm2", "w_in", "w_pool", "pool_scale", "swa_q_norm", "swa_k_norm",
              "swa_sink", "diff_q_norm", "diff_k_norm", "diff_subln", "w_br_pool", "w_br_swa", "w_br_diff",
              "w_out", "w_ff1", "w_ff2"):
        m[k] = f(inp[k])
    m["diff_lambda"] = f(inp["diff_lambda"]).reshape(DEPTH, 256)
    return m


_NC_CACHE = {}


def kernel(**inputs):
    B, TF, _ = inputs["x"].shape
    T = TF // 2
    if T not in _NC_CACHE:
        _NC_CACHE[T] = build_program(T)
    nc = _NC_CACHE[T]
    consts = [make_consts(TF, hf) for hf in range(2)]
    n_cores = 2 * B
    in_maps = [core_inputs(inputs, c // 2, c % 2, consts[c % 2]) for c in range(n_cores)]
    res = run_bass_kernel_spmd(nc, in_maps, core_ids=list(range(n_cores)))
    out = np.empty((B, TF, D), np.float32)
    for c in range(n_cores):
        out[c // 2, (c % 2) * T:(c % 2 + 1) * T] = np.asarray(res.results[c]["y"], dtype=np.float32)
    return out
```
